# Optimizing a Trainium2 kernel written in Bass

```python
import math
import jax, jax.numpy as jnp
from jax import lax
import numpy as np

D_MODEL = 1024
BATCH = 32
SEQ = 2048
DEPTH = 2

CHUNK = 64
N_META = 16
SSD_HEADS = 16
SSD_HEAD_DIM = 64
SSD_INNER = SSD_HEADS * SSD_HEAD_DIM
SSD_GROUPS = 4
SSD_STATE = 128
SSD_CONV = 4
S5_WIDTH = D_MODEL // 2
S5_GROUP = 16
S5_GROUPS = S5_WIDTH // S5_GROUP
S5_STATE = 64
D_FF = 4 * D_MODEL
EPS = 1e-6

XBC_WIDTH = SSD_INNER + 2 * SSD_GROUPS * SSD_STATE
IN_SPLITS = (SSD_INNER, SSD_INNER + XBC_WIDTH, SSD_INNER + XBC_WIDTH + SSD_HEADS,
             SSD_INNER + XBC_WIDTH + SSD_HEADS + S5_WIDTH)
IN_WIDTH = IN_SPLITS[-1] + 2 * D_MODEL

kernel_name = "hybrid_ssd_s5_gated_encoder"


def rmsnorm(x, w):
    xf = x.astype(jnp.float32)
    xf = xf * lax.rsqrt(jnp.mean(xf * xf, axis=-1, keepdims=True) + EPS)
    return (xf * w.astype(jnp.float32)).astype(x.dtype)


def causal_dwconv(u, w, b):
    k, c = w.shape
    y = lax.conv_general_dilated(u, w[:, None, :].astype(u.dtype), window_strides=(1,),
                                 padding=[(k - 1, 0)], dimension_numbers=("NWC", "WIO", "NWC"),
                                 feature_group_count=c)
    return y + b.astype(u.dtype)


def ssd_chunked(xs, dt, a, bm, cm):
    b, seq_len, n_heads, p = xs.shape
    g, n = bm.shape[-2:]
    r = n_heads // g
    nc = seq_len // CHUNK
    dt_c = dt.reshape(b, nc, CHUNK, g, r)
    xdt = xs.reshape(b, nc, CHUNK, g, r, p) * dt_c[..., None]
    bc = bm.reshape(b, nc, CHUNK, g, n)
    cc = cm.reshape(b, nc, CHUNK, g, n)
    da_cs = jnp.cumsum(dt_c * a.reshape(g, r), axis=2)
    causal = jnp.tril(jnp.ones((CHUNK, CHUNK), dtype=bool))[:, :, None, None]
    seg = da_cs[:, :, :, None] - da_cs[:, :, None, :]
    decay = jnp.exp(jnp.where(causal, seg, -jnp.inf))
    scores = jnp.einsum("bclgn,bcsgn->bclsg", cc, bc)
    y_diag = jnp.einsum("bclsg,bclsgr,bcsgrp->bclgrp", scores, decay, xdt)
    decay_states = jnp.exp(da_cs[:, :, -1:] - da_cs)
    states = jnp.einsum("bclgn,bclgr,bclgrp->bcgrpn", bc, decay_states, xdt)
    chunk_decay = jnp.exp(da_cs[:, :, -1])

    def step(h, inp):
        s, d = inp
        return h * d[..., None, None] + s, h

    h0 = jnp.zeros((b, g, r, p, n), xs.dtype)
    _, h_prev = lax.scan(step, h0, (jnp.moveaxis(states, 1, 0), jnp.moveaxis(chunk_decay, 1, 0)))
    h_prev = jnp.moveaxis(h_prev, 0, 1)
    y_off = jnp.einsum("bclgn,bcgrpn,bclgr->bclgrp", cc, h_prev, jnp.exp(da_cs))
    return (y_diag + y_off).reshape(b, seq_len, n_heads, p)


def ssd_branch(z, xbc_raw, dt_raw, conv_w, conv_b, dt_bias, a_log, d_skip, norm_w):
    b, seq_len, _ = z.shape
    xbc = jax.nn.silu(causal_dwconv(xbc_raw, conv_w, conv_b)).astype(jnp.float32)
    xs = xbc[..., :SSD_INNER].reshape(b, seq_len, SSD_HEADS, SSD_HEAD_DIM)
    bm = xbc[..., SSD_INNER:SSD_INNER + SSD_GROUPS * SSD_STATE].reshape(b, seq_len, SSD_GROUPS, SSD_STATE)
    cm = xbc[..., SSD_INNER + SSD_GROUPS * SSD_STATE:].reshape(b, seq_len, SSD_GROUPS, SSD_STATE)
    dt = jax.nn.softplus(dt_raw.astype(jnp.float32) + dt_bias.astype(jnp.float32))
    a = -jnp.exp(a_log.astype(jnp.float32))
    pad = (-seq_len) % CHUNK
    padf = lambda t: jnp.pad(t, [(0, 0), (pad, 0)] + [(0, 0)] * (t.ndim - 2))
    y = ssd_chunked(padf(xs), padf(dt), a, padf(bm), padf(cm))[:, pad:]
    y = y + d_skip.astype(jnp.float32)[:, None] * xs
    y = y.reshape(b, seq_len, SSD_INNER) * jax.nn.silu(z.astype(jnp.float32))
    yg = y.reshape(b, seq_len, SSD_GROUPS, SSD_INNER // SSD_GROUPS)
    yg = yg * lax.rsqrt(jnp.mean(yg * yg, axis=-1, keepdims=True) + EPS)
    return (yg.reshape(b, seq_len, SSD_INNER) * norm_w.astype(jnp.float32)).astype(z.dtype)


def s5_branch(u, a_re, a_im, log_step, b_re, b_im, c_re, c_im, d_skip, w_glu):
    bsz, seq_len, _ = u.shape
    uf = u.astype(jnp.float32).reshape(bsz, seq_len, S5_GROUPS, S5_GROUP)
    step = jnp.exp(log_step.astype(jnp.float32))[:, None]
    lam_re = a_re.astype(jnp.float32)
    lam_im = a_im.astype(jnp.float32)
    mag = jnp.exp(lam_re * step)
    ab_re = mag * jnp.cos(lam_im * step)
    ab_im = mag * jnp.sin(lam_im * step)
    den = lam_re * lam_re + lam_im * lam_im
    nr = ab_re - 1.0
    f_re = (nr * lam_re + ab_im * lam_im) / den
    f_im = (ab_im * lam_re - nr * lam_im) / den
    br = b_re.astype(jnp.float32)
    bi = b_im.astype(jnp.float32)
    bb_re = f_re[..., None] * br - f_im[..., None] * bi
    bb_im = f_re[..., None] * bi + f_im[..., None] * br
    bu_re = jnp.einsum("bltc,tnc->bltn", uf, bb_re)
    bu_im = jnp.einsum("bltc,tnc->bltn", uf, bb_im)
    a_seq_re = jnp.broadcast_to(ab_re[None, None], (1, seq_len, S5_GROUPS, S5_STATE))
    a_seq_im = jnp.broadcast_to(ab_im[None, None], (1, seq_len, S5_GROUPS, S5_STATE))

    def combine(left, right):
        ar1, ai1, br1, bi1 = left
        ar2, ai2, br2, bi2 = right
        return (ar2 * ar1 - ai2 * ai1, ar2 * ai1 + ai2 * ar1,
                ar2 * br1 - ai2 * bi1 + br2, ar2 * bi1 + ai2 * br1 + bi2)

    _, _, h_re, h_im = lax.associative_scan(combine, (a_seq_re, a_seq_im, bu_re, bu_im), axis=1)
    y = (jnp.einsum("bltn,tcn->bltc", h_re, c_re.astype(jnp.float32))
         - jnp.einsum("bltn,tcn->bltc", h_im, c_im.astype(jnp.float32)))
    y = y + d_skip.astype(jnp.float32).reshape(S5_GROUPS, S5_GROUP) * uf
    y = jax.nn.gelu(y.reshape(bsz, seq_len, S5_WIDTH)).astype(u.dtype)
    val, gate = jnp.split(y @ w_glu, 2, axis=-1)
    return val * jax.nn.sigmoid(gate)


def setup_inputs(seed: int = 0) -> dict:
    key = jax.random.key(seed)
    ks = jax.random.split(key, 26)
    f32 = jnp.float32
    nrm = lambda k, shape, s: jax.random.normal(k, shape, f32) * s
    dt0 = jnp.exp(jax.random.uniform(ks[5], (DEPTH, SSD_HEADS), f32, math.log(1e-3), math.log(1e-1)))
    return {
        "x": nrm(ks[0], (BATCH, SEQ, D_MODEL), 1.0),
        "meta_tokens": nrm(ks[1], (N_META, D_MODEL), 1.0),
        "w_in": nrm(ks[2], (DEPTH, D_MODEL, IN_WIDTH), D_MODEL ** -0.5),
        "conv_w": nrm(ks[3], (DEPTH, SSD_CONV, XBC_WIDTH), SSD_CONV ** -0.5),
        "conv_b": nrm(ks[4], (DEPTH, XBC_WIDTH), 0.01),
        "dt_bias": dt0 + jnp.log(-jnp.expm1(-dt0)),
        "ssd_a_log": jnp.log(jax.random.uniform(ks[6], (DEPTH, SSD_HEADS), f32, 1.0, 16.0)),
        "ssd_d": 1.0 + nrm(ks[7], (DEPTH, SSD_HEADS), 0.1),
        "ssd_norm_w": 1.0 + nrm(ks[8], (DEPTH, SSD_INNER), 0.02),
        "s5_a_re": -0.5 + nrm(ks[9], (DEPTH, S5_GROUPS, S5_STATE), 0.01),
        "s5_a_im": jnp.pi * jnp.arange(S5_STATE, dtype=f32) + nrm(ks[10], (DEPTH, S5_GROUPS, S5_STATE), 0.01),
        "s5_log_step": jax.random.uniform(ks[11], (DEPTH, S5_GROUPS), f32, math.log(1e-3), math.log(1e-1)),
        "s5_b_re": nrm(ks[12], (DEPTH, S5_GROUPS, S5_STATE, S5_GROUP), (2 * S5_GROUP) ** -0.5),
        "s5_b_im": nrm(ks[13], (DEPTH, S5_GROUPS, S5_STATE, S5_GROUP), (2 * S5_GROUP) ** -0.5),
        "s5_c_re": nrm(ks[14], (DEPTH, S5_GROUPS, S5_GROUP, S5_STATE), S5_STATE ** -0.5),
        "s5_c_im": nrm(ks[15], (DEPTH, S5_GROUPS, S5_GROUP, S5_STATE), S5_STATE ** -0.5),
        "s5_d": nrm(ks[16], (DEPTH, S5_WIDTH), 1.0),
        "w_glu": nrm(ks[17], (DEPTH, S5_WIDTH, 2 * D_MODEL), S5_WIDTH ** -0.5),
        "w_out": nrm(ks[18], (DEPTH, D_MODEL, D_MODEL), D_MODEL ** -0.5),
        "norm_mix_w": 1.0 + nrm(ks[19], (DEPTH, D_MODEL), 0.02),
        "norm_mlp_w": 1.0 + nrm(ks[20], (DEPTH, D_MODEL), 0.02),
        "w_ff_in": nrm(ks[21], (DEPTH, D_MODEL, D_FF), D_MODEL ** -0.5),
        "w_ff_out": nrm(ks[22], (DEPTH, D_FF, D_MODEL), D_FF ** -0.5),
        "final_norm_w": 1.0 + nrm(ks[23], (D_MODEL,), 0.02),
    }


def reference(x, meta_tokens, w_in, conv_w, conv_b, dt_bias, ssd_a_log, ssd_d, ssd_norm_w,
              s5_a_re, s5_a_im, s5_log_step, s5_b_re, s5_b_im, s5_c_re, s5_c_im, s5_d, w_glu,
              w_out, norm_mix_w, norm_mlp_w, w_ff_in, w_ff_out, final_norm_w):
    bsz = x.shape[0]
    meta = jnp.broadcast_to(meta_tokens[None].astype(x.dtype), (bsz, N_META, D_MODEL))
    h = jnp.concatenate([meta, x], axis=1)
    for i in range(DEPTH):
        xn = rmsnorm(h, norm_mix_w[i])
        z, xbc, dt_raw, u_s5, gates = jnp.split(xn @ w_in[i], IN_SPLITS, axis=-1)
        y_a = ssd_branch(z, xbc, dt_raw, conv_w[i], conv_b[i], dt_bias[i], ssd_a_log[i], ssd_d[i], ssd_norm_w[i])
        y_b = s5_branch(u_s5, s5_a_re[i], s5_a_im[i], s5_log_step[i], s5_b_re[i], s5_b_im[i],
                        s5_c_re[i], s5_c_im[i], s5_d[i], w_glu[i])
        g_a, g_b = jnp.split(jax.nn.sigmoid(gates), 2, axis=-1)
        h = h + ((g_a * y_a + g_b * y_b) @ w_out[i]).astype(h.dtype)
        hn = rmsnorm(h, norm_mlp_w[i])
        h = h + jnp.square(jax.nn.relu(hn @ w_ff_in[i])) @ w_ff_out[i]
    h = rmsnorm(h, final_norm_w)
    return h[:, N_META:]
```

```python
from contextlib import ExitStack
import numpy as np
import ml_dtypes
import concourse.bass as bass
import concourse.mybir as mybir
from concourse.bass_utils import run_bass_kernel_spmd

F32 = mybir.dt.float32
BF = mybir.dt.bfloat16
ALU = mybir.AluOpType
AF = mybir.ActivationFunctionType

D = 1024
NMETA = 16
DEPTH = 2
EPS = 1e-6
NBLK = 35
TWO_PI = 6.283185307179586
PI = 3.141592653589793


class Tok:
    __slots__ = ("sem", "val", "eng")

    def __init__(self, sem, val, eng):
        self.sem, self.val, self.eng = sem, val, eng


class Buf:
    def __init__(self, name):
        self.name = name
        self.w = None
        self.r = []


class Eng:
    def __init__(self, nc, es, e, name):
        self.nc, self.es, self.e, self.name = nc, es, e, name
        self.k = 0
        self._new_sem()
        self.seen = {}
        self.last_sig = True

    def _new_sem(self):
        self.sem = self.es.enter_context(self.nc.semaphore(f"{self.name}_s{self.k}"))
        self.k += 1
        self.cnt = 0

    def wait(self, tok):
        if tok is None:
            return
        key = id(tok.sem)
        if self.seen.get(key, 0) >= tok.val:
            return
        if tok.eng is self and self.name == "pe":
            return
        if tok.eng is not None:
            assert tok.sem is not tok.eng.sem or tok.val <= tok.eng.cnt, (self.name, tok.eng.name)
        self.e.wait_ge(tok.sem, tok.val)
        self.seen[key] = tok.val

    def op(self, fn, reads=(), writes=(), sig=True):
        for b in reads:
            self.wait(b.w)
        for b in writes:
            self.wait(b.w)
            for t in b.r:
                self.wait(t)
        if sig and self.last_sig and self.cnt >= 30000:
            self._new_sem()
        inst = fn()
        if sig:
            self.cnt += 1
            inst.then_inc(self.sem, 1)
            tok = Tok(self.sem, self.cnt, self)
            self.last_sig = True
        else:
            tok = Tok(self.sem, self.cnt + 1, self)
            self.last_sig = False
        for b in reads:
            b.r = [t for t in b.r if t.sem is not tok.sem] + [tok]
        for b in writes:
            b.w = tok
            b.r = []
        return tok


class DmaQ:
    def __init__(self, nc, es, eng):
        self.nc, self.es, self.eng = nc, es, eng
        self.sems = {}

    def dma(self, slot, out, in_, reads=(), writes=()):
        E = self.eng
        for b in reads:
            E.wait(b.w)
        for b in writes:
            E.wait(b.w)
            for t in b.r:
                E.wait(t)
        if slot not in self.sems:
            self.sems[slot] = [self.es.enter_context(self.nc.semaphore(f"dq_{slot}")), 0]
        s = self.sems[slot]
        s[1] += 16
        E.e.dma_start(out=out, in_=in_).then_inc(s[0], 16)
        tok = Tok(s[0], s[1], None)
        for b in reads:
            b.r = [t for t in b.r if t.sem is not tok.sem] + [tok]
        for b in writes:
            b.w = tok
            b.r = []
        return tok


def build_program(NSEQ, NMT, NT=4, depth=DEPTH, debug_taps=False):
    N = NT * 128
    S = NMT * N
    nc = bass.Bass("TRN2", target_bir_lowering=False)
    dt_in = lambda name, shape, dt=F32: nc.dram_tensor(name, shape, dt, kind="ExternalInput")
    x_d = dt_in("x", [NSEQ, S, D])
    meta_d = dt_in("meta_pad", [128, D])
    w_in_d = dt_in("w_in", [depth, D, 5648])
    w_glu_d = dt_in("w_glu", [depth, 512, 2048])
    w_out_d = dt_in("w_out", [depth, D, D])
    w_ffi_d = dt_in("w_ff_in", [depth, D, 4096])
    w_ffo_d = dt_in("w_ff_out", [depth, 4096, D])
    nw_d = dt_in("nw", [2 * depth + 1, D])
    snw_d = dt_in("snw", [depth, D])
    convp_d = dt_in("convp", [depth, 128, 16, 5])
    ssdp_d = dt_in("ssdp", [depth, 3, 16])
    s5s_d = dt_in("s5s", [depth, 3, 128, 16])
    s5b_d = dt_in("s5b", [depth, 2, 128, 16, 16])
    s5c_d = dt_in("s5c", [depth, 2, 128, 16, 16])
    s5d_d = dt_in("s5d", [depth, 128, 4])
    cf_d = dt_in("cf", [128, 4 * 128 + 512 + 1])
    cb_d = dt_in("cb", [128, 128], BF)
    out_d = nc.dram_tensor("out", [NSEQ, S, D], F32, kind="ExternalOutput")
    wblk_d = nc.dram_tensor("wblk", [depth, NBLK, 128, 4096], BF)
    tabs_d = nc.dram_tensor("tabs", [depth, 16, 128, 2, 512], F32)

    es = ExitStack()
    with es:
        sb = lambda name, shape, dt=F32: es.enter_context(nc.sbuf_tensor("s_" + name, shape, dt))
        PE = Eng(nc, es, nc.tensor, "pe")
        ACT = Eng(nc, es, nc.scalar, "act")
        DVE = Eng(nc, es, nc.vector, "dve")
        POOL = Eng(nc, es, nc.gpsimd, "pool")
        SP = Eng(nc, es, nc.sync, "sp")
        QW = DmaQ(nc, es, SP)
        QP = DmaQ(nc, es, POOL)

        cf = sb("cf", [128, 4 * 128 + 512 + 1]); cfB = Buf("cf")
        identb = sb("identb", [128, 128], BF); identbB = Buf("identb")
        QP.dma("c0", cf[:], cf_d.ap(), writes=[cfB])
        QP.dma("c1", identb[:], cb_d.ap(), writes=[identbB])
        identf = cf[:, 0:128]
        tri = cf[:, 128:256]
        strict = cf[:, 256:384]
        ones = cf[:, 384:512]
        tpos = cf[:, 512:1024]
        padmask = cf[:, 1024:1025]

        convp = sb("convp", [128, depth, 16, 5]); convpB = Buf("convp")
        ssdp = sb("ssdp", [128, depth, 3, 16]); ssdpB = Buf("ssdp")
        s5dd = sb("s5dd", [128, depth, 4]); s5ddB = Buf("s5dd")
        dtw = sb("dtw", [128, depth, 8, 16], BF); dtwB = Buf("dtw")
        for l in range(depth):
            QP.dma("c2", convp[:, l], convp_d.ap()[l], writes=[convpB])
            QP.dma("c3", ssdp[:, l].rearrange("p a b -> p (a b)"),
                   ssdp_d.ap()[l].rearrange("a b -> (a b)").partition_broadcast(128), writes=[ssdpB])
            QP.dma("c4", s5dd[:, l], s5d_d.ap()[l], writes=[s5ddB])
            QP.dma("c5", dtw[:, l], w_in_d.ap()[l][:, 3072:3088].rearrange("(kc p) c -> p kc c", p=128),
                   writes=[dtwB])
        arep = sb("arep", [128, depth, 16]); arepB = Buf("arep")
        ACT.op(lambda: nc.scalar.activation(arep[:], ssdp[:, :, 1, :], AF.Exp), [ssdpB], [arepB])
        DVE.op(lambda: nc.vector.tensor_scalar(arep[:], arep[:], -1.0, None, ALU.mult), [arepB], [arepB])

        wscB = Buf("wscratch")
        def blkview(l, b, kc):
            return wblk_d.ap()[l, b][:, 0:kc * 512].rearrange("p (kc c) -> p kc c", kc=kc)
        def wsrc(wd, l, r0, kc, c0):
            return wd.ap()[l][r0:r0 + kc * 128, c0:c0 + 512].rearrange("(kc p) c -> p kc c", p=128)
        for l in range(depth):
            cols = [0, 512] + [1024 + 512 * i for i in range(4)] + [3088] + [3600 + 512 * i for i in range(4)]
            for b, c0 in enumerate(cols):
                QP.dma("pre", blkview(l, b, 8), wsrc(w_in_d, l, 0, 8, c0))
            for b, cbi in enumerate([2, 3, 0, 1]):
                QP.dma("pre", blkview(l, 11 + b, 4), wsrc(w_glu_d, l, 0, 4, cbi * 512))
            for b in range(2):
                QP.dma("pre", blkview(l, 15 + b, 8), wsrc(w_out_d, l, 0, 8, b * 512))
            for kg in range(4):
                for c in range(2):
                    QP.dma("pre", blkview(l, 17 + kg * 4 + c, 8), wsrc(w_ffi_d, l, 0, 8, kg * 1024 + c * 512))
                for c in range(2):
                    QP.dma("pre", blkview(l, 17 + kg * 4 + 2 + c, 8), wsrc(w_ffo_d, l, kg * 1024, 8, c * 512))

        psum = [es.enter_context(nc.psum_tensor(f"ps{i}", [128, 512], F32)) for i in range(8)]
        psB = [Buf(f"ps{i}") for i in range(8)]
        pctr = [0]
        def getps():
            i = pctr[0] % 8
            pctr[0] += 1
            return psum[i], psB[i]

        s5s = sb("s5s", [128, depth, 3, 16]); s5sB = Buf("s5s")
        rtab = sb("rtab", [128, depth, 16]); rtabB = Buf("rtab")
        gst = sb("gst", [128, depth, 16, 2]); gstB = [Buf(f"gst{l}") for l in range(depth)]
        gst0 = sb("gst0", [128, depth, 16, 2]); gst0B = Buf("gst0")
        with ExitStack() as es2:
            sb2 = lambda name, shape, dt=F32: es2.enter_context(nc.sbuf_tensor("s_" + name, shape, dt))
            th = sb2("th", [128, 16]); thB = Buf("th")
            stp = sb2("stp", [128, 16]); stpB = Buf("stp")
            ang = sb2("ang", [128, 2, 512]); angB = Buf("ang")
            ang2 = sb2("ang2", [128, 2, 512]); ang2B = Buf("ang2")
            tabt = [sb2(f"tabt{i}", [128, 2, 512]) for i in range(2)]; tabtB = [Buf(f"tabt{i}") for i in range(2)]
            ab = sb2("ab", [128, 2, 16]); abB = Buf("ab")
            ff = sb2("ff", [128, 6, 16]); ffB = Buf("ff")
            bc_in = sb2("bc_in", [128, 4, 16, 16]); bcinB = Buf("bc_in")
            bbar = sb2("bbar", [128, 2, 16, 16]); bbarB = Buf("bbar")
            tmp = sb2("tmp", [128, 2, 16, 16]); tmpB = Buf("tmp5")
            bd = [sb2(f"bd{i}", [128, 128]) for i in range(2)]; bdB = [Buf(f"bd{i}") for i in range(2)]
            BLs = sb2("BLs", [128, 32, 128], BF); BLsB = Buf("BLs")
            CLs = sb2("CLs", [128, 32, 128], BF); CLsB = Buf("CLs")
            for l in range(depth):
                for a in range(3):
                    QP.dma("c6", s5s[:, l, a], s5s_d.ap()[l, a], writes=[s5sB])
                for a in range(2):
                    QP.dma("c7", bc_in[:, a], s5b_d.ap()[l, a], writes=[bcinB])
                    QP.dma("c7", bc_in[:, 2 + a], s5c_d.ap()[l, a], writes=[bcinB])
                ACT.op(lambda: nc.scalar.activation(stp[:], s5s[:, l, 2, :], AF.Exp), [s5sB], [stpB])
                DVE.op(lambda: nc.vector.tensor_tensor(th[:], s5s[:, l, 1, :], stp[:], ALU.mult), [s5sB, stpB], [thB])
                DVE.op(lambda: nc.vector.tensor_tensor(stp[:], s5s[:, l, 0, :], stp[:], ALU.mult), [s5sB, stpB], [stpB])
                ACT.op(lambda: nc.scalar.activation(rtab[:, l, :], stp[:], AF.Exp), [stpB], [rtabB])
                for j in range(16):
                    tt, ttB = tabt[j % 2], tabtB[j % 2]
                    DVE.op(lambda: nc.vector.tensor_scalar(ang[:, 0, :], tpos, th[:, j:j + 1], 0.5 * PI, ALU.mult, ALU.add),
                           [cfB, thB], [angB])
                    DVE.op(lambda: nc.vector.tensor_scalar(ang[:, 1, :], tpos, th[:, j:j + 1], None, ALU.mult),
                           [cfB, thB], [angB])
                    MAGIC = 12582912.0
                    DVE.op(lambda: nc.vector.tensor_scalar(ang2[:], ang[:], 1.0 / TWO_PI, MAGIC, ALU.mult, ALU.add), [angB], [ang2B])
                    DVE.op(lambda: nc.vector.tensor_scalar(ang2[:], ang2[:], -MAGIC, -TWO_PI, ALU.add, ALU.mult), [ang2B], [ang2B])
                    DVE.op(lambda: nc.vector.tensor_tensor(ang[:], ang[:], ang2[:], ALU.add), [angB, ang2B], [angB])
                    DVE.op(lambda: nc.vector.tensor_scalar(ang[:], ang[:], -PI, PI, ALU.max, ALU.min), [angB], [angB])
                    ACT.op(lambda: nc.scalar.activation(tt[:], ang[:], AF.Sin), [angB], [ttB])
                    QP.dma("tabw", tabs_d.ap()[l, j], tt[:], reads=[ttB])
                    POOL.op(lambda: nc.gpsimd.tensor_copy(ab[:, :, j:j + 1], tt[:, :, 0:1]), [ttB], [abB])
                DVE.op(lambda: nc.vector.tensor_tensor(ab[:], ab[:], rtab[:, l, :].unsqueeze(1).broadcast_to([128, 2, 16]), ALU.mult),
                       [abB, rtabB], [abB])
                are, aim = s5s[:, l, 0, :], s5s[:, l, 1, :]
                V = nc.vector
                DVE.op(lambda: V.tensor_scalar(ff[:, 0], ab[:, 0], -1.0, None, ALU.add), [abB], [ffB])
                DVE.op(lambda: V.tensor_tensor(ff[:, 1], are, are, ALU.mult), [s5sB], [ffB])
                DVE.op(lambda: V.tensor_tensor(ff[:, 4], aim, aim, ALU.mult), [s5sB], [ffB])
                DVE.op(lambda: V.tensor_tensor(ff[:, 1], ff[:, 1], ff[:, 4], ALU.add), [ffB], [ffB])
                DVE.op(lambda: V.reciprocal(ff[:, 1], ff[:, 1]), [ffB], [ffB])
                DVE.op(lambda: V.tensor_tensor(ff[:, 2], ff[:, 0], are, ALU.mult), [ffB, s5sB], [ffB])
                DVE.op(lambda: V.tensor_tensor(ff[:, 4], ab[:, 1], aim, ALU.mult), [abB, s5sB], [ffB])
                DVE.op(lambda: V.tensor_tensor(ff[:, 2], ff[:, 2], ff[:, 4], ALU.add), [ffB], [ffB])
                DVE.op(lambda: V.tensor_tensor(ff[:, 2], ff[:, 2], ff[:, 1], ALU.mult), [ffB], [ffB])
                DVE.op(lambda: V.tensor_tensor(ff[:, 3], ab[:, 1], are, ALU.mult), [abB, s5sB], [ffB])
                DVE.op(lambda: V.tensor_tensor(ff[:, 4], ff[:, 0], aim, ALU.mult), [ffB, s5sB], [ffB])
                DVE.op(lambda: V.tensor_tensor(ff[:, 3], ff[:, 3], ff[:, 4], ALU.subtract), [ffB], [ffB])
                DVE.op(lambda: V.tensor_tensor(ff[:, 3], ff[:, 3], ff[:, 1], ALU.mult), [ffB], [ffB])
                if debug_taps and l == 0:
                    for nm, ap_, bb in (("th", th[:], thB), ("ff", ff[:], ffB), ("ab", ab[:], abB), ("s5s", s5s[:, 0], s5sB), ("tab15", tabt[1][:], tabtB[1])):
                        d_ = nc.dram_tensor("tap_pre_" + nm, list(ap_.shape), ap_.dtype, kind="ExternalOutput")
                        QP.dma("tap", d_.ap(), ap_, reads=[bb])
                fre = ff[:, 2].unsqueeze(2).broadcast_to([128, 16, 16])
                fim = ff[:, 3].unsqueeze(2).broadcast_to([128, 16, 16])
                DVE.op(lambda: V.tensor_tensor(bbar[:, 0], bc_in[:, 0], fre, ALU.mult), [bcinB, ffB], [bbarB])
                DVE.op(lambda: V.tensor_tensor(tmp[:, 0], bc_in[:, 1], fim, ALU.mult), [bcinB, ffB], [tmpB])
                DVE.op(lambda: V.tensor_tensor(bbar[:, 0], bbar[:, 0], tmp[:, 0], ALU.subtract), [bbarB, tmpB], [bbarB])
                DVE.op(lambda: V.tensor_tensor(bbar[:, 1], bc_in[:, 1], fre, ALU.mult), [bcinB, ffB], [bbarB])
                DVE.op(lambda: V.tensor_tensor(tmp[:, 1], bc_in[:, 0], fim, ALU.mult), [bcinB, ffB], [tmpB])
                DVE.op(lambda: V.tensor_tensor(bbar[:, 1], bbar[:, 1], tmp[:, 1], ALU.add), [bbarB, tmpB], [bbarB])
                DVE.op(lambda: V.tensor_scalar(bc_in[:, 3], bc_in[:, 3], -1.0, None, ALU.mult), [bcinB], [bcinB])
                POOL.op(lambda: nc.gpsimd.memset(CLs[:], 0.0), [], [CLsB])
                for j in range(16):
                    q = j % 4
                    for part in range(2):
                        for two in range(2):
                            c0 = 32 * q + 16 * two
                            POOL.op(lambda: nc.gpsimd.tensor_copy(
                                CLs[64 * two:64 * two + 64, 2 * j + part, c0:c0 + 16],
                                bc_in[64 * two:64 * two + 64, 2 + part, j, :]), [bcinB], [CLsB])
                for j in range(16):
                    q = j % 4
                    for part in range(2):
                        k = (2 * j + part) % 2
                        POOL.op(lambda: nc.gpsimd.memset(bd[k][:], 0.0), [], [bdB[k]])
                        for two in range(2):
                            c0 = 32 * q + 16 * two
                            POOL.op(lambda: nc.gpsimd.tensor_copy(
                                bd[k][64 * two:64 * two + 64, c0:c0 + 16],
                                bbar[64 * two:64 * two + 64, part, j, :]), [bbarB], [bdB[k]])
                        ps, pB = getps()
                        PE.op(lambda: nc.tensor.transpose(ps[:, 0:128], bd[k][:], identf), [bdB[k], cfB], [pB])
                        ACT.op(lambda: nc.scalar.copy(BLs[:, 2 * j + part, :], ps[:, 0:128]), [pB], [BLsB])
                if debug_taps and l == 0:
                    for nm, ap_, bb in (("BLs", BLs[:], BLsB), ("CLs", CLs[:], CLsB), ("bbar", bbar[:], bbarB)):
                        d_ = nc.dram_tensor("tap_pre_" + nm, list(ap_.shape), ap_.dtype, kind="ExternalOutput")
                        QP.dma("tap", d_.ap(), ap_, reads=[bb])
                for hf in range(4):
                    QP.dma("tabw", wblk_d.ap()[l, 33][:, hf * 1024:(hf + 1) * 1024].rearrange("p (a b) -> p a b", a=8),
                           BLs[:, hf * 8:(hf + 1) * 8, :], reads=[BLsB])
                    QP.dma("tabw", wblk_d.ap()[l, 34][:, hf * 1024:(hf + 1) * 1024].rearrange("p (a b) -> p a b", a=8),
                           CLs[:, hf * 8:(hf + 1) * 8, :], reads=[CLsB])
            for s in QP.sems.values():
                SP.e.wait_ge(s[0], s[1])
                POOL.e.wait_ge(s[0], s[1])

        h = sb("h", [128, NT, D]); hB = [Buf(f"h{i}") for i in range(NT)]
        xT = sb("xT", [128, 8, N], BF); xTB = Buf("xT")
        zs = sb("zs", [128, NT, D], BF); zsB = [Buf(f"zs{i}") for i in range(NT)]
        obuf = sb("obuf", [128, NT, D]); gts = obuf[:].bitcast(BF); gtsB = [Buf(f"gts{i}") for i in range(NT)]
        xc = sb("xc", [128, 16, N], BF); xcB = Buf("xc")
        xtok = sb("xtok", [128, NT, D], BF); xtokB = [Buf(f"xtok{i}") for i in range(NT)]
        btok = sb("btok", [128, NT, 512], BF); btokB = [Buf(f"btok{i}") for i in range(NT)]
        uTf = sb("uTf", [128, 4, N]); uTfB = Buf("uTf")
        uTb = sb("uTb", [128, 4, N], BF); uTbB = Buf("uTb")
        ybT = sb("ybT", [128, 4, N], BF); ybTB = Buf("ybT")
        yb = sb("yb", [128, NT, D], BF); ybB = [Buf(f"yb{i}") for i in range(NT)]
        dts = sb("dts", [128, NT, 16]); dtsB = [Buf(f"dts{i}") for i in range(NT)]
        NW = 4
        wring = [sb(f"wr{i}", [128, 4096], BF) for i in range(NW)]; wringB = [Buf(f"wr{i}") for i in range(NW)]
        tring = [sb(f"tr{i}", [128, 2, N]) for i in range(3)]; tringB = [Buf(f"tr{i}") for i in range(3)]
        nwr = [sb(f"nwr{i}", [128, D]) for i in range(2)]; nwrB = [Buf(f"nwr{i}") for i in range(2)]
        snw = sb("snwr", [128, D]); snwB = Buf("snw")
        halo = sb("halo", [128, depth, 16, 3]); haloB = [Buf(f"halo{l}") for l in range(depth)]
        halo0 = sb("halo0", [128, depth, 16, 3]); halo0B = Buf("halo0")
        Hs = sb("Hs", [128, depth, 16, 64]); HsB = [Buf(f"Hs{l}") for l in range(depth)]
        Hs0 = sb("Hs0", [128, depth, 16, 64]); Hs0B = Buf("Hs0")
        Hb = sb("Hb", [128, depth, 16, 64], BF); HbB = [Buf(f"Hb{l}") for l in range(depth)]
        junk = sb("junk", [128, D], BF); junkB = Buf("junk")
        ss = sb("ss", [128, 8]); ssB = Buf("ss")
        xnb = [sb(f"xnb{i}", [128, D], BF) for i in range(2)]; xnbB = [Buf(f"xnb{i}") for i in range(2)]
        xr = [sb(f"xr{i}", [128, N + 3]) for i in range(2)]; xrB = [Buf(f"xr{i}") for i in range(2)]
        acc = [sb(f"acc{i}", [128, N]) for i in range(2)]; accB = [Buf(f"acc{i}") for i in range(2)]
        sm = sb("sm", [128, 8, 16]); smB = Buf("sm")
        lseg = sb("lseg", [128, 16, 128]); lsegB = Buf("lseg")
        Lm = sb("Lm", [128, 16, 128]); LmB = Buf("Lm")
        sg = lseg[:].rearrange("p a b -> p (a b)").bitcast(BF).rearrange("p (i c) -> p i c", c=D)
        sgB = [lsegB] * NT
        Mm = sb("Mm", [128, 16, 128], BF); MmB = Buf("Mm")
        scm = sb("scm", [128, 4, 128]); scmB = Buf("scm")
        xdt = sb("xdt", [128, 16, 64], BF); xdtB = Buf("xdt")
        xw = sb("xw", [128, 16, 64], BF); xwB = Buf("xw")
        yv = sb("yv", [128, D]); yvB = Buf("yv")
        y2 = sb("y2", [128, D]); y2B = Buf("y2")
        s5t = [sb(f"s5t{i}", [128, N]) for i in range(8)]; s5tB = [Buf(f"s5t{i}") for i in range(8)]
        hS = sb("hS", [128, 2, 4, N], BF); hSB = Buf("hS")
        rl, rlB = xr, xrB

        V = nc.vector
        A = nc.scalar
        G = nc.gpsimd
        T = nc.tensor

        POOL.op(lambda: G.memset(halo[:], 0.0), [], haloB)
        POOL.op(lambda: G.memset(Hs[:], 0.0), [], HsB)
        POOL.op(lambda: G.memset(Hb[:], 0.0), [], HbB)
        DVE.op(lambda: V.memset(gst[:].rearrange("p a b c -> p (a b c)"), 0.0), [], gstB)
        POOL.op(lambda: G.memset(ss[:], 0.0), [], [ssB])

        wstate = {"next_load": 0, "next_use": 0, "sched": []}

        def schedule_layer(l):
            for b in list(range(11)) + [33, 34] + list(range(11, 33)):
                wstate["sched"].append((l, b))

        def pump():
            while wstate["next_load"] < len(wstate["sched"]) and wstate["next_load"] < wstate["next_use"] + NW - 1:
                k = wstate["next_load"]
                l, b = wstate["sched"][k]
                QW.dma(f"w{k % NW}", wring[k % NW][:], wblk_d.ap()[l, b], writes=[wringB[k % NW]])
                wstate["next_load"] += 1

        def getblk(l, b):
            k = wstate["next_use"]
            assert wstate["sched"][k] == (l, b), (wstate["sched"][k], l, b)
            pump()
            wstate["next_use"] += 1
            return wring[k % NW], wringB[k % NW]

        nwstate = [0]
        def load_nw(row):
            i = nwstate[0] % 2
            nwstate[0] += 1
            QP.dma(f"nw{i}", nwr[i][:], nw_d.ap()[row].partition_broadcast(128), writes=[nwrB[i]])
            return nwr[i], nwrB[i]

        xnctr = [0]
        def norm_stats(i):
            ACT.op(lambda: A.activation(junk[:], h[:, i, :], AF.Square, accum_out=ss[:, i:i + 1]), [hB[i]], [junkB, ssB])
            DVE.op(lambda: V.tensor_scalar(ss[:, 4 + i:5 + i], ss[:, i:i + 1], 1.0 / D, EPS, ALU.mult, ALU.add), [ssB], [ssB])
            ACT.op(lambda: A.activation(ss[:, 4 + i:5 + i], ss[:, 4 + i:5 + i], AF.Sqrt), [ssB], [ssB])
            DVE.op(lambda: V.reciprocal(ss[:, 4 + i:5 + i], ss[:, 4 + i:5 + i]), [ssB], [ssB])

        def transposes_to(srcs, srcB, dst_ap3, dstB, nk):
            ps, pB = getps()
            psb = ps[:].bitcast(BF)
            for k in range(nk):
                PE.op(lambda: T.transpose(psb[:, k * 128:(k + 1) * 128], srcs[k], identb[:]),
                      [srcB, identbB], [pB], sig=(k == nk - 1))
            ACT.op(lambda: A.copy(dst_ap3, psb[:, 0:nk * 128].rearrange("p (k t) -> p k t", k=nk)), [pB], [dstB])

        def rmsnorm_T(nt, row):
            wr, wrB = load_nw(row)
            for i in range(nt):
                norm_stats(i)
                xb, xbB = xnb[xnctr[0] % 2], xnbB[xnctr[0] % 2]
                xnctr[0] += 1
                DVE.op(lambda: V.scalar_tensor_tensor(xb[:], h[:, i, :], ss[:, 4 + i:5 + i], wr[:], ALU.mult, ALU.mult),
                       [hB[i], ssB, wrB], [xbB])
                transposes_to([xb[:, k * 128:(k + 1) * 128] for k in range(8)], xbB,
                              xT[:, :, i * 128:(i + 1) * 128], xTB, 8)

        def mm_tok(blk, blkB, i, kc_n, src, srcB):
            ps, pB = getps()
            bv = blk[:, 0:kc_n * 512].rearrange("p (kc c) -> p kc c", kc=kc_n)
            for kc in range(kc_n):
                PE.op(lambda: T.matmul(ps[:], src[:, kc, i * 128:(i + 1) * 128], bv[:, kc, :],
                                       start=(kc == 0), stop=(kc == kc_n - 1)),
                      [srcB, blkB], [pB], sig=(kc == kc_n - 1))
            return ps, pB

        def mm_feat(blk, blkB, f, n, src, srcB):
            ps, pB = getps()
            bv = blk[:].rearrange("p (kc c) -> p kc c", kc=8)
            for kc in range(8):
                PE.op(lambda: T.matmul(ps[:, 0:n], bv[:, kc, f * 128:(f + 1) * 128], src[:, kc, 0:n],
                                       start=(kc == 0), stop=(kc == 7)),
                      [srcB, blkB], [pB], sig=(kc == 7))
            return ps, pB

        cctr = [0]
        tctr = [0]
        tapn = [0]
        def tap(name, ap, bufs):
            if not debug_taps:
                return
            shape = list(ap.shape)
            d = nc.dram_tensor("tap_" + name, shape, ap.dtype, kind="ExternalOutput")
            QP.dma("tap", d.ap(), ap, reads=bufs)
            tapn[0] += 1

        def layer(l, nt, is_meta):
            n = nt * 128
            QP.dma("snw", snw[:], snw_d.ap()[l].partition_broadcast(128), writes=[snwB])
            rmsnorm_T(nt, 2 * l)
            for cb in range(2):
                blk, bB = getblk(l, cb)
                for i in range(nt):
                    ps, pB = mm_tok(blk, bB, i, 8, xT, xTB)
                    ACT.op(lambda: A.activation(zs[:, i, cb * 512:(cb + 1) * 512], ps[:], AF.Silu), [pB], [zsB[i]])
            for cb in range(4):
                blk, bB = getblk(l, 2 + cb)
                for f in range(4):
                    ft = cb * 4 + f
                    ps, pB = mm_feat(blk, bB, f, n, xT, xTB)
                    c = cctr[0] % 2
                    cctr[0] += 1
                    ACT.op(lambda: A.copy(xr[c][:, 3:3 + n], ps[:, 0:n]), [pB], [xrB[c]])
                    POOL.op(lambda: G.tensor_copy(xr[c][:, 0:3], halo[:, l, ft, :]), [haloB[l]], [xrB[c]])
                    POOL.op(lambda: G.tensor_copy(halo[:, l, ft, :], xr[c][:, n:n + 3]), [xrB[c]], [haloB[l]])
                    cw = convp[:, l, ft, :]
                    DVE.op(lambda: V.tensor_scalar(acc[c][:, 0:n], xr[c][:, 0:n], cw[:, 0:1], cw[:, 4:5], ALU.mult, ALU.add),
                           [xrB[c], convpB], [accB[c]])
                    for k in range(1, 4):
                        DVE.op(lambda: V.scalar_tensor_tensor(acc[c][:, 0:n], xr[c][:, k:k + n], cw[:, k:k + 1], acc[c][:, 0:n],
                                                              ALU.mult, ALU.add), [xrB[c], accB[c], convpB], [accB[c]])
                    ACT.op(lambda: A.activation(xc[:, ft, 0:n], acc[c][:, 0:n], AF.Silu), [accB[c]], [xcB])
            blk, bB = getblk(l, 6)
            for ct in range(4):
                ps, pB = mm_feat(blk, bB, ct, n, xT, xTB)
                ACT.op(lambda: A.copy(uTf[:, ct, 0:n], ps[:, 0:n]), [pB], [uTfB])
                ACT.op(lambda: A.copy(uTb[:, ct, 0:n], ps[:, 0:n]), [pB], [uTbB])
            for cb in range(4):
                blk, bB = getblk(l, 7 + cb)
                for i in range(nt):
                    ps, pB = mm_tok(blk, bB, i, 8, xT, xTB)
                    ACT.op(lambda: A.activation(gts[:, i, cb * 512:(cb + 1) * 512], ps[:], AF.Sigmoid), [pB], [gtsB[i]])
            for i in range(nt):
                ps, pB = getps()
                for kc in range(8):
                    PE.op(lambda: T.matmul(ps[:, 0:16], xT[:, kc, i * 128:(i + 1) * 128], dtw[:, l, kc, :],
                                           start=(kc == 0), stop=(kc == 7)), [xTB, dtwB], [pB], sig=(kc == 7))
                d0 = sm[:, 0, :]; d1 = sm[:, 1, :]
                DVE.op(lambda: V.tensor_tensor(d0, ps[:, 0:16], ssdp[:, l, 0, :], ALU.add), [pB, ssdpB], [smB])
                ACT.op(lambda: A.activation(d1, d0, AF.Abs), [smB], [smB])
                ACT.op(lambda: A.activation(d1, d1, AF.Exp, scale=-1.0), [smB], [smB])
                ACT.op(lambda: A.activation(d1, d1, AF.Ln, bias=1.0), [smB], [smB])
                DVE.op(lambda: V.tensor_scalar(d0, d0, 0.0, None, ALU.max), [smB], [smB])
                if is_meta:
                    DVE.op(lambda: V.tensor_tensor(d0, d0, d1, ALU.add), [smB], [smB])
                    DVE.op(lambda: V.tensor_scalar(dts[:, i, :], d0, padmask, None, ALU.mult), [smB, cfB], [dtsB[i]])
                else:
                    DVE.op(lambda: V.tensor_tensor(dts[:, i, :], d0, d1, ALU.add), [smB], [dtsB[i]])
            for i in range(nt):
                transposes_to([xc[:, k, i * 128:(i + 1) * 128] for k in range(8)], xcB,
                              xtok[:, i, :].rearrange("p (k t) -> p k t", k=8), xtokB[i], 8)
                transposes_to([xc[:, 8 + k, i * 128:(i + 1) * 128] for k in range(4)], xcB,
                              btok[:, i, :].rearrange("p (k t) -> p k t", k=4), btokB[i], 4)
            tg = ("m" if is_meta else "s") + str(l)
            if tg + "a" not in taps:
                taps[tg + "a"] = 1
                tap(tg + "_xT", xT[:, :, 0:n], [xTB])
                tap(tg + "_zs", zs[:, 0:nt, :], zsB)
                tap(tg + "_gts", gts[:, 0:nt, :], gtsB)
                tap(tg + "_xc", xc[:, :, 0:n], [xcB])
                tap(tg + "_uTf", uTf[:, :, 0:n], [uTfB])
                tap(tg + "_dts", dts[:, 0:nt, :], dtsB)
                tap(tg + "_xtok", xtok[:, 0:nt, :], xtokB)
            BLk, BLkB = getblk(l, 33)
            CLk, CLkB = getblk(l, 34)
            BLv = BLk[:].rearrange("p (a b) -> p a b", a=32)
            CLv = CLk[:].rearrange("p (a b) -> p a b", a=32)
            def load_tab(j):
                i = tctr[0] % 3
                tctr[0] += 1
                QP.dma(f"tab{i}", tring[i][:, :, 0:n], tabs_d.ap()[l, j][:, :, 0:n], writes=[tringB[i]])
                return tring[i], tringB[i]
            tabq = [load_tab(0), load_tab(1)]
            for j in range(16):
                ct, q = j // 4, j % 4
                tb, tbB = tabq.pop(0)
                if j + 2 < 16:
                    tabq.append(load_tab(j + 2))
                Ec, Es = tb[:, 0, 0:n], tb[:, 1, 0:n]
                psr, prB = getps()
                PE.op(lambda: T.matmul(psr[:, 0:n], BLv[:, 2 * j, :], uTb[:, ct, 0:n], start=True, stop=True), [BLkB, uTbB], [prB])
                psi, piB = getps()
                PE.op(lambda: T.matmul(psi[:, 0:n], BLv[:, 2 * j + 1, :], uTb[:, ct, 0:n], start=True, stop=True), [BLkB, uTbB], [piB])
                t = [x[:, 0:n] for x in s5t]
                tB = s5tB
                TT = V.tensor_tensor
                DVE.op(lambda: TT(t[0], psr[:, 0:n], Ec, ALU.mult), [prB, tbB], [tB[0]])
                DVE.op(lambda: TT(t[1], psi[:, 0:n], Es, ALU.mult), [piB, tbB], [tB[1]])
                DVE.op(lambda: TT(t[0], t[0], t[1], ALU.add), [tB[0], tB[1]], [tB[0]])
                DVE.op(lambda: TT(t[2], psi[:, 0:n], Ec, ALU.mult), [piB, tbB], [tB[2]])
                DVE.op(lambda: TT(t[3], psr[:, 0:n], Es, ALU.mult), [prB, tbB], [tB[3]])
                DVE.op(lambda: TT(t[2], t[2], t[3], ALU.subtract), [tB[2], tB[3]], [tB[2]])
                rj = rtab[:, l, j:j + 1].broadcast_to([128, n])
                DVE.op(lambda: V.tensor_tensor_scan(t[4], rj, t[0], gst[:, l, j, 0:1], ALU.mult, ALU.add),
                       [rtabB, tB[0], gstB[l]], [tB[4]])
                DVE.op(lambda: V.tensor_tensor_scan(t[5], rj, t[2], gst[:, l, j, 1:2], ALU.mult, ALU.add),
                       [rtabB, tB[2], gstB[l]], [tB[5]])
                DVE.op(lambda: TT(t[0], t[4], Ec, ALU.mult), [tB[4], tbB], [tB[0]])
                DVE.op(lambda: TT(t[1], t[5], Es, ALU.mult), [tB[5], tbB], [tB[1]])
                DVE.op(lambda: TT(t[6], t[0], t[1], ALU.subtract), [tB[0], tB[1]], [tB[6]])
                DVE.op(lambda: TT(t[2], t[5], Ec, ALU.mult), [tB[5], tbB], [tB[2]])
                DVE.op(lambda: TT(t[3], t[4], Es, ALU.mult), [tB[4], tbB], [tB[3]])
                DVE.op(lambda: TT(t[7], t[2], t[3], ALU.add), [tB[2], tB[3]], [tB[7]])
                if j in (0, 8) and is_meta and l == 0 and (tg + f"j{j}") not in taps:
                    taps[tg + f"j{j}"] = 1
                    tap(tg + f"_g{j}", s5t[4][:, 0:n], [tB[4]])
                    tap(tg + f"_tb{j}", tb[:, :, 0:n], [tbB])
                    tap(tg + f"_bt{j}", s5t[0][:, 0:n], [tB[0]])
                POOL.op(lambda: G.tensor_copy(gst[:, l, j, 0:1], t[6][:, n - 1:n]), [tB[6]], [gstB[l]])
                POOL.op(lambda: G.tensor_copy(gst[:, l, j, 1:2], t[7][:, n - 1:n]), [tB[7]], [gstB[l]])
                POOL.op(lambda: G.tensor_copy(hS[:, 0, q, 0:n], t[6]), [tB[6]], [hSB])
                POOL.op(lambda: G.tensor_copy(hS[:, 1, q, 0:n], t[7]), [tB[7]], [hSB])
                if q == 3:
                    psy, pyB = getps()
                    idx = 0
                    for qq in range(4):
                        for part in range(2):
                            jj = ct * 4 + qq
                            PE.op(lambda: T.matmul(psy[:, 0:n], CLv[:, 2 * jj + part, :], hS[:, part, qq, 0:n],
                                                   start=(idx == 0), stop=(idx == 7)), [CLkB, hSB], [pyB], sig=(idx == 7))
                            idx += 1
                    a0, a1, a2 = t[0], t[1], t[2]
                    DVE.op(lambda: V.scalar_tensor_tensor(a0, uTf[:, ct, 0:n], s5dd[:, l, ct:ct + 1], psy[:, 0:n], ALU.mult, ALU.add),
                           [uTfB, s5ddB, pyB], [tB[0]])
                    DVE.op(lambda: TT(a1, a0, a0, ALU.mult), [tB[0]], [tB[1]])
                    DVE.op(lambda: V.tensor_scalar(a1, a1, 0.044715, 1.0, ALU.mult, ALU.add), [tB[1]], [tB[1]])
                    DVE.op(lambda: TT(a1, a1, a0, ALU.mult), [tB[0], tB[1]], [tB[1]])
                    ACT.op(lambda: A.activation(a2, a1, AF.Sigmoid, scale=1.5957691216057308), [tB[1]], [tB[2]])
                    DVE.op(lambda: TT(ybT[:, ct, 0:n], a0, a2, ALU.mult), [tB[0], tB[2]], [ybTB])
            for b in range(4):
                blk, bB = getblk(l, 11 + b)
                for i in range(nt):
                    ps, pB = mm_tok(blk, bB, i, 4, ybT, ybTB)
                    if b < 2:
                        ACT.op(lambda: A.activation(sg[:, i, b * 512:(b + 1) * 512], ps[:], AF.Sigmoid), [pB], [sgB[i]])
                    else:
                        c0 = (b - 2) * 512
                        DVE.op(lambda: V.tensor_tensor(yb[:, i, c0:c0 + 512], ps[:], sg[:, i, c0:c0 + 512], ALU.mult),
                               [pB, sgB[i]], [ybB[i]])
            if tg + "b" not in taps:
                taps[tg + "b"] = 1
                tap(tg + "_ybT", ybT[:, :, 0:n], [ybTB])
                tap(tg + "_yb", yb[:, 0:nt, :], ybB)
                tap(tg + "_gst", gst[:, l], [gstB[l]])
            for i in range(nt):
                dA = sm[:, 2, :]
                DVE.op(lambda: V.tensor_tensor(dA, dts[:, i, :], arep[:, l, :], ALU.mult), [dtsB[i], arepB], [smB])
                DVE.op(lambda: V.tensor_tensor(lseg[:], strict.unsqueeze(1).broadcast_to([128, 16, 128]),
                                               dA.unsqueeze(2).broadcast_to([128, 16, 128]), ALU.mult), [cfB, smB], [lsegB])
                psc, pcB = getps()
                PE.op(lambda: T.matmul(psc[:, 0:16], tri, dA, start=True, stop=True), [cfB, smB], [pcB])
                PE.op(lambda: T.matmul(psc[:, 16:32], ones, dA, start=True, stop=True), [cfB, smB], [pcB])
                cs = sm[:, 3, :]; ecs = sm[:, 4, :]; wv = sm[:, 5, :]; etot = sm[:, 6, :]
                ACT.op(lambda: A.copy(cs, psc[:, 0:16]), [pcB], [smB])
                ACT.op(lambda: A.activation(etot, psc[:, 16:32], AF.Exp), [pcB], [smB])
                DVE.op(lambda: V.tensor_tensor(wv, psc[:, 16:32], cs, ALU.subtract), [pcB, smB], [smB])
                ACT.op(lambda: A.activation(wv, wv, AF.Exp), [smB], [smB])
                ACT.op(lambda: A.activation(ecs, cs, AF.Exp), [smB], [smB])
                DVE.op(lambda: V.tensor_tensor(wv, wv, dts[:, i, :], ALU.mult), [smB, dtsB[i]], [smB])
                for g in range(4):
                    ps, pB = getps()
                    for r in range(4):
                        hh = g * 4 + r
                        PE.op(lambda: T.matmul(ps[:, r * 128:(r + 1) * 128], lseg[:, hh, :], tri, start=True, stop=True),
                              [lsegB, cfB], [pB], sig=(r == 3))
                    ACT.op(lambda: A.activation(Lm[:, g * 4:(g + 1) * 4, :], ps[:].rearrange("p (r t) -> p r t", r=4), AF.Exp),
                           [pB], [LmB])
                ps, pB = getps()
                for g in range(4):
                    PE.op(lambda: T.matmul(ps[:, g * 128:(g + 1) * 128], xc[:, 8 + g, i * 128:(i + 1) * 128],
                                           xc[:, 12 + g, i * 128:(i + 1) * 128], start=True, stop=True), [xcB], [pB], sig=(g == 3))
                DVE.op(lambda: V.tensor_tensor(scm[:], ps[:].rearrange("p (g t) -> p g t", g=4),
                                               tri.unsqueeze(1).broadcast_to([128, 4, 128]), ALU.mult), [pB, cfB], [scmB])
                DVE.op(lambda: V.tensor_tensor(Mm[:].rearrange("p (g r) t -> p g r t", g=4),
                                               Lm[:].rearrange("p (g r) t -> p g r t", g=4),
                                               scm[:].unsqueeze(2).broadcast_to([128, 4, 4, 128]), ALU.mult), [LmB, scmB], [MmB])
                x3 = xtok[:, i, :].rearrange("p (h e) -> p h e", h=16)
                DVE.op(lambda: V.tensor_tensor(xdt[:], x3, dts[:, i, :].unsqueeze(2).broadcast_to([128, 16, 64]), ALU.mult),
                       [xtokB[i], dtsB[i]], [xdtB])
                DVE.op(lambda: V.tensor_tensor(xw[:], x3, wv.unsqueeze(2).broadcast_to([128, 16, 64]), ALU.mult),
                       [xtokB[i], smB], [xwB])
                pyd = [getps() for _ in range(2)]
                for hh in range(16):
                    ps, pB = pyd[hh // 8]
                    PE.op(lambda: T.matmul(ps[:, (hh % 8) * 64:(hh % 8 + 1) * 64], Mm[:, hh, :], xdt[:, hh, :], start=True, stop=True),
                          [MmB, xdtB], [pB], sig=(hh % 8 == 7))
                pyo = [getps() for _ in range(2)]
                for g in range(4):
                    ps, pB = pyo[g // 2]
                    PE.op(lambda: T.matmul(ps[:, (g % 2) * 256:(g % 2 + 1) * 256], xc[:, 12 + g, i * 128:(i + 1) * 128],
                                           Hb[:, l, g * 4:(g + 1) * 4, :].rearrange("p r e -> p (r e)"), start=True, stop=True),
                          [xcB, HbB[l]], [pB], sig=(g % 2 == 1))
                for half in range(2):
                    sl = slice(half * 512, (half + 1) * 512)
                    hsl = slice(half * 8, (half + 1) * 8)
                    DVE.op(lambda: V.tensor_tensor(yv[:, sl].rearrange("p (h e) -> p h e", h=8),
                                                   pyo[half][0][:].rearrange("p (h e) -> p h e", h=8),
                                                   ecs[:, hsl].unsqueeze(2).broadcast_to([128, 8, 64]), ALU.mult),
                           [pyo[half][1], smB], [yvB])
                    DVE.op(lambda: V.tensor_tensor(yv[:, sl], yv[:, sl], pyd[half][0][:], ALU.add), [yvB, pyd[half][1]], [yvB])
                pss = [getps() for _ in range(2)]
                for g in range(4):
                    ps, pB = pss[g // 2]
                    PE.op(lambda: T.matmul(ps[:, (g % 2) * 256:(g % 2 + 1) * 256], btok[:, i, g * 128:(g + 1) * 128],
                                           xw[:, g * 4:(g + 1) * 4, :].rearrange("p r e -> p (r e)"), start=True, stop=True),
                          [btokB[i], xwB], [pB], sig=(g % 2 == 1))
                DVE.op(lambda: V.tensor_tensor(Hs[:, l], Hs[:, l], etot.unsqueeze(2).broadcast_to([128, 16, 64]), ALU.mult),
                       [HsB[l], smB], [HsB[l]])
                for half in range(2):
                    hsl = slice(half * 8, (half + 1) * 8)
                    DVE.op(lambda: V.tensor_tensor(Hs[:, l, hsl, :], Hs[:, l, hsl, :],
                                                   pss[half][0][:].rearrange("p (h e) -> p h e", h=8), ALU.add),
                           [HsB[l], pss[half][1]], [HsB[l]])
                POOL.op(lambda: G.tensor_copy(Hb[:, l], Hs[:, l]), [HsB[l]], [HbB[l]])
                DVE.op(lambda: V.tensor_tensor(y2[:].rearrange("p (h e) -> p h e", h=16), x3,
                                               ssdp[:, l, 2, :].unsqueeze(2).broadcast_to([128, 16, 64]), ALU.mult),
                       [xtokB[i], ssdpB], [y2B])
                DVE.op(lambda: V.tensor_tensor(yv[:], yv[:], y2[:], ALU.add), [yvB, y2B], [yvB])
                DVE.op(lambda: V.tensor_tensor(yv[:], yv[:], zs[:, i, :], ALU.mult), [yvB, zsB[i]], [yvB])
                gss = sm[:, 7, 0:4]; grs = sm[:, 7, 4:8]
                POOL.op(lambda: G.memset(gss, 0.0), [], [smB])
                for g in range(4):
                    ACT.op(lambda: A.activation(y2[:, g * 256:(g + 1) * 256], yv[:, g * 256:(g + 1) * 256], AF.Square,
                                                accum_out=gss[:, g:g + 1]), [yvB, smB], [y2B, smB])
                DVE.op(lambda: V.tensor_scalar(grs, gss, 1.0 / 256, EPS, ALU.mult, ALU.add), [smB], [smB])
                ACT.op(lambda: A.activation(grs, grs, AF.Sqrt), [smB], [smB])
                DVE.op(lambda: V.reciprocal(grs, grs), [smB], [smB])
                DVE.op(lambda: V.tensor_tensor(yv[:].rearrange("p (g e) -> p g e", g=4), yv[:].rearrange("p (g e) -> p g e", g=4),
                                               grs.unsqueeze(2).broadcast_to([128, 4, 256]), ALU.mult), [yvB, smB], [yvB])
                DVE.op(lambda: V.tensor_tensor(yv[:], yv[:], snw[:], ALU.mult), [yvB, snwB], [yvB])
                DVE.op(lambda: V.tensor_tensor(yv[:], yv[:], gts[:, i, 0:D], ALU.mult), [yvB, gtsB[i]], [yvB])
                DVE.op(lambda: V.tensor_tensor(y2[:], yb[:, i, :], gts[:, i, D:2 * D], ALU.mult), [ybB[i], gtsB[i]], [y2B])
                mixb, mixbB = xnb[xnctr[0] % 2], xnbB[xnctr[0] % 2]
                xnctr[0] += 1
                DVE.op(lambda: V.tensor_tensor(mixb[:], yv[:], y2[:], ALU.add), [yvB, y2B], [mixbB])
                transposes_to([mixb[:, k * 128:(k + 1) * 128] for k in range(8)], mixbB,
                              xT[:, :, i * 128:(i + 1) * 128], xTB, 8)
            if tg + "c" not in taps:
                taps[tg + "c"] = 1
                tap(tg + "_mixT", xT[:, :, 0:n], [xTB])
                tap(tg + "_Hs", Hs[:, l], [HsB[l]])
            for cb in range(2):
                blk, bB = getblk(l, 15 + cb)
                for i in range(nt):
                    ps, pB = mm_tok(blk, bB, i, 8, xT, xTB)
                    DVE.op(lambda: V.tensor_tensor(h[:, i, cb * 512:(cb + 1) * 512], h[:, i, cb * 512:(cb + 1) * 512], ps[:], ALU.add),
                           [hB[i], pB], [hB[i]])
            rmsnorm_T(nt, 2 * l + 1)
            hid = xc
            for kg in range(4):
                for c in range(2):
                    blk, bB = getblk(l, 17 + kg * 4 + c)
                    for f in range(4):
                        ps, pB = mm_feat(blk, bB, f, n, xT, xTB)
                        rr = cctr[0] % 2
                        cctr[0] += 1
                        ACT.op(lambda: A.activation(rl[rr][:, 0:n], ps[:, 0:n], AF.Relu), [pB], [rlB[rr]])
                        POOL.op(lambda: G.tensor_tensor(hid[:, (kg % 2) * 8 + c * 4 + f, 0:n], rl[rr][:, 0:n], rl[rr][:, 0:n], ALU.mult),
                                [rlB[rr]], [xcB])
                for c in range(2):
                    blk, bB = getblk(l, 17 + kg * 4 + 2 + c)
                    for i in range(nt):
                        ps, pB = getps()
                        bv = blk[:].rearrange("p (kc c) -> p kc c", kc=8)
                        for kc in range(8):
                            PE.op(lambda: T.matmul(ps[:], hid[:, (kg % 2) * 8 + kc, i * 128:(i + 1) * 128], bv[:, kc, :],
                                                   start=(kc == 0), stop=(kc == 7)), [xcB, bB], [pB], sig=(kc == 7))
                        DVE.op(lambda: V.tensor_tensor(h[:, i, c * 512:(c + 1) * 512], h[:, i, c * 512:(c + 1) * 512], ps[:], ALU.add),
                               [hB[i], pB], [hB[i]])
            tg = ("m" if is_meta else "s") + str(l)
            if tg not in taps:
                taps[tg] = 1
                tap(tg + "_h", h[:, 0:nt, :], hB)

        taps = {}
        def run_macro(kind, seq, m):
            nt = 1 if kind == "meta" else NT
            if kind == "meta":
                QP.dma("xin", h[:, 0, :], meta_d.ap(), writes=[hB[0]])
            else:
                QP.dma("xin", h[:], x_d.ap()[seq, m * N:(m + 1) * N, :].rearrange("(i p) d -> p i d", p=128), writes=hB)
            for l in range(depth):
                schedule_layer(l)
            for l in range(depth):
                layer(l, nt, kind == "meta")
                if kind == "meta":
                    DVE.op(lambda: V.tensor_scalar(h[:, 0, :], h[:, 0, :], padmask, None, ALU.mult), [hB[0], cfB], [hB[0]])
            if kind != "meta":
                wr, wrB = load_nw(2 * depth)
                for i in range(NT):
                    norm_stats(i)
                    DVE.op(lambda: V.scalar_tensor_tensor(obuf[:, i, :], h[:, i, :], ss[:, 4 + i:5 + i], wr[:], ALU.mult, ALU.mult),
                           [hB[i], ssB, wrB], [gtsB[i]])
                QP.dma("oout", out_d.ap()[seq, m * N:(m + 1) * N, :].rearrange("(i p) d -> p i d", p=128), obuf[:],
                       reads=gtsB)

        run_macro("meta", 0, 0)
        POOL.op(lambda: G.tensor_copy(halo0[:], halo[:]), haloB, [halo0B])
        POOL.op(lambda: G.tensor_copy(Hs0[:], Hs[:]), HsB, [Hs0B])
        POOL.op(lambda: G.tensor_copy(gst0[:], gst[:]), gstB, [gst0B])
        for seq in range(NSEQ):
            if seq > 0:
                POOL.op(lambda: G.tensor_copy(halo[:], halo0[:]), [halo0B], haloB)
                POOL.op(lambda: G.tensor_copy(Hs[:], Hs0[:]), [Hs0B], HsB)
                POOL.op(lambda: G.tensor_copy(Hb[:], Hs0[:]), [Hs0B], HbB)
                POOL.op(lambda: G.tensor_copy(gst[:], gst0[:]), [gst0B], gstB)
            for m in range(NMT):
                run_macro("seq", seq, m)
        for s in QP.sems.values():
            POOL.e.wait_ge(s[0], s[1])
        for E in (PE, ACT, DVE, POOL):
            for E2 in (PE, ACT, DVE, POOL):
                if E2.cnt > 0 and E2.last_sig:
                    E.e.wait_ge(E2.sem, E2.cnt)
    return nc


def host_prep(inputs, depth=DEPTH):
    f = lambda a: np.ascontiguousarray(np.asarray(a, dtype=np.float32))
    p = {}
    meta = f(inputs["meta_tokens"])
    mp = np.zeros((128, D), np.float32)
    mp[128 - NMETA:] = meta
    p["meta_pad"] = mp
    for k in ("w_in", "w_glu", "w_out", "w_ff_in", "w_ff_out"):
        p[k] = f(inputs[k])
    rows = []
    for l in range(depth):
        rows += [inputs["norm_mix_w"][l], inputs["norm_mlp_w"][l]]
    rows.append(inputs["final_norm_w"])
    p["nw"] = f(np.stack([np.asarray(r) for r in rows]))
    p["snw"] = f(inputs["ssd_norm_w"])
    cw = np.concatenate([np.asarray(inputs["conv_w"]), np.asarray(inputs["conv_b"])[:, None, :]], axis=1)
    p["convp"] = f(cw.reshape(depth, 5, 16, 128).transpose(0, 3, 2, 1))
    p["ssdp"] = f(np.stack([np.asarray(inputs["dt_bias"]), np.asarray(inputs["ssd_a_log"]), np.asarray(inputs["ssd_d"])], axis=1))
    def st(a):
        return np.asarray(a).reshape(depth, 16, 2, 64).transpose(0, 2, 3, 1).reshape(depth, 128, 16)
    ls = np.broadcast_to(np.asarray(inputs["s5_log_step"])[:, :, None], (depth, 32, 64))
    p["s5s"] = f(np.stack([st(inputs["s5_a_re"]), st(inputs["s5_a_im"]), st(ls)], axis=1))
    def stb(a):
        return np.asarray(a).reshape(depth, 16, 2, 64, 16).transpose(0, 2, 3, 1, 4).reshape(depth, 128, 16, 16)
    p["s5b"] = f(np.stack([stb(inputs["s5_b_re"]), stb(inputs["s5_b_im"])], axis=1))
    cre = np.asarray(inputs["s5_c_re"]).transpose(0, 1, 3, 2)
    cim = np.asarray(inputs["s5_c_im"]).transpose(0, 1, 3, 2)
    p["s5c"] = f(np.stack([stb(cre), stb(cim)], axis=1))
    p["s5d"] = f(np.asarray(inputs["s5_d"]).reshape(depth, 4, 128).transpose(0, 2, 1))
    k = np.arange(128)
    cf = np.zeros((128, 4 * 128 + 512 + 1), np.float32)
    cf[:, 0:128] = np.eye(128)
    cf[:, 128:256] = (k[:, None] <= k[None, :])
    cf[:, 256:384] = (k[:, None] > k[None, :])
    cf[:, 384:512] = 1.0
    cf[:, 512:1024] = np.arange(1, 513)[None, :]
    cf[:, 1024] = (k >= 128 - NMETA)
    p["cf"] = cf
    p["cb"] = np.eye(128).astype(ml_dtypes.bfloat16)
    return p


_CACHE = {}


def kernel(**inputs):
    x = np.asarray(inputs["x"], dtype=np.float32)
    B, S, _ = x.shape
    ncores = 8
    nseq = B // ncores
    NT = 2
    nmt = S // (NT * 128)
    key = (nseq, nmt)
    if key not in _CACHE:
        _CACHE[key] = build_program(nseq, nmt, NT)
    nc = _CACHE[key]
    p = host_prep(inputs)
    in_maps = []
    for c in range(ncores):
        m = dict(p)
        m["x"] = np.ascontiguousarray(x[c * nseq:(c + 1) * nseq])
        in_maps.append(m)
    res = run_bass_kernel_spmd(nc, in_maps, core_ids=list(range(ncores)))
    out = np.concatenate([np.asarray(r["out"]) for r in res.results], axis=0)
    return out.astype(np.float32)
```

```python
from contextlib import ExitStack
import numpy as np
import ml_dtypes
import concourse.bass as bass
import concourse.mybir as mybir
from concourse.bass_utils import run_bass_kernel_spmd

F32 = mybir.dt.float32
BF = mybir.dt.bfloat16
ALU = mybir.AluOpType
AF = mybir.ActivationFunctionType

D = 1024
NMETA = 16
DEPTH = 2
EPS = 1e-6
NBLK = 35
TWO_PI = 6.283185307179586
PI = 3.141592653589793


class Tok:
    __slots__ = ("sem", "val", "eng")

    def __init__(self, sem, val, eng):
        self.sem, self.val, self.eng = sem, val, eng


class Buf:
    def __init__(self, name):
        self.name = name
        self.w = None
        self.r = []


class Eng:
    DRY = False

    def __init__(self, nc, es, e, name):
        self.nc, self.es, self.e, self.name = nc, es, e, name
        self.k = 0
        self._new_sem()
        self.seen = {}
        self.last_sig = True

    def _new_sem(self):
        self.sem = self.es.enter_context(self.nc.semaphore(f"{self.name}_s{self.k}"))
        self.k += 1
        self.cnt = 0

    def wait(self, tok):
        if tok is None:
            return
        key = id(tok.sem)
        if self.seen.get(key, 0) >= tok.val:
            return
        if tok.eng is self and self.name == "pe":
            return
        if tok.eng is not None:
            assert tok.sem is not tok.eng.sem or tok.val <= tok.eng.cnt, (self.name, tok.eng.name)
        self.e.wait_ge(tok.sem, tok.val)
        self.seen[key] = tok.val

    def op(self, fn, reads=(), writes=(), sig=True):
        if Eng.DRY:
            return None
        for b in reads:
            self.wait(b.w)
        for b in writes:
            self.wait(b.w)
            for t in b.r:
                self.wait(t)
        if sig and self.last_sig and self.cnt >= 30000:
            self._new_sem()
        inst = fn()
        if sig:
            self.cnt += 1
            inst.then_inc(self.sem, 1)
            tok = Tok(self.sem, self.cnt, self)
            self.last_sig = True
        else:
            tok = Tok(self.sem, self.cnt + 1, self)
            self.last_sig = False
        for b in reads:
            b.r = [t for t in b.r if t.sem is not tok.sem] + [tok]
        for b in writes:
            b.w = tok
            b.r = []
        return tok


class DmaQ:
    def __init__(self, nc, es, eng):
        self.nc, self.es, self.eng = nc, es, eng
        self.sems = {}

    def dma(self, slot, out, in_, reads=(), writes=()):
        if Eng.DRY:
            return None
        E = self.eng
        for b in reads:
            E.wait(b.w)
        for b in writes:
            E.wait(b.w)
            for t in b.r:
                E.wait(t)
        if slot not in self.sems:
            self.sems[slot] = [self.es.enter_context(self.nc.semaphore(f"dq_{slot}")), 0]
        s = self.sems[slot]
        s[1] += 16
        E.e.dma_start(out=out, in_=in_).then_inc(s[0], 16)
        tok = Tok(s[0], s[1], None)
        for b in reads:
            b.r = [t for t in b.r if t.sem is not tok.sem] + [tok]
        for b in writes:
            b.w = tok
            b.r = []
        return tok


def build_program(NSEQ, NMT, NT=4, depth=DEPTH, debug_taps=False):
    N = NT * 128
    S = NMT * N
    nc = bass.Bass("TRN2", target_bir_lowering=False)
    dt_in = lambda name, shape, dt=F32: nc.dram_tensor(name, shape, dt, kind="ExternalInput")
    x_d = dt_in("x", [NSEQ, S, D])
    meta_d = dt_in("meta_pad", [128, D])
    w_in_d = dt_in("w_in", [depth, D, 5648])
    w_glu_d = dt_in("w_glu", [depth, 512, 2048])
    w_out_d = dt_in("w_out", [depth, D, D])
    w_ffi_d = dt_in("w_ff_in", [depth, D, 4096])
    w_ffo_d = dt_in("w_ff_out", [depth, 4096, D])
    nw_d = dt_in("nw", [2 * depth + 1, D])
    snw_d = dt_in("snw", [depth, D])
    convp_d = dt_in("convp", [depth, 128, 16, 5])
    ssdp_d = dt_in("ssdp", [depth, 3, 16])
    s5s_d = dt_in("s5s", [depth, 3, 128, 16])
    s5b_d = dt_in("s5b", [depth, 2, 128, 16, 16])
    s5c_d = dt_in("s5c", [depth, 2, 128, 16, 16])
    s5d_d = dt_in("s5d", [depth, 128, 4])
    cf_d = dt_in("cf", [128, 4 * 128 + 512 + 1])
    cb_d = dt_in("cb", [128, 128], BF)
    out_d = nc.dram_tensor("out", [NSEQ, S, D], F32, kind="ExternalOutput")
    wblk_d = nc.dram_tensor("wblk", [depth, NBLK, 128, 4096], BF)
    tabs_d = nc.dram_tensor("tabs", [depth, 16, 128, 2, 512], F32)
    hs0_d = nc.dram_tensor("hs0", [depth, 128, 1024], F32)

    es = ExitStack()
    with es:
        sb = lambda name, shape, dt=F32: es.enter_context(nc.sbuf_tensor("s_" + name, shape, dt))
        PE = Eng(nc, es, nc.tensor, "pe")
        ACT = Eng(nc, es, nc.scalar, "act")
        DVE = Eng(nc, es, nc.vector, "dve")
        POOL = Eng(nc, es, nc.gpsimd, "pool")
        SP = Eng(nc, es, nc.sync, "sp")
        QW = DmaQ(nc, es, SP)
        QP = DmaQ(nc, es, POOL)

        cf = sb("cf", [128, 4 * 128 + 512 + 1]); cfB = Buf("cf")
        identb = sb("identb", [128, 128], BF); identbB = Buf("identb")
        QP.dma("c0", cf[:], cf_d.ap(), writes=[cfB])
        QP.dma("c1", identb[:], cb_d.ap(), writes=[identbB])
        identf = cf[:, 0:128]
        tri = cf[:, 128:256]
        strict = cf[:, 256:384]
        ones = cf[:, 384:512]
        tpos = cf[:, 512:1024]
        padmask = cf[:, 1024:1025]

        convp = sb("convp", [128, depth, 16, 5]); convpB = Buf("convp")
        ssdp = sb("ssdp", [128, depth, 3, 16]); ssdpB = Buf("ssdp")
        s5dd = sb("s5dd", [128, depth, 4]); s5ddB = Buf("s5dd")
        dtw = sb("dtw", [128, depth, 8, 16], BF); dtwB = Buf("dtw")
        for l in range(depth):
            QP.dma("c2", convp[:, l], convp_d.ap()[l], writes=[convpB])
            QP.dma("c3", ssdp[:, l].rearrange("p a b -> p (a b)"),
                   ssdp_d.ap()[l].rearrange("a b -> (a b)").partition_broadcast(128), writes=[ssdpB])
            QP.dma("c4", s5dd[:, l], s5d_d.ap()[l], writes=[s5ddB])
            QP.dma("c5", dtw[:, l], w_in_d.ap()[l][:, 3072:3088].rearrange("(kc p) c -> p kc c", p=128),
                   writes=[dtwB])
        arep = sb("arep", [128, depth, 16]); arepB = Buf("arep")
        ACT.op(lambda: nc.scalar.activation(arep[:], ssdp[:, :, 1, :], AF.Exp), [ssdpB], [arepB])
        DVE.op(lambda: nc.vector.tensor_scalar(arep[:], arep[:], -1.0, None, ALU.mult), [arepB], [arepB])

        wscB = Buf("wscratch")
        def blkview(l, b, kc):
            return wblk_d.ap()[l, b][:, 0:kc * 512].rearrange("p (kc c) -> p kc c", kc=kc)
        def wsrc(wd, l, r0, kc, c0):
            return wd.ap()[l][r0:r0 + kc * 128, c0:c0 + 512].rearrange("(kc p) c -> p kc c", p=128)
        for l in range(depth):
            cols = [0, 512] + [1024 + 512 * i for i in range(4)] + [3088] + [3600 + 512 * i for i in range(4)]
            for b, c0 in enumerate(cols):
                QP.dma("pre", blkview(l, b, 8), wsrc(w_in_d, l, 0, 8, c0))
            for b, cbi in enumerate([2, 3, 0, 1]):
                QP.dma("pre", blkview(l, 11 + b, 4), wsrc(w_glu_d, l, 0, 4, cbi * 512))
            for b in range(2):
                QP.dma("pre", blkview(l, 15 + b, 8), wsrc(w_out_d, l, 0, 8, b * 512))
            for kg in range(4):
                for c in range(2):
                    QP.dma("pre", blkview(l, 17 + kg * 4 + c, 8), wsrc(w_ffi_d, l, 0, 8, kg * 1024 + c * 512))
                for c in range(2):
                    QP.dma("pre", blkview(l, 17 + kg * 4 + 2 + c, 8), wsrc(w_ffo_d, l, kg * 1024, 8, c * 512))

        psum = [es.enter_context(nc.psum_tensor(f"ps{i}", [128, 512], F32)) for i in range(8)]
        psB = [Buf(f"ps{i}") for i in range(8)]
        pctr = [0]
        def getps():
            i = pctr[0] % 8
            pctr[0] += 1
            return psum[i], psB[i]

        s5s = sb("s5s", [128, depth, 3, 16]); s5sB = Buf("s5s")
        rtab = sb("rtab", [128, depth, 16]); rtabB = Buf("rtab")
        gst = sb("gst", [128, depth, 16, 2]); gstB = [Buf(f"gst{l}") for l in range(depth)]
        gst0 = sb("gst0", [128, depth, 16, 2]); gst0B = Buf("gst0")
        with ExitStack() as es2:
            sb2 = lambda name, shape, dt=F32: es2.enter_context(nc.sbuf_tensor("s_" + name, shape, dt))
            th = sb2("th", [128, 16]); thB = Buf("th")
            stp = sb2("stp", [128, 16]); stpB = Buf("stp")
            ang = sb2("ang", [128, 2, 512]); angB = Buf("ang")
            ang2 = sb2("ang2", [128, 2, 512]); ang2B = Buf("ang2")
            tabt = [sb2(f"tabt{i}", [128, 2, 512]) for i in range(2)]; tabtB = [Buf(f"tabt{i}") for i in range(2)]
            ab = sb2("ab", [128, 2, 16]); abB = Buf("ab")
            ff = sb2("ff", [128, 6, 16]); ffB = Buf("ff")
            bc_in = sb2("bc_in", [128, 4, 16, 16]); bcinB = Buf("bc_in")
            bbar = sb2("bbar", [128, 2, 16, 16]); bbarB = Buf("bbar")
            tmp = sb2("tmp", [128, 2, 16, 16]); tmpB = Buf("tmp5")
            bd = [sb2(f"bd{i}", [128, 128]) for i in range(2)]; bdB = [Buf(f"bd{i}") for i in range(2)]
            BLs = sb2("BLs", [128, 32, 128], BF); BLsB = Buf("BLs")
            CLs = sb2("CLs", [128, 32, 128], BF); CLsB = Buf("CLs")
            for l in range(depth):
                for a in range(3):
                    QP.dma("c6", s5s[:, l, a], s5s_d.ap()[l, a], writes=[s5sB])
                for a in range(2):
                    QP.dma("c7", bc_in[:, a], s5b_d.ap()[l, a], writes=[bcinB])
                    QP.dma("c7", bc_in[:, 2 + a], s5c_d.ap()[l, a], writes=[bcinB])
                ACT.op(lambda: nc.scalar.activation(stp[:], s5s[:, l, 2, :], AF.Exp), [s5sB], [stpB])
                DVE.op(lambda: nc.vector.tensor_tensor(th[:], s5s[:, l, 1, :], stp[:], ALU.mult), [s5sB, stpB], [thB])
                DVE.op(lambda: nc.vector.tensor_tensor(stp[:], s5s[:, l, 0, :], stp[:], ALU.mult), [s5sB, stpB], [stpB])
                ACT.op(lambda: nc.scalar.activation(rtab[:, l, :], stp[:], AF.Exp), [stpB], [rtabB])
                for j in range(16):
                    tt, ttB = tabt[j % 2], tabtB[j % 2]
                    DVE.op(lambda: nc.vector.tensor_scalar(ang[:, 0, :], tpos, th[:, j:j + 1], 0.5 * PI, ALU.mult, ALU.add),
                           [cfB, thB], [angB])
                    DVE.op(lambda: nc.vector.tensor_scalar(ang[:, 1, :], tpos, th[:, j:j + 1], None, ALU.mult),
                           [cfB, thB], [angB])
                    MAGIC = 12582912.0
                    DVE.op(lambda: nc.vector.tensor_scalar(ang2[:], ang[:], 1.0 / TWO_PI, MAGIC, ALU.mult, ALU.add), [angB], [ang2B])
                    DVE.op(lambda: nc.vector.tensor_scalar(ang2[:], ang2[:], -MAGIC, -TWO_PI, ALU.add, ALU.mult), [ang2B], [ang2B])
                    DVE.op(lambda: nc.vector.tensor_tensor(ang[:], ang[:], ang2[:], ALU.add), [angB, ang2B], [angB])
                    DVE.op(lambda: nc.vector.tensor_scalar(ang[:], ang[:], -PI, PI, ALU.max, ALU.min), [angB], [angB])
                    ACT.op(lambda: nc.scalar.activation(tt[:], ang[:], AF.Sin), [angB], [ttB])
                    QP.dma("tabw", tabs_d.ap()[l, j], tt[:], reads=[ttB])
                    POOL.op(lambda: nc.gpsimd.tensor_copy(ab[:, :, j:j + 1], tt[:, :, 0:1]), [ttB], [abB])
                DVE.op(lambda: nc.vector.tensor_tensor(ab[:], ab[:], rtab[:, l, :].unsqueeze(1).broadcast_to([128, 2, 16]), ALU.mult),
                       [abB, rtabB], [abB])
                are, aim = s5s[:, l, 0, :], s5s[:, l, 1, :]
                V = nc.vector
                DVE.op(lambda: V.tensor_scalar(ff[:, 0], ab[:, 0], -1.0, None, ALU.add), [abB], [ffB])
                DVE.op(lambda: V.tensor_tensor(ff[:, 1], are, are, ALU.mult), [s5sB], [ffB])
                DVE.op(lambda: V.tensor_tensor(ff[:, 4], aim, aim, ALU.mult), [s5sB], [ffB])
                DVE.op(lambda: V.tensor_tensor(ff[:, 1], ff[:, 1], ff[:, 4], ALU.add), [ffB], [ffB])
                DVE.op(lambda: V.reciprocal(ff[:, 1], ff[:, 1]), [ffB], [ffB])
                DVE.op(lambda: V.tensor_tensor(ff[:, 2], ff[:, 0], are, ALU.mult), [ffB, s5sB], [ffB])
                DVE.op(lambda: V.tensor_tensor(ff[:, 4], ab[:, 1], aim, ALU.mult), [abB, s5sB], [ffB])
                DVE.op(lambda: V.tensor_tensor(ff[:, 2], ff[:, 2], ff[:, 4], ALU.add), [ffB], [ffB])
                DVE.op(lambda: V.tensor_tensor(ff[:, 2], ff[:, 2], ff[:, 1], ALU.mult), [ffB], [ffB])
                DVE.op(lambda: V.tensor_tensor(ff[:, 3], ab[:, 1], are, ALU.mult), [abB, s5sB], [ffB])
                DVE.op(lambda: V.tensor_tensor(ff[:, 4], ff[:, 0], aim, ALU.mult), [ffB, s5sB], [ffB])
                DVE.op(lambda: V.tensor_tensor(ff[:, 3], ff[:, 3], ff[:, 4], ALU.subtract), [ffB], [ffB])
                DVE.op(lambda: V.tensor_tensor(ff[:, 3], ff[:, 3], ff[:, 1], ALU.mult), [ffB], [ffB])
                if debug_taps and l == 0:
                    for nm, ap_, bb in (("th", th[:], thB), ("ff", ff[:], ffB), ("ab", ab[:], abB), ("s5s", s5s[:, 0], s5sB), ("tab15", tabt[1][:], tabtB[1])):
                        d_ = nc.dram_tensor("tap_pre_" + nm, list(ap_.shape), ap_.dtype, kind="ExternalOutput")
                        QP.dma("tap", d_.ap(), ap_, reads=[bb])
                fre = ff[:, 2].unsqueeze(2).broadcast_to([128, 16, 16])
                fim = ff[:, 3].unsqueeze(2).broadcast_to([128, 16, 16])
                DVE.op(lambda: V.tensor_tensor(bbar[:, 0], bc_in[:, 0], fre, ALU.mult), [bcinB, ffB], [bbarB])
                DVE.op(lambda: V.tensor_tensor(tmp[:, 0], bc_in[:, 1], fim, ALU.mult), [bcinB, ffB], [tmpB])
                DVE.op(lambda: V.tensor_tensor(bbar[:, 0], bbar[:, 0], tmp[:, 0], ALU.subtract), [bbarB, tmpB], [bbarB])
                DVE.op(lambda: V.tensor_tensor(bbar[:, 1], bc_in[:, 1], fre, ALU.mult), [bcinB, ffB], [bbarB])
                DVE.op(lambda: V.tensor_tensor(tmp[:, 1], bc_in[:, 0], fim, ALU.mult), [bcinB, ffB], [tmpB])
                DVE.op(lambda: V.tensor_tensor(bbar[:, 1], bbar[:, 1], tmp[:, 1], ALU.add), [bbarB, tmpB], [bbarB])
                DVE.op(lambda: V.tensor_scalar(bc_in[:, 3], bc_in[:, 3], -1.0, None, ALU.mult), [bcinB], [bcinB])
                POOL.op(lambda: nc.gpsimd.memset(CLs[:], 0.0), [], [CLsB])
                for j in range(16):
                    q = j % 4
                    for part in range(2):
                        for two in range(2):
                            c0 = 32 * q + 16 * two
                            POOL.op(lambda: nc.gpsimd.tensor_copy(
                                CLs[64 * two:64 * two + 64, 2 * j + part, c0:c0 + 16],
                                bc_in[64 * two:64 * two + 64, 2 + part, j, :]), [bcinB], [CLsB])
                for j in range(16):
                    q = j % 4
                    for part in range(2):
                        k = (2 * j + part) % 2
                        POOL.op(lambda: nc.gpsimd.memset(bd[k][:], 0.0), [], [bdB[k]])
                        for two in range(2):
                            c0 = 32 * q + 16 * two
                            POOL.op(lambda: nc.gpsimd.tensor_copy(
                                bd[k][64 * two:64 * two + 64, c0:c0 + 16],
                                bbar[64 * two:64 * two + 64, part, j, :]), [bbarB], [bdB[k]])
                        ps, pB = getps()
                        PE.op(lambda: nc.tensor.transpose(ps[:, 0:128], bd[k][:], identf), [bdB[k], cfB], [pB])
                        ACT.op(lambda: nc.scalar.copy(BLs[:, 2 * j + part, :], ps[:, 0:128]), [pB], [BLsB])
                if debug_taps and l == 0:
                    for nm, ap_, bb in (("BLs", BLs[:], BLsB), ("CLs", CLs[:], CLsB), ("bbar", bbar[:], bbarB)):
                        d_ = nc.dram_tensor("tap_pre_" + nm, list(ap_.shape), ap_.dtype, kind="ExternalOutput")
                        QP.dma("tap", d_.ap(), ap_, reads=[bb])
                for hf in range(4):
                    QP.dma("tabw", wblk_d.ap()[l, 33][:, hf * 1024:(hf + 1) * 1024].rearrange("p (a b) -> p a b", a=8),
                           BLs[:, hf * 8:(hf + 1) * 8, :], reads=[BLsB])
                    QP.dma("tabw", wblk_d.ap()[l, 34][:, hf * 1024:(hf + 1) * 1024].rearrange("p (a b) -> p a b", a=8),
                           CLs[:, hf * 8:(hf + 1) * 8, :], reads=[CLsB])
            for s in QP.sems.values():
                SP.e.wait_ge(s[0], s[1])
                POOL.e.wait_ge(s[0], s[1])

        hbuf = [sb(f"h{k}", [128, NT, D]) for k in range(2)]
        hbufB = [[Buf(f"h{k}_{i}") for i in range(NT)] for k in range(2)]
        xT = sb("xT", [128, 8, N], BF); xTB = Buf("xT")
        hnT = sb("hnT", [128, 8, N], BF); hnTB = Buf("hnT")
        zs = sb("zs", [128, NT, D], BF); zsB = [Buf(f"zs{i}") for i in range(NT)]
        gts = sb("gts", [128, NT, 2048], BF); gtsB = [Buf(f"gts{i}") for i in range(NT)]
        xc = sb("xc", [128, 16, N], BF); xcB = Buf("xc")
        hid = sb("hid", [128, 16, N], BF); hidB = Buf("hid")
        obuf = hid[:].rearrange("p a b -> p (a b)").bitcast(F32).rearrange("p (i d) -> p i d", d=D)
        xtok = sb("xtok", [128, NT, D], BF); xtokB = [Buf(f"xtok{i}") for i in range(NT)]
        btok = sb("btok", [128, NT, 512], BF); btokB = [Buf(f"btok{i}") for i in range(NT)]
        uTf = sb("uTf", [128, 4, N]); uTfB = Buf("uTf")
        uTb = sb("uTb", [128, 4, N], BF); uTbB = Buf("uTb")
        ybT = sb("ybT", [128, 4, N], BF); ybTB = Buf("ybT")
        yb = sb("yb", [128, NT, D], BF); ybB = [Buf(f"yb{i}") for i in range(NT)]
        dts = sb("dts", [128, NT, 16]); dtsB = [Buf(f"dts{i}") for i in range(NT)]
        NW = 5
        wring = [sb(f"wr{i}", [128, 4096], BF) for i in range(NW)]; wringB = [Buf(f"wr{i}") for i in range(NW)]
        tring = [sb(f"tr{i}", [128, 2, N]) for i in range(2)]; tringB = [Buf(f"tr{i}") for i in range(2)]
        snw = sb("snwr", [128, D]); snwB = Buf("snw")
        halo = sb("halo", [128, depth, 16, 3]); haloB = [Buf(f"halo{l}") for l in range(depth)]
        halo0 = sb("halo0", [128, depth, 16, 3]); halo0B = [Buf(f"halo0{l}") for l in range(depth)]
        gst0B = [Buf(f"gst0{l}") for l in range(depth)]
        Hs = sb("Hs", [128, depth, 16, 64]); HsB = [Buf(f"Hs{l}") for l in range(depth)]
        Hb = sb("Hb", [128, depth, 16, 64], BF); HbB = [Buf(f"Hb{l}") for l in range(depth)]
        xr = [sb(f"xr{i}", [128, N + 3]) for i in range(2)]; xrB = [Buf(f"xr{i}") for i in range(2)]
        acc = [sb(f"acc{i}", [128, N]) for i in range(2)]; accB = [Buf(f"acc{i}") for i in range(2)]
        rl = [sb(f"rl{i}", [128, N]) for i in range(2)]; rlB = [Buf(f"rl{i}") for i in range(2)]
        sm = sb("sm", [128, 8, 16]); smB = Buf("sm")
        lseg = sb("lseg", [128, 16, 128]); lsegB = Buf("lseg")
        Lm = sb("Lm", [128, 16, 128]); LmB = Buf("Lm")
        sg = lseg[:].rearrange("p a b -> p (a b)").bitcast(BF).rearrange("p (i c) -> p i c", c=D)
        sgB = [lsegB] * NT
        Mm = sb("Mm", [128, 16, 128], BF); MmB = Buf("Mm")
        scm = sb("scm", [128, 4, 128]); scmB = Buf("scm")
        xdt = sb("xdt", [128, 16, 64], BF); xdtB = Buf("xdt")
        xw = sb("xw", [128, 16, 64], BF); xwB = Buf("xw")
        yv = sb("yv", [128, D]); yvB = Buf("yv")
        y2 = sb("y2", [128, D]); y2B = Buf("y2")
        s5t = [sb(f"s5t{i}", [128, N]) for i in range(8)]; s5tB = [Buf(f"s5t{i}") for i in range(8)]
        hS = sb("hS", [128, 2, 4, N], BF); hSB = Buf("hS")

        class Th:
            pass
        TM, TF = Th(), Th()
        for nm, th, xt_, xtB_ in (("M", TM, xT, xTB), ("F", TF, hnT, hnTB)):
            th.name = nm
            th.xT, th.xTB = xt_, xtB_
            th.ss = sb("ss" + nm, [128, 8]); th.ssB = Buf("ss" + nm)
            th.xnb = [sb(f"xnb{nm}{i}", [128, D], BF) for i in range(2)]
            th.xnbB = [Buf(f"xnb{nm}{i}") for i in range(2)]
            th.nw = sb("nwr" + nm, [128, D]); th.nwB = Buf("nwr" + nm)
            th.ctr = [0]
            th.held = []

        TM.junk = y2[:].bitcast(BF)[:, 0:D]; TM.junkB = y2B
        TF.junk = hid[:].rearrange("p a b -> p (a b)")[:, 0:D]; TF.junkB = hidB
        import os as _os
        if _os.environ.get('KDEBUG'): print('SBUF remaining after allocs', nc.sbuf_bytes_remaining)
        V = nc.vector
        A = nc.scalar
        G = nc.gpsimd
        T = nc.tensor

        POOL.op(lambda: G.memset(halo[:], 0.0), [], haloB)
        POOL.op(lambda: G.memset(Hs[:], 0.0), [], HsB)
        POOL.op(lambda: G.memset(Hb[:], 0.0), [], HbB)
        DVE.op(lambda: V.memset(gst[:].rearrange("p a b c -> p (a b c)"), 0.0), [], gstB)

        wstate = {"next_load": 0, "next_use": 0, "sched": [], "free": [], "slot_of": {}}

        def pump():
            while wstate["free"] and wstate["next_load"] < len(wstate["sched"]):
                k = wstate["next_load"]
                sl = wstate["free"].pop(0)
                l, b = wstate["sched"][k]
                QW.dma(f"w{sl}", wring[sl][:], wblk_d.ap()[l, b], writes=[wringB[sl]])
                wstate["slot_of"][k] = sl
                wstate["next_load"] += 1

        def relall(th):
            if not Eng.DRY:
                for k in th.held:
                    wstate["free"].append(wstate["slot_of"][k])
                pump()
            th.held = []

        def getblk(l, b, th, keep=False):
            if not keep:
                relall(th)
            k = wstate["next_use"]
            wstate["next_use"] += 1
            if Eng.DRY:
                wstate["sched"].append((l, b))
                return wring[0], wringB[0]
            assert wstate["sched"][k] == (l, b), (wstate["sched"][k], l, b)
            pump()
            assert k < wstate["next_load"], "weight ring exhausted"
            th.held.append(k)
            sl = wstate["slot_of"][k]
            return wring[sl], wringB[sl]

        def load_nw(th, row):
            QP.dma("nw" + th.name, th.nw[:], nw_d.ap()[row].partition_broadcast(128), writes=[th.nwB])
            return th.nw, th.nwB

        def norm_stats(th, hb, hbB, i):
            ss, ssB = th.ss, th.ssB
            ACT.op(lambda: A.activation(th.junk, hb[:, i, :], AF.Square, accum_out=ss[:, i:i + 1]), [hbB[i]], [th.junkB, ssB])
            DVE.op(lambda: V.tensor_scalar(ss[:, 4 + i:5 + i], ss[:, i:i + 1], 1.0 / D, EPS, ALU.mult, ALU.add), [ssB], [ssB])
            ACT.op(lambda: A.activation(ss[:, 4 + i:5 + i], ss[:, 4 + i:5 + i], AF.Sqrt), [ssB], [ssB])
            DVE.op(lambda: V.reciprocal(ss[:, 4 + i:5 + i], ss[:, 4 + i:5 + i]), [ssB], [ssB])

        def transposes_to(srcs, srcB, dst_ap3, dstB, nk):
            ps, pB = getps()
            psb = ps[:].bitcast(BF)
            for k in range(nk):
                PE.op(lambda: T.transpose(psb[:, k * 128:(k + 1) * 128], srcs[k], identb[:]),
                      [srcB, identbB], [pB], sig=(k == nk - 1))
            ACT.op(lambda: A.copy(dst_ap3, psb[:, 0:nk * 128].rearrange("p (k t) -> p k t", k=nk)), [pB], [dstB])

        def rmsnorm_T(th, hb, hbB, nt, row):
            wr, wrB = load_nw(th, row)
            for i in range(nt):
                norm_stats(th, hb, hbB, i)
                xb, xbB = th.xnb[th.ctr[0] % 2], th.xnbB[th.ctr[0] % 2]
                th.ctr[0] += 1
                DVE.op(lambda: V.scalar_tensor_tensor(xb[:], hb[:, i, :], th.ss[:, 4 + i:5 + i], wr[:], ALU.mult, ALU.mult),
                       [hbB[i], th.ssB, wrB], [xbB])
                transposes_to([xb[:, k * 128:(k + 1) * 128] for k in range(8)], xbB,
                              th.xT[:, :, i * 128:(i + 1) * 128], th.xTB, 8)

        def mm_tok(blk, blkB, i, kc_n, src, srcB):
            ps, pB = getps()
            bv = blk[:, 0:kc_n * 512].rearrange("p (kc c) -> p kc c", kc=kc_n)
            for kc in range(kc_n):
                PE.op(lambda: T.matmul(ps[:], src[:, kc, i * 128:(i + 1) * 128], bv[:, kc, :],
                                       start=(kc == 0), stop=(kc == kc_n - 1)),
                      [srcB, blkB], [pB], sig=(kc == kc_n - 1))
            return ps, pB

        def mm_feat(blk, blkB, f, n, src, srcB):
            ps, pB = getps()
            bv = blk[:].rearrange("p (kc c) -> p kc c", kc=8)
            for kc in range(8):
                PE.op(lambda: T.matmul(ps[:, 0:n], bv[:, kc, f * 128:(f + 1) * 128], src[:, kc, 0:n],
                                       start=(kc == 0), stop=(kc == 7)),
                      [srcB, blkB], [pB], sig=(kc == 7))
            return ps, pB

        cctr = [0]
        rctr = [0]
        tctr = [0]
        taps = {}
        hs0_tok = [None] * depth

        def tap(name, ap, bufs):
            if not debug_taps or Eng.DRY:
                return
            shape = list(ap.shape)
            d = nc.dram_tensor("tap_" + name, shape, ap.dtype, kind="ExternalOutput")
            QP.dma("tap", d.ap(), ap, reads=bufs)

        def mixer(l, nt, is_meta, hb, hbB, info):
            n = nt * 128
            sc = nt / 2.0
            if l == 0:
                if is_meta:
                    QP.dma("xin" + info["par"], hb[:, 0, :], meta_d.ap(), writes=[hbB[0]])
                else:
                    seq, m = info["seq"], info["m"]
                    QP.dma("xin" + info["par"], hb[:], x_d.ap()[seq, m * N:(m + 1) * N, :].rearrange("(i p) d -> p i d", p=128),
                           writes=hbB)
            if info["restore"]:
                POOL.op(lambda: G.tensor_copy(halo[:, l], halo0[:, l]), [halo0B[l]], [haloB[l]])
                POOL.op(lambda: G.tensor_copy(gst[:, l], gst0[:, l]), [gst0B[l]], [gstB[l]])
                if not Eng.DRY:
                    POOL.wait(hs0_tok[l])
                QP.dma("hs0r", Hs[:, l].rearrange("p h e -> p (h e)"), hs0_d.ap()[l], writes=[HsB[l]])
                ACT.op(lambda: A.copy(Hb[:, l], Hs[:, l]), [HsB[l]], [HbB[l]])
            QP.dma("snw", snw[:], snw_d.ap()[l].partition_broadcast(128), writes=[snwB])
            rmsnorm_T(TM, hb, hbB, nt, 2 * l)
            yield 6 * sc
            for cb in range(2):
                blk, bB = getblk(l, cb, TM)
                for i in range(nt):
                    ps, pB = mm_tok(blk, bB, i, 8, xT, xTB)
                    ACT.op(lambda: A.activation(zs[:, i, cb * 512:(cb + 1) * 512], ps[:], AF.Silu), [pB], [zsB[i]])
                    yield 1.8
            for cb in range(4):
                blk, bB = getblk(l, 2 + cb, TM)
                for f in range(4):
                    ft = cb * 4 + f
                    ps, pB = mm_feat(blk, bB, f, n, xT, xTB)
                    c = cctr[0] % 2
                    cctr[0] += 1
                    cw = convp[:, l, ft, :]
                    ACT.op(lambda: A.copy(xr[c][:, 3:3 + n], ps[:, 0:n]), [pB], [xrB[c]])
                    ACT.op(lambda: A.activation(acc[c][:, 0:n], ps[:, 0:n], AF.Identity, bias=cw[:, 4:5], scale=cw[:, 3:4]),
                           [pB, convpB], [accB[c]])
                    POOL.op(lambda: G.tensor_copy(xr[c][:, 0:3], halo[:, l, ft, :]), [haloB[l]], [xrB[c]])
                    POOL.op(lambda: G.tensor_copy(halo[:, l, ft, :], xr[c][:, n:n + 3]), [xrB[c]], [haloB[l]])
                    for k in range(3):
                        DVE.op(lambda: V.scalar_tensor_tensor(acc[c][:, 0:n], xr[c][:, k:k + n], cw[:, k:k + 1], acc[c][:, 0:n],
                                                              ALU.mult, ALU.add), [xrB[c], accB[c], convpB], [accB[c]])
                    ACT.op(lambda: A.activation(xc[:, ft, 0:n], acc[c][:, 0:n], AF.Silu), [accB[c]], [xcB])
                    yield 1.5 * sc
            blk, bB = getblk(l, 6, TM)
            for ct in range(4):
                ps, pB = mm_feat(blk, bB, ct, n, xT, xTB)
                ACT.op(lambda: A.copy(uTf[:, ct, 0:n], ps[:, 0:n]), [pB], [uTfB])
                ACT.op(lambda: A.copy(uTb[:, ct, 0:n], ps[:, 0:n]), [pB], [uTbB])
                yield 1.2 * sc
            for cb in range(4):
                blk, bB = getblk(l, 7 + cb, TM)
                for i in range(nt):
                    ps, pB = mm_tok(blk, bB, i, 8, xT, xTB)
                    ACT.op(lambda: A.activation(gts[:, i, cb * 512:(cb + 1) * 512], ps[:], AF.Sigmoid), [pB], [gtsB[i]])
                    yield 1.8
            for i in range(nt):
                ps, pB = getps()
                for kc in range(8):
                    PE.op(lambda: T.matmul(ps[:, 0:16], xT[:, kc, i * 128:(i + 1) * 128], dtw[:, l, kc, :],
                                           start=(kc == 0), stop=(kc == 7)), [xTB, dtwB], [pB], sig=(kc == 7))
                d0 = sm[:, 0, :]; d1 = sm[:, 1, :]
                DVE.op(lambda: V.tensor_tensor(d0, ps[:, 0:16], ssdp[:, l, 0, :], ALU.add), [pB, ssdpB], [smB])
                ACT.op(lambda: A.activation(d1, d0, AF.Abs), [smB], [smB])
                ACT.op(lambda: A.activation(d1, d1, AF.Exp, scale=-1.0), [smB], [smB])
                ACT.op(lambda: A.activation(d1, d1, AF.Ln, bias=1.0), [smB], [smB])
                DVE.op(lambda: V.tensor_scalar(d0, d0, 0.0, None, ALU.max), [smB], [smB])
                if is_meta:
                    DVE.op(lambda: V.tensor_tensor(d0, d0, d1, ALU.add), [smB], [smB])
                    DVE.op(lambda: V.tensor_scalar(dts[:, i, :], d0, padmask, None, ALU.mult), [smB, cfB], [dtsB[i]])
                else:
                    DVE.op(lambda: V.tensor_tensor(dts[:, i, :], d0, d1, ALU.add), [smB], [dtsB[i]])
                yield 1.5
            for i in range(nt):
                transposes_to([xc[:, k, i * 128:(i + 1) * 128] for k in range(8)], xcB,
                              xtok[:, i, :].rearrange("p (k t) -> p k t", k=8), xtokB[i], 8)
                transposes_to([xc[:, 8 + k, i * 128:(i + 1) * 128] for k in range(4)], xcB,
                              btok[:, i, :].rearrange("p (k t) -> p k t", k=4), btokB[i], 4)
                yield 2.0
            BLk, BLkB = getblk(l, 33, TM)
            CLk, CLkB = getblk(l, 34, TM, keep=True)
            BLv = BLk[:].rearrange("p (a b) -> p a b", a=32)
            CLv = CLk[:].rearrange("p (a b) -> p a b", a=32)
            def load_tab(j):
                i = tctr[0] % 2
                tctr[0] += 1
                QP.dma(f"tab{i}", tring[i][:, :, 0:n], tabs_d.ap()[l, j][:, :, 0:n], writes=[tringB[i]])
                return tring[i], tringB[i]
            t = [x[:, 0:n] for x in s5t]
            tB = s5tB
            TT = V.tensor_tensor
            def cproj(ct):
                psy, pyB = getps()
                idx = 0
                for qq in range(4):
                    for part in range(2):
                        jj = ct * 4 + qq
                        PE.op(lambda: T.matmul(psy[:, 0:n], CLv[:, 2 * jj + part, :], hS[:, part, qq, 0:n],
                                               start=(idx == 0), stop=(idx == 7)), [CLkB, hSB], [pyB], sig=(idx == 7))
                        idx += 1
                a0, a1, a2 = t[0], t[1], t[2]
                DVE.op(lambda: V.scalar_tensor_tensor(a0, uTf[:, ct, 0:n], s5dd[:, l, ct:ct + 1], psy[:, 0:n], ALU.mult, ALU.add),
                       [uTfB, s5ddB, pyB], [tB[0]])
                DVE.op(lambda: TT(a1, a0, a0, ALU.mult), [tB[0]], [tB[1]])
                DVE.op(lambda: V.tensor_scalar(a1, a1, 0.044715, 1.0, ALU.mult, ALU.add), [tB[1]], [tB[1]])
                DVE.op(lambda: TT(a1, a1, a0, ALU.mult), [tB[0], tB[1]], [tB[1]])
                ACT.op(lambda: A.activation(a2, a1, AF.Sigmoid, scale=1.5957691216057308), [tB[1]], [tB[2]])
                DVE.op(lambda: TT(ybT[:, ct, 0:n], a0, a2, ALU.mult), [tB[0], tB[2]], [ybTB])
            tabq = [load_tab(0)]
            pending = []
            for j in range(16):
                ct, q = j // 4, j % 4
                tb, tbB = tabq.pop(0)
                if j + 1 < 16:
                    tabq.append(load_tab(j + 1))
                Ec, Es = tb[:, 0, 0:n], tb[:, 1, 0:n]
                psr, prB = getps()
                PE.op(lambda: T.matmul(psr[:, 0:n], BLv[:, 2 * j, :], uTb[:, ct, 0:n], start=True, stop=True), [BLkB, uTbB], [prB])
                psi, piB = getps()
                PE.op(lambda: T.matmul(psi[:, 0:n], BLv[:, 2 * j + 1, :], uTb[:, ct, 0:n], start=True, stop=True), [BLkB, uTbB], [piB])
                DVE.op(lambda: TT(t[0], psr[:, 0:n], Ec, ALU.mult), [prB, tbB], [tB[0]])
                DVE.op(lambda: TT(t[1], psi[:, 0:n], Es, ALU.mult), [piB, tbB], [tB[1]])
                DVE.op(lambda: TT(t[0], t[0], t[1], ALU.add), [tB[0], tB[1]], [tB[0]])
                DVE.op(lambda: TT(t[2], psi[:, 0:n], Ec, ALU.mult), [piB, tbB], [tB[2]])
                DVE.op(lambda: TT(t[3], psr[:, 0:n], Es, ALU.mult), [prB, tbB], [tB[3]])
                DVE.op(lambda: TT(t[2], t[2], t[3], ALU.subtract), [tB[2], tB[3]], [tB[2]])
                rj = rtab[:, l, j:j + 1].broadcast_to([128, n])
                DVE.op(lambda: V.tensor_tensor_scan(t[4], rj, t[0], gst[:, l, j, 0:1], ALU.mult, ALU.add),
                       [rtabB, tB[0], gstB[l]], [tB[4]])
                DVE.op(lambda: V.tensor_tensor_scan(t[5], rj, t[2], gst[:, l, j, 1:2], ALU.mult, ALU.add),
                       [rtabB, tB[2], gstB[l]], [tB[5]])
                DVE.op(lambda: TT(t[0], t[4], Ec, ALU.mult), [tB[4], tbB], [tB[0]])
                DVE.op(lambda: TT(t[1], t[5], Es, ALU.mult), [tB[5], tbB], [tB[1]])
                DVE.op(lambda: TT(t[6], t[0], t[1], ALU.subtract), [tB[0], tB[1]], [tB[6]])
                DVE.op(lambda: TT(t[2], t[5], Ec, ALU.mult), [tB[5], tbB], [tB[2]])
                DVE.op(lambda: TT(t[3], t[4], Es, ALU.mult), [tB[4], tbB], [tB[3]])
                DVE.op(lambda: TT(t[7], t[2], t[3], ALU.add), [tB[2], tB[3]], [tB[7]])
                yield 5.2 * sc
                while pending:
                    cproj(pending.pop(0))
                    yield 2.0 * sc
                POOL.op(lambda: G.tensor_copy(gst[:, l, j, 0:1], t[6][:, n - 1:n]), [tB[6]], [gstB[l]])
                POOL.op(lambda: G.tensor_copy(gst[:, l, j, 1:2], t[7][:, n - 1:n]), [tB[7]], [gstB[l]])
                ACT.op(lambda: A.copy(hS[:, 0, q, 0:n], t[6]), [tB[6]], [hSB])
                ACT.op(lambda: A.copy(hS[:, 1, q, 0:n], t[7]), [tB[7]], [hSB])
                if q == 3:
                    pending.append(ct)
            while pending:
                cproj(pending.pop(0))
                yield 2.0 * sc
            for b in range(4):
                blk, bB = getblk(l, 11 + b, TM)
                for i in range(nt):
                    ps, pB = mm_tok(blk, bB, i, 4, ybT, ybTB)
                    if b < 2:
                        ACT.op(lambda: A.activation(sg[:, i, b * 512:(b + 1) * 512], ps[:], AF.Sigmoid), [pB], [sgB[i]])
                    else:
                        c0 = (b - 2) * 512
                        DVE.op(lambda: V.tensor_tensor(yb[:, i, c0:c0 + 512], ps[:], sg[:, i, c0:c0 + 512], ALU.mult),
                               [pB, sgB[i]], [ybB[i]])
                    yield 1.0
            for i in range(nt):
                dA = sm[:, 2, :]
                DVE.op(lambda: V.tensor_tensor(dA, dts[:, i, :], arep[:, l, :], ALU.mult), [dtsB[i], arepB], [smB])
                DVE.op(lambda: V.tensor_tensor(lseg[:], strict.unsqueeze(1).broadcast_to([128, 16, 128]),
                                               dA.unsqueeze(2).broadcast_to([128, 16, 128]), ALU.mult), [cfB, smB], [lsegB])
                psc, pcB = getps()
                PE.op(lambda: T.matmul(psc[:, 0:16], tri, dA, start=True, stop=True), [cfB, smB], [pcB])
                PE.op(lambda: T.matmul(psc[:, 16:32], ones, dA, start=True, stop=True), [cfB, smB], [pcB])
                cs = sm[:, 3, :]; ecs = sm[:, 4, :]; wv = sm[:, 5, :]; etot = sm[:, 6, :]
                ACT.op(lambda: A.copy(cs, psc[:, 0:16]), [pcB], [smB])
                ACT.op(lambda: A.activation(etot, psc[:, 16:32], AF.Exp), [pcB], [smB])
                DVE.op(lambda: V.tensor_tensor(wv, psc[:, 16:32], cs, ALU.subtract), [pcB, smB], [smB])
                ACT.op(lambda: A.activation(wv, wv, AF.Exp), [smB], [smB])
                ACT.op(lambda: A.activation(ecs, cs, AF.Exp), [smB], [smB])
                DVE.op(lambda: V.tensor_tensor(wv, wv, dts[:, i, :], ALU.mult), [smB, dtsB[i]], [smB])
                yield 4.0
                for g in range(4):
                    ps, pB = getps()
                    for r in range(4):
                        hh = g * 4 + r
                        PE.op(lambda: T.matmul(ps[:, r * 128:(r + 1) * 128], lseg[:, hh, :], tri, start=True, stop=True),
                              [lsegB, cfB], [pB], sig=(r == 3))
                    ACT.op(lambda: A.activation(Lm[:, g * 4:(g + 1) * 4, :], ps[:].rearrange("p (r t) -> p r t", r=4), AF.Exp),
                           [pB], [LmB])
                ps, pB = getps()
                for g in range(4):
                    PE.op(lambda: T.matmul(ps[:, g * 128:(g + 1) * 128], xc[:, 8 + g, i * 128:(i + 1) * 128],
                                           xc[:, 12 + g, i * 128:(i + 1) * 128], start=True, stop=True), [xcB], [pB], sig=(g == 3))
                DVE.op(lambda: V.tensor_tensor(scm[:], ps[:].rearrange("p (g t) -> p g t", g=4),
                                               tri.unsqueeze(1).broadcast_to([128, 4, 128]), ALU.mult), [pB, cfB], [scmB])
                yield 3.0
                DVE.op(lambda: V.tensor_tensor(Mm[:].rearrange("p (g r) t -> p g r t", g=4),
                                               Lm[:].rearrange("p (g r) t -> p g r t", g=4),
                                               scm[:].unsqueeze(2).broadcast_to([128, 4, 4, 128]), ALU.mult), [LmB, scmB], [MmB])
                x3 = xtok[:, i, :].rearrange("p (h e) -> p h e", h=16)
                DVE.op(lambda: V.tensor_tensor(xdt[:], x3, dts[:, i, :].unsqueeze(2).broadcast_to([128, 16, 64]), ALU.mult),
                       [xtokB[i], dtsB[i]], [xdtB])
                DVE.op(lambda: V.tensor_tensor(xw[:], x3, wv.unsqueeze(2).broadcast_to([128, 16, 64]), ALU.mult),
                       [xtokB[i], smB], [xwB])
                yield 5.0
                pyd = [getps() for _ in range(2)]
                for hh in range(16):
                    ps, pB = pyd[hh // 8]
                    PE.op(lambda: T.matmul(ps[:, (hh % 8) * 64:(hh % 8 + 1) * 64], Mm[:, hh, :], xdt[:, hh, :], start=True, stop=True),
                          [MmB, xdtB], [pB], sig=(hh % 8 == 7))
                pyo = [getps() for _ in range(2)]
                for g in range(4):
                    ps, pB = pyo[g // 2]
                    PE.op(lambda: T.matmul(ps[:, (g % 2) * 256:(g % 2 + 1) * 256], xc[:, 12 + g, i * 128:(i + 1) * 128],
                                           Hb[:, l, g * 4:(g + 1) * 4, :].rearrange("p r e -> p (r e)"), start=True, stop=True),
                          [xcB, HbB[l]], [pB], sig=(g % 2 == 1))
                for half in range(2):
                    sl = slice(half * 512, (half + 1) * 512)
                    hsl = slice(half * 8, (half + 1) * 8)
                    DVE.op(lambda: V.tensor_tensor(yv[:, sl].rearrange("p (h e) -> p h e", h=8),
                                                   pyo[half][0][:].rearrange("p (h e) -> p h e", h=8),
                                                   ecs[:, hsl].unsqueeze(2).broadcast_to([128, 8, 64]), ALU.mult),
                           [pyo[half][1], smB], [yvB])
                    DVE.op(lambda: V.tensor_tensor(yv[:, sl], yv[:, sl], pyd[half][0][:], ALU.add), [yvB, pyd[half][1]], [yvB])
                yield 3.0
                pss = [getps() for _ in range(2)]
                for g in range(4):
                    ps, pB = pss[g // 2]
                    PE.op(lambda: T.matmul(ps[:, (g % 2) * 256:(g % 2 + 1) * 256], btok[:, i, g * 128:(g + 1) * 128],
                                           xw[:, g * 4:(g + 1) * 4, :].rearrange("p r e -> p (r e)"), start=True, stop=True),
                          [btokB[i], xwB], [pB], sig=(g % 2 == 1))
                Hfl = Hs[:, l].rearrange("p h e -> p (h e)")
                DVE.op(lambda: V.tensor_tensor(Hs[:, l], Hs[:, l], etot.unsqueeze(2).broadcast_to([128, 16, 64]), ALU.mult),
                       [HsB[l], smB], [HsB[l]])
                for half in range(2):
                    sl = slice(half * 512, (half + 1) * 512)
                    DVE.op(lambda: V.tensor_tensor(Hfl[:, sl], Hfl[:, sl], pss[half][0][:], ALU.add),
                           [HsB[l], pss[half][1]], [HsB[l]])
                ACT.op(lambda: A.copy(Hb[:, l], Hs[:, l]), [HsB[l]], [HbB[l]])
                yield 3.0
                DVE.op(lambda: V.tensor_tensor(y2[:].rearrange("p (h e) -> p h e", h=16), x3,
                                               ssdp[:, l, 2, :].unsqueeze(2).broadcast_to([128, 16, 64]), ALU.mult),
                       [xtokB[i], ssdpB], [y2B])
                DVE.op(lambda: V.tensor_tensor(yv[:], yv[:], y2[:], ALU.add), [yvB, y2B], [yvB])
                DVE.op(lambda: V.tensor_tensor(yv[:], yv[:], zs[:, i, :], ALU.mult), [yvB, zsB[i]], [yvB])
                gss = sm[:, 7, 0:4]; grs = sm[:, 7, 4:8]
                for g in range(4):
                    ACT.op(lambda: A.activation(y2[:, g * 256:(g + 1) * 256], yv[:, g * 256:(g + 1) * 256], AF.Square,
                                                accum_out=gss[:, g:g + 1]), [yvB, smB], [y2B, smB])
                DVE.op(lambda: V.tensor_scalar(grs, gss, 1.0 / 256, EPS, ALU.mult, ALU.add), [smB], [smB])
                ACT.op(lambda: A.activation(grs, grs, AF.Sqrt), [smB], [smB])
                DVE.op(lambda: V.reciprocal(grs, grs), [smB], [smB])
                yield 5.0
                DVE.op(lambda: V.tensor_tensor(yv[:].rearrange("p (g e) -> p g e", g=4), yv[:].rearrange("p (g e) -> p g e", g=4),
                                               grs.unsqueeze(2).broadcast_to([128, 4, 256]), ALU.mult), [yvB, smB], [yvB])
                DVE.op(lambda: V.tensor_tensor(yv[:], yv[:], snw[:], ALU.mult), [yvB, snwB], [yvB])
                DVE.op(lambda: V.tensor_tensor(yv[:], yv[:], gts[:, i, 0:D], ALU.mult), [yvB, gtsB[i]], [yvB])
                DVE.op(lambda: V.tensor_tensor(y2[:], yb[:, i, :], gts[:, i, D:2 * D], ALU.mult), [ybB[i], gtsB[i]], [y2B])
                mixb, mixbB = TM.xnb[TM.ctr[0] % 2], TM.xnbB[TM.ctr[0] % 2]
                TM.ctr[0] += 1
                DVE.op(lambda: V.tensor_tensor(mixb[:], yv[:], y2[:], ALU.add), [yvB, y2B], [mixbB])
                transposes_to([mixb[:, k * 128:(k + 1) * 128] for k in range(8)], mixbB,
                              xT[:, :, i * 128:(i + 1) * 128], xTB, 8)
                yield 6.0
            for cb in range(2):
                blk, bB = getblk(l, 15 + cb, TM)
                for i in range(nt):
                    ps, pB = mm_tok(blk, bB, i, 8, xT, xTB)
                    DVE.op(lambda: V.tensor_tensor(hb[:, i, cb * 512:(cb + 1) * 512], hb[:, i, cb * 512:(cb + 1) * 512], ps[:], ALU.add),
                           [hbB[i], pB], [hbB[i]])
                    yield 1.8
            relall(TM)
            if info["save"]:
                POOL.op(lambda: G.tensor_copy(halo0[:, l], halo[:, l]), [haloB[l]], [halo0B[l]])
                POOL.op(lambda: G.tensor_copy(gst0[:, l], gst[:, l]), [gstB[l]], [gst0B[l]])
                tk = QP.dma("hs0w", hs0_d.ap()[l], Hs[:, l].rearrange("p h e -> p (h e)"), reads=[HsB[l]])
                if not Eng.DRY:
                    hs0_tok[l] = tk

        def ffn(l, nt, is_meta, hb, hbB, info):
            n = nt * 128
            sc = nt / 2.0
            rmsnorm_T(TF, hb, hbB, nt, 2 * l + 1)
            yield 6 * sc * FW
            for kg in range(4):
                for c in range(2):
                    blk, bB = getblk(l, 17 + kg * 4 + c, TF)
                    for f in range(4):
                        ps, pB = mm_feat(blk, bB, f, n, hnT, hnTB)
                        rr = rctr[0] % 2
                        rctr[0] += 1
                        ACT.op(lambda: A.activation(rl[rr][:, 0:n], ps[:, 0:n], AF.Relu), [pB], [rlB[rr]])
                        POOL.op(lambda: G.tensor_tensor(hid[:, (kg % 2) * 8 + c * 4 + f, 0:n], rl[rr][:, 0:n], rl[rr][:, 0:n], ALU.mult),
                                [rlB[rr]], [hidB])
                        yield 1.2 * sc * FW
                for c in range(2):
                    blk, bB = getblk(l, 17 + kg * 4 + 2 + c, TF)
                    for i in range(nt):
                        ps, pB = getps()
                        bv = blk[:].rearrange("p (kc c) -> p kc c", kc=8)
                        for kc in range(8):
                            PE.op(lambda: T.matmul(ps[:], hid[:, (kg % 2) * 8 + kc, i * 128:(i + 1) * 128], bv[:, kc, :],
                                                   start=(kc == 0), stop=(kc == 7)), [hidB, bB], [pB], sig=(kc == 7))
                        DVE.op(lambda: V.tensor_tensor(hb[:, i, c * 512:(c + 1) * 512], hb[:, i, c * 512:(c + 1) * 512], ps[:], ALU.add),
                               [hbB[i], pB], [hbB[i]])
                        yield 1.8 * FW
            relall(TF)
            if is_meta:
                DVE.op(lambda: V.tensor_scalar(hb[:, 0, :], hb[:, 0, :], padmask, None, ALU.mult), [hbB[0], cfB], [hbB[0]])
            tg = ("m" if is_meta else "s") + str(l)
            if tg not in taps:
                taps[tg] = 1
                tap(tg + "_h", hb[:, 0:nt, :], hbB)
            if l == depth - 1 and not is_meta:
                seq, m = info["seq"], info["m"]
                wr, wrB = load_nw(TF, 2 * depth)
                for i in range(NT):
                    norm_stats(TF, hb, hbB, i)
                for i in range(NT):
                    DVE.op(lambda: V.scalar_tensor_tensor(obuf[:, i, :], hb[:, i, :], TF.ss[:, 4 + i:5 + i], wr[:], ALU.mult, ALU.mult),
                           [hbB[i], TF.ssB, wrB], [hidB])
                QP.dma("oout", out_d.ap()[seq, m * N:(m + 1) * N, :].rearrange("(i p) d -> p i d", p=128), obuf[:],
                       reads=[hidB])
                yield 4.0 * FW

        FW = 3.0

        items = [("meta", 0, 0)] + [("seq", s_, m_) for s_ in range(NSEQ) for m_ in range(NMT)]

        def emit_all():
            for cnt in (pctr, cctr, rctr, tctr, TM.ctr, TF.ctr):
                cnt[0] = 0
            wstate["next_use"] = 0
            wstate["next_load"] = 0
            wstate["free"] = list(range(NW))
            wstate["slot_of"] = {}
            TM.held, TF.held = [], []
            taps.clear()
            Mx, Fx = [], []
            for p0 in range(0, len(items), 2):
                pair = [(k, items[k]) for k in range(p0, min(p0 + 2, len(items)))]
                for l in range(depth):
                    for slot in range(2):
                        if slot < len(pair):
                            k, (kind, seq, m) = pair[slot]
                            is_meta = kind == "meta"
                            nt = 1 if is_meta else NT
                            info = {"seq": seq, "m": m, "par": str(k % 2), "save": is_meta,
                                    "restore": (kind == "seq" and m == 0 and seq > 0)}
                            Mx.append((mixer, (l, nt, is_meta, hbuf[k % 2], hbufB[k % 2], info)))
                            Fx.append((ffn, (l, nt, is_meta, hbuf[k % 2], hbufB[k % 2], info)))
                        else:
                            Mx.append(None)
                            Fx.append(None)
            nsteps = len(Mx) + 1
            for st in range(nsteps):
                gens = []
                if st < len(Mx) and Mx[st] is not None:
                    gens.append(Mx[st][0](*Mx[st][1]))
                if st >= 1 and Fx[st - 1] is not None:
                    gens.append(Fx[st - 1][0](*Fx[st - 1][1]))
                tacc = [0.0] * len(gens)
                alive = list(range(len(gens)))
                while alive:
                    gi = min(alive, key=lambda a: tacc[a])
                    try:
                        tacc[gi] += next(gens[gi])
                    except StopIteration:
                        alive.remove(gi)

        Eng.DRY = True
        wstate["sched"] = []
        emit_all()
        Eng.DRY = False
        emit_all()
        assert wstate["next_use"] == len(wstate["sched"])

        for s in QP.sems.values():
            POOL.e.wait_ge(s[0], s[1])
        for E in (PE, ACT, DVE, POOL):
            for E2 in (PE, ACT, DVE, POOL):
                if E2.cnt > 0 and E2.last_sig:
                    E.e.wait_ge(E2.sem, E2.cnt)
    return nc


def host_prep(inputs, depth=DEPTH):
    f = lambda a: np.ascontiguousarray(np.asarray(a, dtype=np.float32))
    p = {}
    meta = f(inputs["meta_tokens"])
    mp = np.zeros((128, D), np.float32)
    mp[128 - NMETA:] = meta
    p["meta_pad"] = mp
    for k in ("w_in", "w_glu", "w_out", "w_ff_in", "w_ff_out"):
        p[k] = f(inputs[k])
    rows = []
    for l in range(depth):
        rows += [inputs["norm_mix_w"][l], inputs["norm_mlp_w"][l]]
    rows.append(inputs["final_norm_w"])
    p["nw"] = f(np.stack([np.asarray(r) for r in rows]))
    p["snw"] = f(inputs["ssd_norm_w"])
    cw = np.concatenate([np.asarray(inputs["conv_w"]), np.asarray(inputs["conv_b"])[:, None, :]], axis=1)
    p["convp"] = f(cw.reshape(depth, 5, 16, 128).transpose(0, 3, 2, 1))
    p["ssdp"] = f(np.stack([np.asarray(inputs["dt_bias"]), np.asarray(inputs["ssd_a_log"]), np.asarray(inputs["ssd_d"])], axis=1))
    def st(a):
        return np.asarray(a).reshape(depth, 16, 2, 64).transpose(0, 2, 3, 1).reshape(depth, 128, 16)
    ls = np.broadcast_to(np.asarray(inputs["s5_log_step"])[:, :, None], (depth, 32, 64))
    p["s5s"] = f(np.stack([st(inputs["s5_a_re"]), st(inputs["s5_a_im"]), st(ls)], axis=1))
    def stb(a):
        return np.asarray(a).reshape(depth, 16, 2, 64, 16).transpose(0, 2, 3, 1, 4).reshape(depth, 128, 16, 16)
    p["s5b"] = f(np.stack([stb(inputs["s5_b_re"]), stb(inputs["s5_b_im"])], axis=1))
    cre = np.asarray(inputs["s5_c_re"]).transpose(0, 1, 3, 2)
    cim = np.asarray(inputs["s5_c_im"]).transpose(0, 1, 3, 2)
    p["s5c"] = f(np.stack([stb(cre), stb(cim)], axis=1))
    p["s5d"] = f(np.asarray(inputs["s5_d"]).reshape(depth, 4, 128).transpose(0, 2, 1))
    k = np.arange(128)
    cf = np.zeros((128, 4 * 128 + 512 + 1), np.float32)
    cf[:, 0:128] = np.eye(128)
    cf[:, 128:256] = (k[:, None] <= k[None, :])
    cf[:, 256:384] = (k[:, None] > k[None, :])
    cf[:, 384:512] = 1.0
    cf[:, 512:1024] = np.arange(1, 513)[None, :]
    cf[:, 1024] = (k >= 128 - NMETA)
    p["cf"] = cf
    p["cb"] = np.eye(128).astype(ml_dtypes.bfloat16)
    return p


_CACHE = {}


def kernel(**inputs):
    x = np.asarray(inputs["x"], dtype=np.float32)
    B, S, _ = x.shape
    ncores = 8
    nseq = B // ncores
    NT = 2
    nmt = S // (NT * 128)
    key = (nseq, nmt)
    if key not in _CACHE:
        _CACHE[key] = build_program(nseq, nmt, NT)
    nc = _CACHE[key]
    p = host_prep(inputs)
    in_maps = []
    for c in range(ncores):
        m = dict(p)
        m["x"] = np.ascontiguousarray(x[c * nseq:(c + 1) * nseq])
        in_maps.append(m)
    res = run_bass_kernel_spmd(nc, in_maps, core_ids=list(range(ncores)))
    out = np.concatenate([np.asarray(r["out"]) for r in res.results], axis=0)
    return out.astype(np.float32)
```

```python
from contextlib import ExitStack
import numpy as np
import ml_dtypes
import concourse.bass as bass
import concourse.mybir as mybir
from concourse.bass_utils import run_bass_kernel_spmd

F32 = mybir.dt.float32
BF = mybir.dt.bfloat16
ALU = mybir.AluOpType
AF = mybir.ActivationFunctionType

D = 1024
NMETA = 16
DEPTH = 2
EPS = 1e-6
NBLK = 35
TWO_PI = 6.283185307179586
PI = 3.141592653589793


class Tok:
    __slots__ = ("sem", "val", "eng")

    def __init__(self, sem, val, eng):
        self.sem, self.val, self.eng = sem, val, eng


class Buf:
    def __init__(self, name):
        self.name = name
        self.w = None
        self.r = []


class Eng:
    DRY = False

    def __init__(self, nc, es, e, name):
        self.nc, self.es, self.e, self.name = nc, es, e, name
        self.k = 0
        self._new_sem()
        self.seen = {}
        self.last_sig = True

    def _new_sem(self):
        self.sem = self.es.enter_context(self.nc.semaphore(f"{self.name}_s{self.k}"))
        self.k += 1
        self.cnt = 0

    def wait(self, tok):
        if tok is None:
            return
        key = id(tok.sem)
        if self.seen.get(key, 0) >= tok.val:
            return
        if tok.eng is self and self.name == "pe":
            return
        if tok.eng is not None:
            assert tok.sem is not tok.eng.sem or tok.val <= tok.eng.cnt, (self.name, tok.eng.name)
        self.e.wait_ge(tok.sem, tok.val)
        self.seen[key] = tok.val

    def op(self, fn, reads=(), writes=(), sig=True):
        if Eng.DRY:
            return None
        for b in reads:
            self.wait(b.w)
        for b in writes:
            self.wait(b.w)
            for t in b.r:
                self.wait(t)
        if sig and self.last_sig and self.cnt >= 30000:
            self._new_sem()
        inst = fn()
        if sig:
            self.cnt += 1
            inst.then_inc(self.sem, 1)
            tok = Tok(self.sem, self.cnt, self)
            self.last_sig = True
        else:
            tok = Tok(self.sem, self.cnt + 1, self)
            self.last_sig = False
        for b in reads:
            b.r = [t for t in b.r if t.sem is not tok.sem] + [tok]
        for b in writes:
            b.w = tok
            b.r = []
        return tok


class DmaQ:
    def __init__(self, nc, es, eng):
        self.nc, self.es, self.eng = nc, es, eng
        self.sems = {}

    def dma(self, slot, out, in_, reads=(), writes=()):
        if Eng.DRY:
            return None
        E = self.eng
        for b in reads:
            E.wait(b.w)
        for b in writes:
            E.wait(b.w)
            for t in b.r:
                E.wait(t)
        if slot not in self.sems:
            self.sems[slot] = [self.es.enter_context(self.nc.semaphore(f"dq_{slot}")), 0]
        s = self.sems[slot]
        s[1] += 16
        E.e.dma_start(out=out, in_=in_).then_inc(s[0], 16)
        tok = Tok(s[0], s[1], None)
        for b in reads:
            b.r = [t for t in b.r if t.sem is not tok.sem] + [tok]
        for b in writes:
            b.w = tok
            b.r = []
        return tok


def build_program(NSEQ, NMT, NT=4, depth=DEPTH, debug_taps=False):
    N = NT * 128
    S = NMT * N
    nc = bass.Bass("TRN2", target_bir_lowering=False)
    dt_in = lambda name, shape, dt=F32: nc.dram_tensor(name, shape, dt, kind="ExternalInput")
    x_d = dt_in("x", [NSEQ, S, D])
    meta_d = dt_in("meta_pad", [128, D])
    w_in_d = dt_in("w_in", [depth, D, 5648])
    w_glu_d = dt_in("w_glu", [depth, 512, 2048])
    w_out_d = dt_in("w_out", [depth, D, D])
    w_ffi_d = dt_in("w_ff_in", [depth, D, 4096])
    w_ffo_d = dt_in("w_ff_out", [depth, 4096, D])
    nw_d = dt_in("nw", [2 * depth + 1, D])
    snw_d = dt_in("snw", [depth, D])
    convp_d = dt_in("convp", [depth, 128, 16, 5])
    ssdp_d = dt_in("ssdp", [depth, 3, 16])
    s5s_d = dt_in("s5s", [depth, 3, 128, 16])
    s5b_d = dt_in("s5b", [depth, 2, 128, 16, 16])
    s5c_d = dt_in("s5c", [depth, 2, 128, 16, 16])
    s5d_d = dt_in("s5d", [depth, 128, 4])
    cf_d = dt_in("cf", [128, 4 * 128 + 512 + 2])
    cb_d = dt_in("cb", [128, 128], BF)
    out_d = nc.dram_tensor("out", [NSEQ, S, D], F32, kind="ExternalOutput")
    wblk_d = nc.dram_tensor("wblk", [depth, NBLK, 128, 4096], BF)
    tabs_d = nc.dram_tensor("tabs", [depth, 16, 128, 2, 512], F32)
    hs0_d = nc.dram_tensor("hs0", [depth, 128, 1024], F32)

    es = ExitStack()
    with es:
        sb = lambda name, shape, dt=F32: es.enter_context(nc.sbuf_tensor("s_" + name, shape, dt))
        PE = Eng(nc, es, nc.tensor, "pe")
        ACT = Eng(nc, es, nc.scalar, "act")
        DVE = Eng(nc, es, nc.vector, "dve")
        POOL = Eng(nc, es, nc.gpsimd, "pool")
        SP = Eng(nc, es, nc.sync, "sp")
        QW = DmaQ(nc, es, SP)
        QP = DmaQ(nc, es, POOL)

        cf = sb("cf", [128, 4 * 128 + 512 + 2]); cfB = Buf("cf")
        identb = sb("identb", [128, 128], BF); identbB = Buf("identb")
        QP.dma("c0", cf[:], cf_d.ap(), writes=[cfB])
        QP.dma("c1", identb[:], cb_d.ap(), writes=[identbB])
        identf = cf[:, 0:128]
        tri = cf[:, 128:256]
        strict = cf[:, 256:384]
        ones = cf[:, 384:512]
        tpos = cf[:, 512:1024]
        padmask = cf[:, 1024:1025]
        epsc = cf[:, 1025:1026]

        convp = sb("convp", [128, depth, 16, 5]); convpB = Buf("convp")
        ssdp = sb("ssdp", [128, depth, 3, 16]); ssdpB = Buf("ssdp")
        s5dd = sb("s5dd", [128, depth, 4]); s5ddB = Buf("s5dd")
        dtw = sb("dtw", [128, depth, 8, 16], BF); dtwB = Buf("dtw")
        for l in range(depth):
            QP.dma("c2", convp[:, l], convp_d.ap()[l], writes=[convpB])
            QP.dma("c3", ssdp[:, l].rearrange("p a b -> p (a b)"),
                   ssdp_d.ap()[l].rearrange("a b -> (a b)").partition_broadcast(128), writes=[ssdpB])
            QP.dma("c4", s5dd[:, l], s5d_d.ap()[l], writes=[s5ddB])
            QP.dma("c5", dtw[:, l], w_in_d.ap()[l][:, 3072:3088].rearrange("(kc p) c -> p kc c", p=128),
                   writes=[dtwB])
        arep = sb("arep", [128, depth, 16]); arepB = Buf("arep")
        ACT.op(lambda: nc.scalar.activation(arep[:], ssdp[:, :, 1, :], AF.Exp), [ssdpB], [arepB])
        DVE.op(lambda: nc.vector.tensor_scalar(arep[:], arep[:], -1.0, None, ALU.mult), [arepB], [arepB])

        wscB = Buf("wscratch")
        def blkview(l, b, kc):
            return wblk_d.ap()[l, b][:, 0:kc * 512].rearrange("p (kc c) -> p kc c", kc=kc)
        def wsrc(wd, l, r0, kc, c0):
            return wd.ap()[l][r0:r0 + kc * 128, c0:c0 + 512].rearrange("(kc p) c -> p kc c", p=128)
        for l in range(depth):
            cols = [0, 512] + [1024 + 512 * i for i in range(4)] + [3088] + [3600 + 512 * i for i in range(4)]
            for b, c0 in enumerate(cols):
                QP.dma("pre", blkview(l, b, 8), wsrc(w_in_d, l, 0, 8, c0))
            for b, cbi in enumerate([2, 3, 0, 1]):
                QP.dma("pre", blkview(l, 11 + b, 4), wsrc(w_glu_d, l, 0, 4, cbi * 512))
            for b in range(2):
                QP.dma("pre", blkview(l, 15 + b, 8), wsrc(w_out_d, l, 0, 8, b * 512))
            for kg in range(4):
                for c in range(2):
                    QP.dma("pre", blkview(l, 17 + kg * 4 + c, 8), wsrc(w_ffi_d, l, 0, 8, kg * 1024 + c * 512))
                for c in range(2):
                    QP.dma("pre", blkview(l, 17 + kg * 4 + 2 + c, 8), wsrc(w_ffo_d, l, kg * 1024, 8, c * 512))

        psum = [es.enter_context(nc.psum_tensor(f"ps{i}", [128, 512], F32)) for i in range(8)]
        psB = [Buf(f"ps{i}") for i in range(8)]
        pctr = [0]
        def getps():
            i = pctr[0] % 8
            pctr[0] += 1
            return psum[i], psB[i]

        s5s = sb("s5s", [128, depth, 3, 16]); s5sB = Buf("s5s")
        rtab = sb("rtab", [128, depth, 16]); rtabB = Buf("rtab")
        gst = sb("gst", [128, depth, 16, 2]); gstB = [Buf(f"gst{l}") for l in range(depth)]
        gst0 = sb("gst0", [128, depth, 16, 2]); gst0B = Buf("gst0")
        with ExitStack() as es2:
            sb2 = lambda name, shape, dt=F32: es2.enter_context(nc.sbuf_tensor("s_" + name, shape, dt))
            th = sb2("th", [128, 16]); thB = Buf("th")
            stp = sb2("stp", [128, 16]); stpB = Buf("stp")
            ang = sb2("ang", [128, 2, 512]); angB = Buf("ang")
            ang2 = sb2("ang2", [128, 2, 512]); ang2B = Buf("ang2")
            tabt = [sb2(f"tabt{i}", [128, 2, 512]) for i in range(2)]; tabtB = [Buf(f"tabt{i}") for i in range(2)]
            ab = sb2("ab", [128, 2, 16]); abB = Buf("ab")
            ff = sb2("ff", [128, 6, 16]); ffB = Buf("ff")
            bc_in = sb2("bc_in", [128, 4, 16, 16]); bcinB = Buf("bc_in")
            bbar = sb2("bbar", [128, 2, 16, 16]); bbarB = Buf("bbar")
            tmp = sb2("tmp", [128, 2, 16, 16]); tmpB = Buf("tmp5")
            bd = [sb2(f"bd{i}", [128, 128]) for i in range(2)]; bdB = [Buf(f"bd{i}") for i in range(2)]
            BLs = sb2("BLs", [128, 32, 128], BF); BLsB = Buf("BLs")
            CLs = sb2("CLs", [128, 32, 128], BF); CLsB = Buf("CLs")
            for l in range(depth):
                for a in range(3):
                    QP.dma("c6", s5s[:, l, a], s5s_d.ap()[l, a], writes=[s5sB])
                for a in range(2):
                    QP.dma("c7", bc_in[:, a], s5b_d.ap()[l, a], writes=[bcinB])
                    QP.dma("c7", bc_in[:, 2 + a], s5c_d.ap()[l, a], writes=[bcinB])
                ACT.op(lambda: nc.scalar.activation(stp[:], s5s[:, l, 2, :], AF.Exp), [s5sB], [stpB])
                DVE.op(lambda: nc.vector.tensor_tensor(th[:], s5s[:, l, 1, :], stp[:], ALU.mult), [s5sB, stpB], [thB])
                DVE.op(lambda: nc.vector.tensor_tensor(stp[:], s5s[:, l, 0, :], stp[:], ALU.mult), [s5sB, stpB], [stpB])
                ACT.op(lambda: nc.scalar.activation(rtab[:, l, :], stp[:], AF.Exp), [stpB], [rtabB])
                for j in range(16):
                    tt, ttB = tabt[j % 2], tabtB[j % 2]
                    DVE.op(lambda: nc.vector.tensor_scalar(ang[:, 0, :], tpos, th[:, j:j + 1], 0.5 * PI, ALU.mult, ALU.add),
                           [cfB, thB], [angB])
                    DVE.op(lambda: nc.vector.tensor_scalar(ang[:, 1, :], tpos, th[:, j:j + 1], None, ALU.mult),
                           [cfB, thB], [angB])
                    MAGIC = 12582912.0
                    DVE.op(lambda: nc.vector.tensor_scalar(ang2[:], ang[:], 1.0 / TWO_PI, MAGIC, ALU.mult, ALU.add), [angB], [ang2B])
                    DVE.op(lambda: nc.vector.tensor_scalar(ang2[:], ang2[:], -MAGIC, -TWO_PI, ALU.add, ALU.mult), [ang2B], [ang2B])
                    DVE.op(lambda: nc.vector.tensor_tensor(ang[:], ang[:], ang2[:], ALU.add), [angB, ang2B], [angB])
                    DVE.op(lambda: nc.vector.tensor_scalar(ang[:], ang[:], -PI, PI, ALU.max, ALU.min), [angB], [angB])
                    ACT.op(lambda: nc.scalar.activation(tt[:], ang[:], AF.Sin), [angB], [ttB])
                    QP.dma("tabw", tabs_d.ap()[l, j], tt[:], reads=[ttB])
                    POOL.op(lambda: nc.gpsimd.tensor_copy(ab[:, :, j:j + 1], tt[:, :, 0:1]), [ttB], [abB])
                DVE.op(lambda: nc.vector.tensor_tensor(ab[:], ab[:], rtab[:, l, :].unsqueeze(1).broadcast_to([128, 2, 16]), ALU.mult),
                       [abB, rtabB], [abB])
                are, aim = s5s[:, l, 0, :], s5s[:, l, 1, :]
                V = nc.vector
                DVE.op(lambda: V.tensor_scalar(ff[:, 0], ab[:, 0], -1.0, None, ALU.add), [abB], [ffB])
                DVE.op(lambda: V.tensor_tensor(ff[:, 1], are, are, ALU.mult), [s5sB], [ffB])
                DVE.op(lambda: V.tensor_tensor(ff[:, 4], aim, aim, ALU.mult), [s5sB], [ffB])
                DVE.op(lambda: V.tensor_tensor(ff[:, 1], ff[:, 1], ff[:, 4], ALU.add), [ffB], [ffB])
                DVE.op(lambda: V.reciprocal(ff[:, 1], ff[:, 1]), [ffB], [ffB])
                DVE.op(lambda: V.tensor_tensor(ff[:, 2], ff[:, 0], are, ALU.mult), [ffB, s5sB], [ffB])
                DVE.op(lambda: V.tensor_tensor(ff[:, 4], ab[:, 1], aim, ALU.mult), [abB, s5sB], [ffB])
                DVE.op(lambda: V.tensor_tensor(ff[:, 2], ff[:, 2], ff[:, 4], ALU.add), [ffB], [ffB])
                DVE.op(lambda: V.tensor_tensor(ff[:, 2], ff[:, 2], ff[:, 1], ALU.mult), [ffB], [ffB])
                DVE.op(lambda: V.tensor_tensor(ff[:, 3], ab[:, 1], are, ALU.mult), [abB, s5sB], [ffB])
                DVE.op(lambda: V.tensor_tensor(ff[:, 4], ff[:, 0], aim, ALU.mult), [ffB, s5sB], [ffB])
                DVE.op(lambda: V.tensor_tensor(ff[:, 3], ff[:, 3], ff[:, 4], ALU.subtract), [ffB], [ffB])
                DVE.op(lambda: V.tensor_tensor(ff[:, 3], ff[:, 3], ff[:, 1], ALU.mult), [ffB], [ffB])
                if debug_taps and l == 0:
                    for nm, ap_, bb in (("th", th[:], thB), ("ff", ff[:], ffB), ("ab", ab[:], abB), ("s5s", s5s[:, 0], s5sB), ("tab15", tabt[1][:], tabtB[1])):
                        d_ = nc.dram_tensor("tap_pre_" + nm, list(ap_.shape), ap_.dtype, kind="ExternalOutput")
                        QP.dma("tap", d_.ap(), ap_, reads=[bb])
                fre = ff[:, 2].unsqueeze(2).broadcast_to([128, 16, 16])
                fim = ff[:, 3].unsqueeze(2).broadcast_to([128, 16, 16])
                DVE.op(lambda: V.tensor_tensor(bbar[:, 0], bc_in[:, 0], fre, ALU.mult), [bcinB, ffB], [bbarB])
                DVE.op(lambda: V.tensor_tensor(tmp[:, 0], bc_in[:, 1], fim, ALU.mult), [bcinB, ffB], [tmpB])
                DVE.op(lambda: V.tensor_tensor(bbar[:, 0], bbar[:, 0], tmp[:, 0], ALU.subtract), [bbarB, tmpB], [bbarB])
                DVE.op(lambda: V.tensor_tensor(bbar[:, 1], bc_in[:, 1], fre, ALU.mult), [bcinB, ffB], [bbarB])
                DVE.op(lambda: V.tensor_tensor(tmp[:, 1], bc_in[:, 0], fim, ALU.mult), [bcinB, ffB], [tmpB])
                DVE.op(lambda: V.tensor_tensor(bbar[:, 1], bbar[:, 1], tmp[:, 1], ALU.add), [bbarB, tmpB], [bbarB])
                DVE.op(lambda: V.tensor_scalar(bc_in[:, 3], bc_in[:, 3], -1.0, None, ALU.mult), [bcinB], [bcinB])
                POOL.op(lambda: nc.gpsimd.memset(CLs[:], 0.0), [], [CLsB])
                for j in range(16):
                    q = j % 4
                    for part in range(2):
                        for two in range(2):
                            c0 = 32 * q + 16 * two
                            POOL.op(lambda: nc.gpsimd.tensor_copy(
                                CLs[64 * two:64 * two + 64, 2 * j + part, c0:c0 + 16],
                                bc_in[64 * two:64 * two + 64, 2 + part, j, :]), [bcinB], [CLsB])
                for j in range(16):
                    q = j % 4
                    for part in range(2):
                        k = (2 * j + part) % 2
                        POOL.op(lambda: nc.gpsimd.memset(bd[k][:], 0.0), [], [bdB[k]])
                        for two in range(2):
                            c0 = 32 * q + 16 * two
                            POOL.op(lambda: nc.gpsimd.tensor_copy(
                                bd[k][64 * two:64 * two + 64, c0:c0 + 16],
                                bbar[64 * two:64 * two + 64, part, j, :]), [bbarB], [bdB[k]])
                        ps, pB = getps()
                        PE.op(lambda: nc.tensor.transpose(ps[:, 0:128], bd[k][:], identf), [bdB[k], cfB], [pB])
                        ACT.op(lambda: nc.scalar.copy(BLs[:, 2 * j + part, :], ps[:, 0:128]), [pB], [BLsB])
                if debug_taps and l == 0:
                    for nm, ap_, bb in (("BLs", BLs[:], BLsB), ("CLs", CLs[:], CLsB), ("bbar", bbar[:], bbarB)):
                        d_ = nc.dram_tensor("tap_pre_" + nm, list(ap_.shape), ap_.dtype, kind="ExternalOutput")
                        QP.dma("tap", d_.ap(), ap_, reads=[bb])
                for hf in range(4):
                    QP.dma("tabw", wblk_d.ap()[l, 33][:, hf * 1024:(hf + 1) * 1024].rearrange("p (a b) -> p a b", a=8),
                           BLs[:, hf * 8:(hf + 1) * 8, :], reads=[BLsB])
                    QP.dma("tabw", wblk_d.ap()[l, 34][:, hf * 1024:(hf + 1) * 1024].rearrange("p (a b) -> p a b", a=8),
                           CLs[:, hf * 8:(hf + 1) * 8, :], reads=[CLsB])
            for s in QP.sems.values():
                SP.e.wait_ge(s[0], s[1])
                POOL.e.wait_ge(s[0], s[1])

        hbuf = [sb(f"h{k}", [128, NT, D]) for k in range(2)]
        hbufB = [[Buf(f"h{k}_{i}") for i in range(NT)] for k in range(2)]
        xT = sb("xT", [128, 8, N], BF); xTB = Buf("xT")
        hnT = sb("hnT", [128, 8, N], BF); hnTB = Buf("hnT")
        zs = sb("zs", [128, NT, D], BF); zsB = [Buf(f"zs{i}") for i in range(NT)]
        gts = sb("gts", [128, NT, 2048], BF); gtsB = [Buf(f"gts{i}") for i in range(NT)]
        xc = sb("xc", [128, 16, N], BF); xcB = Buf("xc")
        hid = sb("hid", [128, 16, N], BF); hidB = Buf("hid")
        obuf = hid[:].rearrange("p a b -> p (a b)").bitcast(F32).rearrange("p (i d) -> p i d", d=D)
        xtok = sb("xtok", [128, NT, D], BF); xtokB = [Buf(f"xtok{i}") for i in range(NT)]
        btok = sb("btok", [128, NT, 512], BF); btokB = [Buf(f"btok{i}") for i in range(NT)]
        uTf = sb("uTf", [128, 4, N]); uTfB = Buf("uTf")
        uTb = sb("uTb", [128, 4, N], BF); uTbB = Buf("uTb")
        ybT = sb("ybT", [128, 4, N], BF); ybTB = Buf("ybT")
        yb = sb("yb", [128, NT, D], BF); ybB = [Buf(f"yb{i}") for i in range(NT)]
        dts = sb("dts", [128, NT, 16]); dtsB = [Buf(f"dts{i}") for i in range(NT)]
        NW = 5
        wring = [sb(f"wr{i}", [128, 4096], BF) for i in range(NW)]; wringB = [Buf(f"wr{i}") for i in range(NW)]
        tring = [sb(f"tr{i}", [128, 2, N]) for i in range(2)]; tringB = [Buf(f"tr{i}") for i in range(2)]
        snw = sb("snwr", [128, D]); snwB = Buf("snw")
        halo = sb("halo", [128, depth, 16, 3]); haloB = [Buf(f"halo{l}") for l in range(depth)]
        halo0 = sb("halo0", [128, depth, 16, 3]); halo0B = [Buf(f"halo0{l}") for l in range(depth)]
        gst0B = [Buf(f"gst0{l}") for l in range(depth)]
        Hs = sb("Hs", [128, depth, 16, 64]); HsB = [Buf(f"Hs{l}") for l in range(depth)]
        Hb = sb("Hb", [128, depth, 16, 64], BF); HbB = [Buf(f"Hb{l}") for l in range(depth)]
        xr = [sb(f"xr{i}", [128, N + 3]) for i in range(2)]; xrB = [Buf(f"xr{i}") for i in range(2)]
        acc = [sb(f"acc{i}", [128, N]) for i in range(2)]; accB = [Buf(f"acc{i}") for i in range(2)]
        rl = [sb(f"rl{i}", [128, N]) for i in range(2)]; rlB = [Buf(f"rl{i}") for i in range(2)]
        sm = sb("sm", [128, 8, 16]); smB = Buf("sm")
        lseg = sb("lseg", [128, 16, 128]); lsegB = Buf("lseg")
        Lm = sb("Lm", [128, 16, 128], BF); LmB = Buf("Lm")
        sg = lseg[:].rearrange("p a b -> p (a b)").bitcast(BF).rearrange("p (i c) -> p i c", c=D)
        sgB = [lsegB] * NT
        Mm = sb("Mm", [128, 16, 128], BF); MmB = Buf("Mm")
        scm = sb("scm", [128, 4, 128]); scmB = Buf("scm")
        xdt = sb("xdt", [128, 16, 64], BF); xdtB = Buf("xdt")
        xw = sb("xw", [128, 16, 64], BF); xwB = Buf("xw")
        yv = sb("yv", [128, D]); yvB = Buf("yv")
        y2 = sb("y2", [128, D]); y2B = Buf("y2")
        s5t = [sb(f"s5t{i}", [128, N]) for i in range(8)]; s5tB = [Buf(f"s5t{i}") for i in range(8)]
        hS = sb("hS", [128, 2, 2, 4, N], BF); hSB = [Buf("hS0"), Buf("hS1")]

        class Th:
            pass
        TM, TF = Th(), Th()
        for nm, th, xt_, xtB_ in (("M", TM, xT, xTB), ("F", TF, hnT, hnTB)):
            th.name = nm
            th.xT, th.xTB = xt_, xtB_
            th.ss = sb("ss" + nm, [128, 8]); th.ssB = Buf("ss" + nm)
            th.xnb = [sb(f"xnb{nm}{i}", [128, D], BF) for i in range(2)]
            th.xnbB = [Buf(f"xnb{nm}{i}") for i in range(2)]
            th.nw = sb("nwr" + nm, [128, D]); th.nwB = Buf("nwr" + nm)
            th.ctr = [0]
            th.held = []

        TM.junk = y2[:].bitcast(BF)[:, 0:D]; TM.junkB = y2B
        TM.tmpf = y2[:]; TM.tmpfB = y2B
        TF.tmpf = hid[:].rearrange("p a b -> p (a b)").bitcast(F32)[:, 0:D]; TF.tmpfB = hidB
        TF.junk = hid[:].rearrange("p a b -> p (a b)")[:, 0:D]; TF.junkB = hidB
        import os as _os
        if _os.environ.get('KDEBUG'): print('SBUF remaining after allocs', nc.sbuf_bytes_remaining)
        V = nc.vector
        A = nc.scalar
        G = nc.gpsimd
        T = nc.tensor

        POOL.op(lambda: G.memset(halo[:], 0.0), [], haloB)
        POOL.op(lambda: G.memset(Hs[:], 0.0), [], HsB)
        POOL.op(lambda: G.memset(Hb[:], 0.0), [], HbB)
        DVE.op(lambda: V.memset(gst[:].rearrange("p a b c -> p (a b c)"), 0.0), [], gstB)

        wstate = {"next_load": 0, "next_use": 0, "sched": [], "free": [], "slot_of": {}}

        def pump():
            while wstate["free"] and wstate["next_load"] < len(wstate["sched"]):
                k = wstate["next_load"]
                sl = wstate["free"].pop(0)
                l, b = wstate["sched"][k]
                QW.dma(f"w{sl}", wring[sl][:], wblk_d.ap()[l, b], writes=[wringB[sl]])
                wstate["slot_of"][k] = sl
                wstate["next_load"] += 1

        def relall(th):
            if not Eng.DRY:
                for k in th.held:
                    wstate["free"].append(wstate["slot_of"][k])
                pump()
            th.held = []

        def getblk(l, b, th, keep=False):
            if not keep:
                relall(th)
            k = wstate["next_use"]
            wstate["next_use"] += 1
            if Eng.DRY:
                wstate["sched"].append((l, b))
                return wring[0], wringB[0]
            assert wstate["sched"][k] == (l, b), (wstate["sched"][k], l, b)
            pump()
            assert k < wstate["next_load"], "weight ring exhausted"
            th.held.append(k)
            sl = wstate["slot_of"][k]
            return wring[sl], wringB[sl]

        def load_nw(th, row):
            QW.dma("nw" + th.name, th.nw[:], nw_d.ap()[row].partition_broadcast(128), writes=[th.nwB])
            return th.nw, th.nwB

        def norm_stats(th, hb, hbB, i):
            ss, ssB = th.ss, th.ssB
            ACT.op(lambda: A.activation(th.junk, hb[:, i, :], AF.Square, accum_out=ss[:, i:i + 1]), [hbB[i]], [th.junkB, ssB])
            ACT.op(lambda: A.activation(ss[:, 4 + i:5 + i], ss[:, i:i + 1], AF.Ln, bias=epsc, scale=1.0 / D), [ssB, cfB], [ssB])
            ACT.op(lambda: A.activation(ss[:, 4 + i:5 + i], ss[:, 4 + i:5 + i], AF.Exp, scale=-0.5), [ssB], [ssB])

        def transposes_to(srcs, srcB, dst_ap3, dstB, nk):
            ps, pB = getps()
            psb = ps[:].bitcast(BF)
            for k in range(nk):
                PE.op(lambda: T.transpose(psb[:, k * 128:(k + 1) * 128], srcs[k], identb[:]),
                      [srcB, identbB], [pB], sig=(k == nk - 1))
            ACT.op(lambda: A.copy(dst_ap3, psb[:, 0:nk * 128].rearrange("p (k t) -> p k t", k=nk)), [pB], [dstB])

        def rmsnorm_T(th, hb, hbB, nt, row):
            wr, wrB = load_nw(th, row)
            for i in range(nt):
                norm_stats(th, hb, hbB, i)
                xb, xbB = th.xnb[th.ctr[0] % 2], th.xnbB[th.ctr[0] % 2]
                th.ctr[0] += 1
                ACT.op(lambda: A.activation(th.tmpf, hb[:, i, :], AF.Copy, scale=th.ss[:, 4 + i:5 + i]), [hbB[i], th.ssB], [th.tmpfB])
                POOL.op(lambda: G.tensor_tensor(xb[:], th.tmpf, wr[:], ALU.mult), [th.tmpfB, wrB], [xbB])
                transposes_to([xb[:, k * 128:(k + 1) * 128] for k in range(8)], xbB,
                              th.xT[:, :, i * 128:(i + 1) * 128], th.xTB, 8)

        def mm_tok(blk, blkB, i, kc_n, src, srcB):
            ps, pB = getps()
            bv = blk[:, 0:kc_n * 512].rearrange("p (kc c) -> p kc c", kc=kc_n)
            for kc in range(kc_n):
                PE.op(lambda: T.matmul(ps[:], src[:, kc, i * 128:(i + 1) * 128], bv[:, kc, :],
                                       start=(kc == 0), stop=(kc == kc_n - 1)),
                      [srcB, blkB], [pB], sig=(kc == kc_n - 1))
            return ps, pB

        def mm_feat(blk, blkB, f, n, src, srcB):
            ps, pB = getps()
            bv = blk[:].rearrange("p (kc c) -> p kc c", kc=8)
            for kc in range(8):
                PE.op(lambda: T.matmul(ps[:, 0:n], bv[:, kc, f * 128:(f + 1) * 128], src[:, kc, 0:n],
                                       start=(kc == 0), stop=(kc == 7)),
                      [srcB, blkB], [pB], sig=(kc == 7))
            return ps, pB

        cctr = [0]
        rctr = [0]
        tctr = [0]
        taps = {}
        hs0_tok = [None] * depth

        def tap(name, ap, bufs):
            if not debug_taps or Eng.DRY:
                return
            shape = list(ap.shape)
            d = nc.dram_tensor("tap_" + name, shape, ap.dtype, kind="ExternalOutput")
            QP.dma("tap", d.ap(), ap, reads=bufs)

        def mixer(l, nt, is_meta, hb, hbB, info):
            n = nt * 128
            sc = nt / 2.0
            if l == 0:
                if is_meta:
                    QW.dma("xin" + info["par"], hb[:, 0, :], meta_d.ap(), writes=[hbB[0]])
                else:
                    seq, m = info["seq"], info["m"]
                    QW.dma("xin" + info["par"], hb[:], x_d.ap()[seq, m * N:(m + 1) * N, :].rearrange("(i p) d -> p i d", p=128),
                           writes=hbB)
            if info["restore"]:
                POOL.op(lambda: G.tensor_copy(halo[:, l], halo0[:, l]), [halo0B[l]], [haloB[l]])
                POOL.op(lambda: G.tensor_copy(gst[:, l], gst0[:, l]), [gst0B[l]], [gstB[l]])
                if not Eng.DRY:
                    SP.wait(hs0_tok[l])
                QW.dma("hs0r", Hs[:, l].rearrange("p h e -> p (h e)"), hs0_d.ap()[l], writes=[HsB[l]])
                ACT.op(lambda: A.copy(Hb[:, l], Hs[:, l]), [HsB[l]], [HbB[l]])
            QW.dma("snw", snw[:], snw_d.ap()[l].partition_broadcast(128), writes=[snwB])
            rmsnorm_T(TM, hb, hbB, nt, 2 * l)
            yield 6 * sc
            for cb in range(2):
                blk, bB = getblk(l, cb, TM)
                for i in range(nt):
                    ps, pB = mm_tok(blk, bB, i, 8, xT, xTB)
                    ACT.op(lambda: A.activation(zs[:, i, cb * 512:(cb + 1) * 512], ps[:], AF.Silu), [pB], [zsB[i]])
                    yield 1.8
            prev = None
            for cb in range(4):
                blk, bB = getblk(l, 2 + cb, TM)
                for f in range(4):
                    ft = cb * 4 + f
                    ps, pB = mm_feat(blk, bB, f, n, xT, xTB)
                    c = cctr[0] % 2
                    cctr[0] += 1
                    cw = convp[:, l, ft, :]
                    ACT.op(lambda: A.copy(xr[c][:, 3:3 + n], ps[:, 0:n]), [pB], [xrB[c]])
                    ACT.op(lambda: A.copy(xr[c][:, 0:3], halo[:, l, ft, :]), [haloB[l]], [xrB[c]])
                    ACT.op(lambda: A.activation(acc[c][:, 0:n], ps[:, 0:n], AF.Identity, bias=cw[:, 4:5], scale=cw[:, 3:4]),
                           [pB, convpB], [accB[c]])
                    ACT.op(lambda: A.copy(halo[:, l, ft, :], ps[:, n - 3:n]), [pB], [haloB[l]])
                    if prev is not None:
                        pft, pc = prev
                        ACT.op(lambda: A.activation(xc[:, pft, 0:n], acc[pc][:, 0:n], AF.Silu), [accB[pc]], [xcB])
                    for k in range(3):
                        DVE.op(lambda: V.scalar_tensor_tensor(acc[c][:, 0:n], xr[c][:, k:k + n], cw[:, k:k + 1], acc[c][:, 0:n],
                                                              ALU.mult, ALU.add), [xrB[c], accB[c], convpB], [accB[c]])
                    prev = (ft, c)
                    yield 1.5 * sc
            pft, pc = prev
            ACT.op(lambda: A.activation(xc[:, pft, 0:n], acc[pc][:, 0:n], AF.Silu), [accB[pc]], [xcB])
            blk, bB = getblk(l, 6, TM)
            for ct in range(4):
                ps, pB = mm_feat(blk, bB, ct, n, xT, xTB)
                ACT.op(lambda: A.copy(uTf[:, ct, 0:n], ps[:, 0:n]), [pB], [uTfB])
                ACT.op(lambda: A.copy(uTb[:, ct, 0:n], ps[:, 0:n]), [pB], [uTbB])
                yield 1.2 * sc
            for cb in range(4):
                blk, bB = getblk(l, 7 + cb, TM)
                for i in range(nt):
                    ps, pB = mm_tok(blk, bB, i, 8, xT, xTB)
                    ACT.op(lambda: A.activation(gts[:, i, cb * 512:(cb + 1) * 512], ps[:], AF.Sigmoid), [pB], [gtsB[i]])
                    yield 1.8
            for i in range(nt):
                ps, pB = getps()
                for kc in range(8):
                    PE.op(lambda: T.matmul(ps[:, 0:16], xT[:, kc, i * 128:(i + 1) * 128], dtw[:, l, kc, :],
                                           start=(kc == 0), stop=(kc == 7)), [xTB, dtwB], [pB], sig=(kc == 7))
                d0 = sm[:, 0, :]; d1 = sm[:, 1, :]
                DVE.op(lambda: V.tensor_tensor(d0, ps[:, 0:16], ssdp[:, l, 0, :], ALU.add), [pB, ssdpB], [smB])
                ACT.op(lambda: A.activation(d1, d0, AF.Abs), [smB], [smB])
                ACT.op(lambda: A.activation(d1, d1, AF.Exp, scale=-1.0), [smB], [smB])
                ACT.op(lambda: A.activation(d1, d1, AF.Ln, bias=1.0), [smB], [smB])
                DVE.op(lambda: V.tensor_scalar(d0, d0, 0.0, None, ALU.max), [smB], [smB])
                if is_meta:
                    DVE.op(lambda: V.tensor_tensor(d0, d0, d1, ALU.add), [smB], [smB])
                    DVE.op(lambda: V.tensor_scalar(dts[:, i, :], d0, padmask, None, ALU.mult), [smB, cfB], [dtsB[i]])
                else:
                    DVE.op(lambda: V.tensor_tensor(dts[:, i, :], d0, d1, ALU.add), [smB], [dtsB[i]])
                yield 1.5
            for i in range(nt):
                transposes_to([xc[:, k, i * 128:(i + 1) * 128] for k in range(8)], xcB,
                              xtok[:, i, :].rearrange("p (k t) -> p k t", k=8), xtokB[i], 8)
                transposes_to([xc[:, 8 + k, i * 128:(i + 1) * 128] for k in range(4)], xcB,
                              btok[:, i, :].rearrange("p (k t) -> p k t", k=4), btokB[i], 4)
                yield 2.0
            BLk, BLkB = getblk(l, 33, TM)
            CLk, CLkB = getblk(l, 34, TM, keep=True)
            BLv = BLk[:].rearrange("p (a b) -> p a b", a=32)
            CLv = CLk[:].rearrange("p (a b) -> p a b", a=32)
            def load_tab(j):
                i = tctr[0] % 2
                tctr[0] += 1
                QW.dma(f"tab{i}", tring[i][:, :, 0:n], tabs_d.ap()[l, j][:, :, 0:n], writes=[tringB[i]])
                return tring[i], tringB[i]
            t = [x[:, 0:n] for x in s5t]
            tB = s5tB
            TT = V.tensor_tensor
            def cproj(ct):
                psy, pyB = getps()
                idx = 0
                for qq in range(4):
                    for part in range(2):
                        jj = ct * 4 + qq
                        PE.op(lambda: T.matmul(psy[:, 0:n], CLv[:, 2 * jj + part, :], hS[:, ct % 2, part, qq, 0:n],
                                               start=(idx == 0), stop=(idx == 7)), [CLkB, hSB[ct % 2]], [pyB], sig=(idx == 7))
                        idx += 1
                a0, a1, a2 = t[0], t[1], t[2]
                DVE.op(lambda: V.scalar_tensor_tensor(a0, uTf[:, ct, 0:n], s5dd[:, l, ct:ct + 1], psy[:, 0:n], ALU.mult, ALU.add),
                       [uTfB, s5ddB, pyB], [tB[0]])
                DVE.op(lambda: TT(a1, a0, a0, ALU.mult), [tB[0]], [tB[1]])
                DVE.op(lambda: V.tensor_scalar(a1, a1, 0.044715, 1.0, ALU.mult, ALU.add), [tB[1]], [tB[1]])
                DVE.op(lambda: TT(a1, a1, a0, ALU.mult), [tB[0], tB[1]], [tB[1]])
                ACT.op(lambda: A.activation(a2, a1, AF.Sigmoid, scale=1.5957691216057308), [tB[1]], [tB[2]])
                DVE.op(lambda: TT(ybT[:, ct, 0:n], a0, a2, ALU.mult), [tB[0], tB[2]], [ybTB])
            tabq = [load_tab(0)]
            pending = []
            for j in range(16):
                ct, q = j // 4, j % 4
                tb, tbB = tabq.pop(0)
                if j + 1 < 16:
                    tabq.append(load_tab(j + 1))
                Ec, Es = tb[:, 0, 0:n], tb[:, 1, 0:n]
                psr, prB = getps()
                PE.op(lambda: T.matmul(psr[:, 0:n], BLv[:, 2 * j, :], uTb[:, ct, 0:n], start=True, stop=True), [BLkB, uTbB], [prB])
                psi, piB = getps()
                PE.op(lambda: T.matmul(psi[:, 0:n], BLv[:, 2 * j + 1, :], uTb[:, ct, 0:n], start=True, stop=True), [BLkB, uTbB], [piB])
                DVE.op(lambda: TT(t[0], psr[:, 0:n], Ec, ALU.mult), [prB, tbB], [tB[0]])
                DVE.op(lambda: TT(t[1], psi[:, 0:n], Es, ALU.mult), [piB, tbB], [tB[1]])
                DVE.op(lambda: TT(t[0], t[0], t[1], ALU.add), [tB[0], tB[1]], [tB[0]])
                DVE.op(lambda: TT(t[2], psi[:, 0:n], Ec, ALU.mult), [piB, tbB], [tB[2]])
                DVE.op(lambda: TT(t[3], psr[:, 0:n], Es, ALU.mult), [prB, tbB], [tB[3]])
                DVE.op(lambda: TT(t[2], t[2], t[3], ALU.subtract), [tB[2], tB[3]], [tB[2]])
                rj = rtab[:, l, j:j + 1].broadcast_to([128, n])
                DVE.op(lambda: V.tensor_tensor_scan(t[4], rj, t[0], gst[:, l, j, 0:1], ALU.mult, ALU.add),
                       [rtabB, tB[0], gstB[l]], [tB[4]])
                DVE.op(lambda: V.tensor_tensor_scan(t[5], rj, t[2], gst[:, l, j, 1:2], ALU.mult, ALU.add),
                       [rtabB, tB[2], gstB[l]], [tB[5]])
                DVE.op(lambda: TT(t[0], t[4], Ec, ALU.mult), [tB[4], tbB], [tB[0]])
                DVE.op(lambda: TT(t[1], t[5], Es, ALU.mult), [tB[5], tbB], [tB[1]])
                DVE.op(lambda: TT(t[2], t[5], Ec, ALU.mult), [tB[5], tbB], [tB[2]])
                DVE.op(lambda: TT(t[3], t[4], Es, ALU.mult), [tB[4], tbB], [tB[3]])
                yield 5.0 * sc
                DVE.op(lambda: TT(hS[:, ct % 2, 0, q, 0:n], t[0], t[1], ALU.subtract), [tB[0], tB[1]], [hSB[ct % 2]])
                DVE.op(lambda: TT(hS[:, ct % 2, 1, q, 0:n], t[2], t[3], ALU.add), [tB[2], tB[3]], [hSB[ct % 2]])
                DVE.op(lambda: TT(gst[:, l, j, 0:1], t[0][:, n - 1:n], t[1][:, n - 1:n], ALU.subtract), [tB[0], tB[1]], [gstB[l]])
                DVE.op(lambda: TT(gst[:, l, j, 1:2], t[2][:, n - 1:n], t[3][:, n - 1:n], ALU.add), [tB[2], tB[3]], [gstB[l]])
                if pending and q == 2:
                    cproj(pending.pop(0))
                    yield 2.0 * sc
                if q == 3:
                    pending.append(ct)
            while pending:
                cproj(pending.pop(0))
                yield 2.0 * sc
            for b in range(4):
                blk, bB = getblk(l, 11 + b, TM)
                for i in range(nt):
                    ps, pB = mm_tok(blk, bB, i, 4, ybT, ybTB)
                    if b < 2:
                        ACT.op(lambda: A.activation(sg[:, i, b * 512:(b + 1) * 512], ps[:], AF.Sigmoid), [pB], [sgB[i]])
                    else:
                        c0 = (b - 2) * 512
                        DVE.op(lambda: V.tensor_tensor(yb[:, i, c0:c0 + 512], ps[:], sg[:, i, c0:c0 + 512], ALU.mult),
                               [pB, sgB[i]], [ybB[i]])
                    yield 1.0
            for i in range(nt):
                dA = sm[:, 2, :]
                DVE.op(lambda: V.tensor_tensor(dA, dts[:, i, :], arep[:, l, :], ALU.mult), [dtsB[i], arepB], [smB])
                DVE.op(lambda: V.tensor_tensor(lseg[:], strict.unsqueeze(1).broadcast_to([128, 16, 128]),
                                               dA.unsqueeze(2).broadcast_to([128, 16, 128]), ALU.mult), [cfB, smB], [lsegB])
                psc, pcB = getps()
                PE.op(lambda: T.matmul(psc[:, 0:16], tri, dA, start=True, stop=True), [cfB, smB], [pcB])
                PE.op(lambda: T.matmul(psc[:, 16:32], ones, dA, start=True, stop=True), [cfB, smB], [pcB])
                cs = sm[:, 3, :]; ecs = sm[:, 4, :]; wv = sm[:, 5, :]; etot = sm[:, 6, :]
                ACT.op(lambda: A.copy(cs, psc[:, 0:16]), [pcB], [smB])
                ACT.op(lambda: A.activation(etot, psc[:, 16:32], AF.Exp), [pcB], [smB])
                DVE.op(lambda: V.tensor_tensor(wv, psc[:, 16:32], cs, ALU.subtract), [pcB, smB], [smB])
                ACT.op(lambda: A.activation(wv, wv, AF.Exp), [smB], [smB])
                ACT.op(lambda: A.activation(ecs, cs, AF.Exp), [smB], [smB])
                DVE.op(lambda: V.tensor_tensor(wv, wv, dts[:, i, :], ALU.mult), [smB, dtsB[i]], [smB])
                yield 4.0
                for g in range(4):
                    ps, pB = getps()
                    for r in range(4):
                        hh = g * 4 + r
                        PE.op(lambda: T.matmul(ps[:, r * 128:(r + 1) * 128], lseg[:, hh, :], tri, start=True, stop=True),
                              [lsegB, cfB], [pB], sig=(r == 3))
                    ACT.op(lambda: A.activation(Lm[:, g * 4:(g + 1) * 4, :], ps[:].rearrange("p (r t) -> p r t", r=4), AF.Exp),
                           [pB], [LmB])
                ps, pB = getps()
                for g in range(4):
                    PE.op(lambda: T.matmul(ps[:, g * 128:(g + 1) * 128], xc[:, 8 + g, i * 128:(i + 1) * 128],
                                           xc[:, 12 + g, i * 128:(i + 1) * 128], start=True, stop=True), [xcB], [pB], sig=(g == 3))
                DVE.op(lambda: V.tensor_tensor(scm[:], ps[:].rearrange("p (g t) -> p g t", g=4),
                                               tri.unsqueeze(1).broadcast_to([128, 4, 128]), ALU.mult), [pB, cfB], [scmB])
                yield 3.0
                DVE.op(lambda: V.tensor_tensor(Mm[:].rearrange("p (g r) t -> p g r t", g=4),
                                               Lm[:].rearrange("p (g r) t -> p g r t", g=4),
                                               scm[:].unsqueeze(2).broadcast_to([128, 4, 4, 128]), ALU.mult), [LmB, scmB], [MmB])
                x3 = xtok[:, i, :].rearrange("p (h e) -> p h e", h=16)
                DVE.op(lambda: V.tensor_tensor(xdt[:], x3, dts[:, i, :].unsqueeze(2).broadcast_to([128, 16, 64]), ALU.mult),
                       [xtokB[i], dtsB[i]], [xdtB])
                DVE.op(lambda: V.tensor_tensor(xw[:], x3, wv.unsqueeze(2).broadcast_to([128, 16, 64]), ALU.mult),
                       [xtokB[i], smB], [xwB])
                yield 5.0
                pyd = [getps() for _ in range(2)]
                for hh in range(16):
                    ps, pB = pyd[hh // 8]
                    PE.op(lambda: T.matmul(ps[:, (hh % 8) * 64:(hh % 8 + 1) * 64], Mm[:, hh, :], xdt[:, hh, :], start=True, stop=True),
                          [MmB, xdtB], [pB], sig=(hh % 8 == 7))
                pyo = [getps() for _ in range(2)]
                for g in range(4):
                    ps, pB = pyo[g // 2]
                    PE.op(lambda: T.matmul(ps[:, (g % 2) * 256:(g % 2 + 1) * 256], xc[:, 12 + g, i * 128:(i + 1) * 128],
                                           Hb[:, l, g * 4:(g + 1) * 4, :].rearrange("p r e -> p (r e)"), start=True, stop=True),
                          [xcB, HbB[l]], [pB], sig=(g % 2 == 1))
                for half in range(2):
                    sl = slice(half * 512, (half + 1) * 512)
                    hsl = slice(half * 8, (half + 1) * 8)
                    DVE.op(lambda: V.tensor_tensor(yv[:, sl].rearrange("p (h e) -> p h e", h=8),
                                                   pyo[half][0][:].rearrange("p (h e) -> p h e", h=8),
                                                   ecs[:, hsl].unsqueeze(2).broadcast_to([128, 8, 64]), ALU.mult),
                           [pyo[half][1], smB], [yvB])
                    DVE.op(lambda: V.tensor_tensor(yv[:, sl], yv[:, sl], pyd[half][0][:], ALU.add), [yvB, pyd[half][1]], [yvB])
                yield 3.0
                pss = [getps() for _ in range(2)]
                for g in range(4):
                    ps, pB = pss[g // 2]
                    PE.op(lambda: T.matmul(ps[:, (g % 2) * 256:(g % 2 + 1) * 256], btok[:, i, g * 128:(g + 1) * 128],
                                           xw[:, g * 4:(g + 1) * 4, :].rearrange("p r e -> p (r e)"), start=True, stop=True),
                          [btokB[i], xwB], [pB], sig=(g % 2 == 1))
                Hfl = Hs[:, l].rearrange("p h e -> p (h e)")
                DVE.op(lambda: V.tensor_tensor(Hs[:, l], Hs[:, l], etot.unsqueeze(2).broadcast_to([128, 16, 64]), ALU.mult),
                       [HsB[l], smB], [HsB[l]])
                for half in range(2):
                    sl = slice(half * 512, (half + 1) * 512)
                    DVE.op(lambda: V.tensor_tensor(Hfl[:, sl], Hfl[:, sl], pss[half][0][:], ALU.add),
                           [HsB[l], pss[half][1]], [HsB[l]])
                ACT.op(lambda: A.copy(Hb[:, l], Hs[:, l]), [HsB[l]], [HbB[l]])
                yield 3.0
                DVE.op(lambda: V.tensor_tensor(y2[:].rearrange("p (h e) -> p h e", h=16), x3,
                                               ssdp[:, l, 2, :].unsqueeze(2).broadcast_to([128, 16, 64]), ALU.mult),
                       [xtokB[i], ssdpB], [y2B])
                DVE.op(lambda: V.tensor_tensor(yv[:], yv[:], y2[:], ALU.add), [yvB, y2B], [yvB])
                DVE.op(lambda: V.tensor_tensor(yv[:], yv[:], zs[:, i, :], ALU.mult), [yvB, zsB[i]], [yvB])
                gss = sm[:, 7, 0:4]; grs = sm[:, 7, 4:8]
                for g in range(4):
                    ACT.op(lambda: A.activation(y2[:, g * 256:(g + 1) * 256], yv[:, g * 256:(g + 1) * 256], AF.Square,
                                                accum_out=gss[:, g:g + 1]), [yvB, smB], [y2B, smB])
                DVE.op(lambda: V.tensor_scalar(grs, gss, 1.0 / 256, EPS, ALU.mult, ALU.add), [smB], [smB])
                ACT.op(lambda: A.activation(grs, grs, AF.Sqrt), [smB], [smB])
                DVE.op(lambda: V.reciprocal(grs, grs), [smB], [smB])
                yield 5.0
                DVE.op(lambda: V.tensor_tensor(yv[:].rearrange("p (g e) -> p g e", g=4), yv[:].rearrange("p (g e) -> p g e", g=4),
                                               grs.unsqueeze(2).broadcast_to([128, 4, 256]), ALU.mult), [yvB, smB], [yvB])
                DVE.op(lambda: V.tensor_tensor(yv[:], yv[:], snw[:], ALU.mult), [yvB, snwB], [yvB])
                DVE.op(lambda: V.tensor_tensor(yv[:], yv[:], gts[:, i, 0:D], ALU.mult), [yvB, gtsB[i]], [yvB])
                DVE.op(lambda: V.tensor_tensor(y2[:], yb[:, i, :], gts[:, i, D:2 * D], ALU.mult), [ybB[i], gtsB[i]], [y2B])
                mixb, mixbB = TM.xnb[TM.ctr[0] % 2], TM.xnbB[TM.ctr[0] % 2]
                TM.ctr[0] += 1
                DVE.op(lambda: V.tensor_tensor(mixb[:], yv[:], y2[:], ALU.add), [yvB, y2B], [mixbB])
                transposes_to([mixb[:, k * 128:(k + 1) * 128] for k in range(8)], mixbB,
                              xT[:, :, i * 128:(i + 1) * 128], xTB, 8)
                yield 6.0
            for cb in range(2):
                blk, bB = getblk(l, 15 + cb, TM)
                for i in range(nt):
                    ps, pB = mm_tok(blk, bB, i, 8, xT, xTB)
                    DVE.op(lambda: V.tensor_tensor(hb[:, i, cb * 512:(cb + 1) * 512], hb[:, i, cb * 512:(cb + 1) * 512], ps[:], ALU.add),
                           [hbB[i], pB], [hbB[i]])
                    yield 1.8
            relall(TM)
            if info["save"]:
                POOL.op(lambda: G.tensor_copy(halo0[:, l], halo[:, l]), [haloB[l]], [halo0B[l]])
                POOL.op(lambda: G.tensor_copy(gst0[:, l], gst[:, l]), [gstB[l]], [gst0B[l]])
                tk = QW.dma("hs0w", hs0_d.ap()[l], Hs[:, l].rearrange("p h e -> p (h e)"), reads=[HsB[l]])
                if not Eng.DRY:
                    hs0_tok[l] = tk

        def ffn(l, nt, is_meta, hb, hbB, info):
            n = nt * 128
            sc = nt / 2.0
            rmsnorm_T(TF, hb, hbB, nt, 2 * l + 1)
            yield 6 * sc * FW
            for kg in range(4):
                for c in range(2):
                    blk, bB = getblk(l, 17 + kg * 4 + c, TF)
                    for f in range(4):
                        ps, pB = mm_feat(blk, bB, f, n, hnT, hnTB)
                        rr = rctr[0] % 2
                        rctr[0] += 1
                        ACT.op(lambda: A.activation(rl[rr][:, 0:n], ps[:, 0:n], AF.Relu), [pB], [rlB[rr]])
                        POOL.op(lambda: G.tensor_tensor(hid[:, (kg % 2) * 8 + c * 4 + f, 0:n], rl[rr][:, 0:n], rl[rr][:, 0:n], ALU.mult),
                                [rlB[rr]], [hidB])
                        yield 1.2 * sc * FW
                for c in range(2):
                    blk, bB = getblk(l, 17 + kg * 4 + 2 + c, TF)
                    for i in range(nt):
                        ps, pB = getps()
                        bv = blk[:].rearrange("p (kc c) -> p kc c", kc=8)
                        for kc in range(8):
                            PE.op(lambda: T.matmul(ps[:], hid[:, (kg % 2) * 8 + kc, i * 128:(i + 1) * 128], bv[:, kc, :],
                                                   start=(kc == 0), stop=(kc == 7)), [hidB, bB], [pB], sig=(kc == 7))
                        DVE.op(lambda: V.tensor_tensor(hb[:, i, c * 512:(c + 1) * 512], hb[:, i, c * 512:(c + 1) * 512], ps[:], ALU.add),
                               [hbB[i], pB], [hbB[i]])
                        yield 1.8 * FW
            relall(TF)
            if is_meta:
                DVE.op(lambda: V.tensor_scalar(hb[:, 0, :], hb[:, 0, :], padmask, None, ALU.mult), [hbB[0], cfB], [hbB[0]])
            tg = ("m" if is_meta else "s") + str(l)
            if tg not in taps:
                taps[tg] = 1
                tap(tg + "_h", hb[:, 0:nt, :], hbB)
            if l == depth - 1 and not is_meta:
                seq, m = info["seq"], info["m"]
                wr, wrB = load_nw(TF, 2 * depth)
                for i in range(NT):
                    norm_stats(TF, hb, hbB, i)
                for i in range(NT):
                    DVE.op(lambda: V.scalar_tensor_tensor(obuf[:, i, :], hb[:, i, :], TF.ss[:, 4 + i:5 + i], wr[:], ALU.mult, ALU.mult),
                           [hbB[i], TF.ssB, wrB], [hidB])
                QW.dma("oout", out_d.ap()[seq, m * N:(m + 1) * N, :].rearrange("(i p) d -> p i d", p=128), obuf[:],
                       reads=[hidB])
                yield 4.0 * FW

        FW = 3.0

        items = [("meta", 0, 0)] + [("seq", s_, m_) for s_ in range(NSEQ) for m_ in range(NMT)]

        def emit_all():
            for cnt in (pctr, cctr, rctr, tctr, TM.ctr, TF.ctr):
                cnt[0] = 0
            wstate["next_use"] = 0
            wstate["next_load"] = 0
            wstate["free"] = list(range(NW))
            wstate["slot_of"] = {}
            TM.held, TF.held = [], []
            taps.clear()
            Mx, Fx = [], []
            for p0 in range(0, len(items), 2):
                pair = [(k, items[k]) for k in range(p0, min(p0 + 2, len(items)))]
                for l in range(depth):
                    for slot in range(2):
                        if slot < len(pair):
                            k, (kind, seq, m) = pair[slot]
                            is_meta = kind == "meta"
                            nt = 1 if is_meta else NT
                            info = {"seq": seq, "m": m, "par": str(k % 2), "save": is_meta,
                                    "restore": (kind == "seq" and m == 0 and seq > 0)}
                            Mx.append((mixer, (l, nt, is_meta, hbuf[k % 2], hbufB[k % 2], info)))
                            Fx.append((ffn, (l, nt, is_meta, hbuf[k % 2], hbufB[k % 2], info)))
                        else:
                            Mx.append(None)
                            Fx.append(None)
            nsteps = len(Mx) + 1
            for st in range(nsteps):
                gens = []
                if st < len(Mx) and Mx[st] is not None:
                    gens.append(Mx[st][0](*Mx[st][1]))
                if st >= 1 and Fx[st - 1] is not None:
                    gens.append(Fx[st - 1][0](*Fx[st - 1][1]))
                tacc = [0.0] * len(gens)
                alive = list(range(len(gens)))
                while alive:
                    gi = min(alive, key=lambda a: tacc[a])
                    try:
                        tacc[gi] += next(gens[gi])
                    except StopIteration:
                        alive.remove(gi)

        Eng.DRY = True
        wstate["sched"] = []
        emit_all()
        Eng.DRY = False
        emit_all()
        assert wstate["next_use"] == len(wstate["sched"])

        for s in QP.sems.values():
            POOL.e.wait_ge(s[0], s[1])
        for s in QW.sems.values():
            SP.e.wait_ge(s[0], s[1])
        for E in (PE, ACT, DVE, POOL):
            for E2 in (PE, ACT, DVE, POOL):
                if E2.cnt > 0 and E2.last_sig:
                    E.e.wait_ge(E2.sem, E2.cnt)
    return nc


def host_prep(inputs, depth=DEPTH):
    f = lambda a: np.ascontiguousarray(np.asarray(a, dtype=np.float32))
    p = {}
    meta = f(inputs["meta_tokens"])
    mp = np.zeros((128, D), np.float32)
    mp[128 - NMETA:] = meta
    p["meta_pad"] = mp
    for k in ("w_in", "w_glu", "w_out", "w_ff_in", "w_ff_out"):
        p[k] = f(inputs[k])
    rows = []
    for l in range(depth):
        rows += [inputs["norm_mix_w"][l], inputs["norm_mlp_w"][l]]
    rows.append(inputs["final_norm_w"])
    p["nw"] = f(np.stack([np.asarray(r) for r in rows]))
    p["snw"] = f(inputs["ssd_norm_w"])
    cw = np.concatenate([np.asarray(inputs["conv_w"]), np.asarray(inputs["conv_b"])[:, None, :]], axis=1)
    p["convp"] = f(cw.reshape(depth, 5, 16, 128).transpose(0, 3, 2, 1))
    p["ssdp"] = f(np.stack([np.asarray(inputs["dt_bias"]), np.asarray(inputs["ssd_a_log"]), np.asarray(inputs["ssd_d"])], axis=1))
    def st(a):
        return np.asarray(a).reshape(depth, 16, 2, 64).transpose(0, 2, 3, 1).reshape(depth, 128, 16)
    ls = np.broadcast_to(np.asarray(inputs["s5_log_step"])[:, :, None], (depth, 32, 64))
    p["s5s"] = f(np.stack([st(inputs["s5_a_re"]), st(inputs["s5_a_im"]), st(ls)], axis=1))
    def stb(a):
        return np.asarray(a).reshape(depth, 16, 2, 64, 16).transpose(0, 2, 3, 1, 4).reshape(depth, 128, 16, 16)
    p["s5b"] = f(np.stack([stb(inputs["s5_b_re"]), stb(inputs["s5_b_im"])], axis=1))
    cre = np.asarray(inputs["s5_c_re"]).transpose(0, 1, 3, 2)
    cim = np.asarray(inputs["s5_c_im"]).transpose(0, 1, 3, 2)
    p["s5c"] = f(np.stack([stb(cre), stb(cim)], axis=1))
    p["s5d"] = f(np.asarray(inputs["s5_d"]).reshape(depth, 4, 128).transpose(0, 2, 1))
    k = np.arange(128)
    cf = np.zeros((128, 4 * 128 + 512 + 2), np.float32)
    cf[:, 0:128] = np.eye(128)
    cf[:, 128:256] = (k[:, None] <= k[None, :])
    cf[:, 256:384] = (k[:, None] > k[None, :])
    cf[:, 384:512] = 1.0
    cf[:, 512:1024] = np.arange(1, 513)[None, :]
    cf[:, 1024] = (k >= 128 - NMETA)
    cf[:, 1025] = EPS
    p["cf"] = cf
    p["cb"] = np.eye(128).astype(ml_dtypes.bfloat16)
    return p


_CACHE = {}


def kernel(**inputs):
    x = np.asarray(inputs["x"], dtype=np.float32)
    B, S, _ = x.shape
    ncores = 8
    nseq = B // ncores
    NT = 2
    nmt = S // (NT * 128)
    key = (nseq, nmt)
    if key not in _CACHE:
        _CACHE[key] = build_program(nseq, nmt, NT)
    nc = _CACHE[key]
    p = host_prep(inputs)
    in_maps = []
    for c in range(ncores):
        m = dict(p)
        m["x"] = np.ascontiguousarray(x[c * nseq:(c + 1) * nseq])
        in_maps.append(m)
    res = run_bass_kernel_spmd(nc, in_maps, core_ids=list(range(ncores)))
    out = np.concatenate([np.asarray(r["out"]) for r in res.results], axis=0)
    return out.astype(np.float32)
```

```python
from contextlib import ExitStack
import numpy as np
import ml_dtypes
import concourse.bass as bass
import concourse.mybir as mybir
from concourse.bass_utils import run_bass_kernel_spmd

F32 = mybir.dt.float32
BF = mybir.dt.bfloat16
ALU = mybir.AluOpType
AF = mybir.ActivationFunctionType

D = 1024
NMETA = 16
DEPTH = 2
EPS = 1e-6
NBLK = 35
TWO_PI = 6.283185307179586
PI = 3.141592653589793


class Tok:
    __slots__ = ("sem", "val", "eng")

    def __init__(self, sem, val, eng):
        self.sem, self.val, self.eng = sem, val, eng


class Buf:
    def __init__(self, name):
        self.name = name
        self.w = None
        self.r = []


class Eng:
    DRY = False

    def __init__(self, nc, es, e, name):
        self.nc, self.es, self.e, self.name = nc, es, e, name
        self.k = 0
        self._new_sem()
        self.seen = {}
        self.last_sig = True

    def _new_sem(self):
        self.sem = self.es.enter_context(self.nc.semaphore(f"{self.name}_s{self.k}"))
        self.k += 1
        self.cnt = 0

    def wait(self, tok):
        if tok is None:
            return
        key = id(tok.sem)
        if self.seen.get(key, 0) >= tok.val:
            return
        if tok.eng is self and self.name == "pe":
            return
        if tok.eng is not None:
            assert tok.sem is not tok.eng.sem or tok.val <= tok.eng.cnt, (self.name, tok.eng.name)
        self.e.wait_ge(tok.sem, tok.val)
        self.seen[key] = tok.val

    def op(self, fn, reads=(), writes=(), sig=True):
        if Eng.DRY:
            return None
        for b in reads:
            self.wait(b.w)
        for b in writes:
            self.wait(b.w)
            for t in b.r:
                self.wait(t)
        if sig and self.last_sig and self.cnt >= 30000:
            self._new_sem()
        inst = fn()
        if sig:
            self.cnt += 1
            inst.then_inc(self.sem, 1)
            tok = Tok(self.sem, self.cnt, self)
            self.last_sig = True
        else:
            tok = Tok(self.sem, self.cnt + 1, self)
            self.last_sig = False
        for b in reads:
            b.r = [t for t in b.r if t.sem is not tok.sem] + [tok]
        for b in writes:
            b.w = tok
            b.r = []
        return tok


class DmaQ:
    def __init__(self, nc, es, eng):
        self.nc, self.es, self.eng = nc, es, eng
        self.sems = {}

    def dma(self, slot, out, in_, reads=(), writes=()):
        if Eng.DRY:
            return None
        E = self.eng
        for b in reads:
            E.wait(b.w)
        for b in writes:
            E.wait(b.w)
            for t in b.r:
                E.wait(t)
        if slot not in self.sems:
            self.sems[slot] = [self.es.enter_context(self.nc.semaphore(f"dq_{slot}")), 0]
        s = self.sems[slot]
        s[1] += 16
        E.e.dma_start(out=out, in_=in_).then_inc(s[0], 16)
        tok = Tok(s[0], s[1], None)
        for b in reads:
            b.r = [t for t in b.r if t.sem is not tok.sem] + [tok]
        for b in writes:
            b.w = tok
            b.r = []
        return tok


def build_program(NSEQ, NMT, NT=4, depth=DEPTH, debug_taps=False):
    N = NT * 128
    S = NMT * N
    nc = bass.Bass("TRN2", target_bir_lowering=False)
    dt_in = lambda name, shape, dt=F32: nc.dram_tensor(name, shape, dt, kind="ExternalInput")
    x_d = dt_in("x", [NSEQ, S, D])
    meta_d = dt_in("meta_pad", [128, D])
    w_in_d = dt_in("w_in", [depth, D, 5648])
    w_glu_d = dt_in("w_glu", [depth, 512, 2048])
    w_out_d = dt_in("w_out", [depth, D, D])
    w_ffi_d = dt_in("w_ff_in", [depth, D, 4096])
    w_ffo_d = dt_in("w_ff_out", [depth, 4096, D])
    nw_d = dt_in("nw", [2 * depth + 1, D])
    snw_d = dt_in("snw", [depth, D])
    convp_d = dt_in("convp", [depth, 128, 16, 5])
    ssdp_d = dt_in("ssdp", [depth, 3, 16])
    s5s_d = dt_in("s5s", [depth, 3, 128, 16])
    s5b_d = dt_in("s5b", [depth, 2, 128, 16, 16])
    s5c_d = dt_in("s5c", [depth, 2, 128, 16, 16])
    s5d_d = dt_in("s5d", [depth, 128, 4])
    cf_d = dt_in("cf", [128, 4 * 128 + 512 + 2])
    cb_d = dt_in("cb", [128, 128], BF)
    out_d = nc.dram_tensor("out", [NSEQ, S, D], F32, kind="ExternalOutput")
    wblk_d = nc.dram_tensor("wblk", [depth, NBLK, 128, 4096], BF)
    tabs_d = nc.dram_tensor("tabs", [depth, 16, 128, 2, 512], F32)
    hs0_d = nc.dram_tensor("hs0", [depth, 128, 1024], F32)

    es = ExitStack()
    with es:
        sb = lambda name, shape, dt=F32: es.enter_context(nc.sbuf_tensor("s_" + name, shape, dt))
        PE = Eng(nc, es, nc.tensor, "pe")
        ACT = Eng(nc, es, nc.scalar, "act")
        DVE = Eng(nc, es, nc.vector, "dve")
        POOL = Eng(nc, es, nc.gpsimd, "pool")
        SP = Eng(nc, es, nc.sync, "sp")
        QW = DmaQ(nc, es, SP)
        QP = DmaQ(nc, es, POOL)

        cf = sb("cf", [128, 4 * 128 + 512 + 2]); cfB = Buf("cf")
        identb = sb("identb", [128, 128], BF); identbB = Buf("identb")
        QP.dma("c0", cf[:], cf_d.ap(), writes=[cfB])
        QP.dma("c1", identb[:], cb_d.ap(), writes=[identbB])
        identf = cf[:, 0:128]
        tri = cf[:, 128:256]
        strict = cf[:, 256:384]
        ones = cf[:, 384:512]
        tpos = cf[:, 512:1024]
        padmask = cf[:, 1024:1025]
        epsc = cf[:, 1025:1026]

        convp = sb("convp", [128, depth, 16, 5]); convpB = Buf("convp")
        ssdp = sb("ssdp", [128, depth, 3, 16]); ssdpB = Buf("ssdp")
        s5dd = sb("s5dd", [128, depth, 4]); s5ddB = Buf("s5dd")
        dtw = sb("dtw", [128, depth, 8, 16], BF); dtwB = Buf("dtw")
        for l in range(depth):
            QP.dma("c2", convp[:, l], convp_d.ap()[l], writes=[convpB])
            QP.dma("c3", ssdp[:, l].rearrange("p a b -> p (a b)"),
                   ssdp_d.ap()[l].rearrange("a b -> (a b)").partition_broadcast(128), writes=[ssdpB])
            QP.dma("c4", s5dd[:, l], s5d_d.ap()[l], writes=[s5ddB])
            QP.dma("c5", dtw[:, l], w_in_d.ap()[l][:, 3072:3088].rearrange("(kc p) c -> p kc c", p=128),
                   writes=[dtwB])
        arep = sb("arep", [128, depth, 16]); arepB = Buf("arep")
        ACT.op(lambda: nc.scalar.activation(arep[:], ssdp[:, :, 1, :], AF.Exp), [ssdpB], [arepB])
        DVE.op(lambda: nc.vector.tensor_scalar(arep[:], arep[:], -1.0, None, ALU.mult), [arepB], [arepB])

        wscB = Buf("wscratch")
        def blkview(l, b, kc):
            return wblk_d.ap()[l, b][:, 0:kc * 512].rearrange("p (kc c) -> p kc c", kc=kc)
        def wsrc(wd, l, r0, kc, c0):
            return wd.ap()[l][r0:r0 + kc * 128, c0:c0 + 512].rearrange("(kc p) c -> p kc c", p=128)
        for l in range(depth):
            cols = [0, 512] + [1024 + 512 * i for i in range(4)] + [3088] + [3600 + 512 * i for i in range(4)]
            for b, c0 in enumerate(cols):
                QP.dma("pre", blkview(l, b, 8), wsrc(w_in_d, l, 0, 8, c0))
            for b, cbi in enumerate([2, 3, 0, 1]):
                QP.dma("pre", blkview(l, 11 + b, 4), wsrc(w_glu_d, l, 0, 4, cbi * 512))
            for b in range(2):
                QP.dma("pre", blkview(l, 15 + b, 8), wsrc(w_out_d, l, 0, 8, b * 512))
            for kg in range(4):
                for c in range(2):
                    QP.dma("pre", blkview(l, 17 + kg * 4 + c, 8), wsrc(w_ffi_d, l, 0, 8, kg * 1024 + c * 512))
                for c in range(2):
                    QP.dma("pre", blkview(l, 17 + kg * 4 + 2 + c, 8), wsrc(w_ffo_d, l, kg * 1024, 8, c * 512))

        psum = [es.enter_context(nc.psum_tensor(f"ps{i}", [128, 512], F32)) for i in range(8)]
        psB = [Buf(f"ps{i}") for i in range(8)]
        pctr = [0]
        def getps():
            i = pctr[0] % 8
            pctr[0] += 1
            return psum[i], psB[i]

        s5s = sb("s5s", [128, depth, 3, 16]); s5sB = Buf("s5s")
        rtab = sb("rtab", [128, depth, 16]); rtabB = Buf("rtab")
        gst = sb("gst", [128, depth, 16, 2]); gstB = [Buf(f"gst{l}") for l in range(depth)]
        gst0 = sb("gst0", [128, depth, 16, 2]); gst0B = Buf("gst0")
        with ExitStack() as es2:
            sb2 = lambda name, shape, dt=F32: es2.enter_context(nc.sbuf_tensor("s_" + name, shape, dt))
            th = sb2("th", [128, 16]); thB = Buf("th")
            stp = sb2("stp", [128, 16]); stpB = Buf("stp")
            ang = sb2("ang", [128, 2, 512]); angB = Buf("ang")
            ang2 = sb2("ang2", [128, 2, 512]); ang2B = Buf("ang2")
            tabt = [sb2(f"tabt{i}", [128, 2, 512]) for i in range(2)]; tabtB = [Buf(f"tabt{i}") for i in range(2)]
            ab = sb2("ab", [128, 2, 16]); abB = Buf("ab")
            ff = sb2("ff", [128, 6, 16]); ffB = Buf("ff")
            bc_in = sb2("bc_in", [128, 4, 16, 16]); bcinB = Buf("bc_in")
            bbar = sb2("bbar", [128, 2, 16, 16]); bbarB = Buf("bbar")
            tmp = sb2("tmp", [128, 2, 16, 16]); tmpB = Buf("tmp5")
            bd = [sb2(f"bd{i}", [128, 128]) for i in range(2)]; bdB = [Buf(f"bd{i}") for i in range(2)]
            BLs = sb2("BLs", [128, 32, 128], BF); BLsB = Buf("BLs")
            CLs = sb2("CLs", [128, 32, 128], BF); CLsB = Buf("CLs")
            for l in range(depth):
                for a in range(3):
                    QP.dma("c6", s5s[:, l, a], s5s_d.ap()[l, a], writes=[s5sB])
                for a in range(2):
                    QP.dma("c7", bc_in[:, a], s5b_d.ap()[l, a], writes=[bcinB])
                    QP.dma("c7", bc_in[:, 2 + a], s5c_d.ap()[l, a], writes=[bcinB])
                ACT.op(lambda: nc.scalar.activation(stp[:], s5s[:, l, 2, :], AF.Exp), [s5sB], [stpB])
                DVE.op(lambda: nc.vector.tensor_tensor(th[:], s5s[:, l, 1, :], stp[:], ALU.mult), [s5sB, stpB], [thB])
                DVE.op(lambda: nc.vector.tensor_tensor(stp[:], s5s[:, l, 0, :], stp[:], ALU.mult), [s5sB, stpB], [stpB])
                ACT.op(lambda: nc.scalar.activation(rtab[:, l, :], stp[:], AF.Exp), [stpB], [rtabB])
                for j in range(16):
                    tt, ttB = tabt[j % 2], tabtB[j % 2]
                    DVE.op(lambda: nc.vector.tensor_scalar(ang[:, 0, :], tpos, th[:, j:j + 1], 0.5 * PI, ALU.mult, ALU.add),
                           [cfB, thB], [angB])
                    DVE.op(lambda: nc.vector.tensor_scalar(ang[:, 1, :], tpos, th[:, j:j + 1], None, ALU.mult),
                           [cfB, thB], [angB])
                    MAGIC = 12582912.0
                    DVE.op(lambda: nc.vector.tensor_scalar(ang2[:], ang[:], 1.0 / TWO_PI, MAGIC, ALU.mult, ALU.add), [angB], [ang2B])
                    DVE.op(lambda: nc.vector.tensor_scalar(ang2[:], ang2[:], -MAGIC, -TWO_PI, ALU.add, ALU.mult), [ang2B], [ang2B])
                    DVE.op(lambda: nc.vector.tensor_tensor(ang[:], ang[:], ang2[:], ALU.add), [angB, ang2B], [angB])
                    DVE.op(lambda: nc.vector.tensor_scalar(ang[:], ang[:], -PI, PI, ALU.max, ALU.min), [angB], [angB])
                    ACT.op(lambda: nc.scalar.activation(tt[:], ang[:], AF.Sin), [angB], [ttB])
                    QP.dma("tabw", tabs_d.ap()[l, j], tt[:], reads=[ttB])
                    POOL.op(lambda: nc.gpsimd.tensor_copy(ab[:, :, j:j + 1], tt[:, :, 0:1]), [ttB], [abB])
                DVE.op(lambda: nc.vector.tensor_tensor(ab[:], ab[:], rtab[:, l, :].unsqueeze(1).broadcast_to([128, 2, 16]), ALU.mult),
                       [abB, rtabB], [abB])
                are, aim = s5s[:, l, 0, :], s5s[:, l, 1, :]
                V = nc.vector
                DVE.op(lambda: V.tensor_scalar(ff[:, 0], ab[:, 0], -1.0, None, ALU.add), [abB], [ffB])
                DVE.op(lambda: V.tensor_tensor(ff[:, 1], are, are, ALU.mult), [s5sB], [ffB])
                DVE.op(lambda: V.tensor_tensor(ff[:, 4], aim, aim, ALU.mult), [s5sB], [ffB])
                DVE.op(lambda: V.tensor_tensor(ff[:, 1], ff[:, 1], ff[:, 4], ALU.add), [ffB], [ffB])
                DVE.op(lambda: V.reciprocal(ff[:, 1], ff[:, 1]), [ffB], [ffB])
                DVE.op(lambda: V.tensor_tensor(ff[:, 2], ff[:, 0], are, ALU.mult), [ffB, s5sB], [ffB])
                DVE.op(lambda: V.tensor_tensor(ff[:, 4], ab[:, 1], aim, ALU.mult), [abB, s5sB], [ffB])
                DVE.op(lambda: V.tensor_tensor(ff[:, 2], ff[:, 2], ff[:, 4], ALU.add), [ffB], [ffB])
                DVE.op(lambda: V.tensor_tensor(ff[:, 2], ff[:, 2], ff[:, 1], ALU.mult), [ffB], [ffB])
                DVE.op(lambda: V.tensor_tensor(ff[:, 3], ab[:, 1], are, ALU.mult), [abB, s5sB], [ffB])
                DVE.op(lambda: V.tensor_tensor(ff[:, 4], ff[:, 0], aim, ALU.mult), [ffB, s5sB], [ffB])
                DVE.op(lambda: V.tensor_tensor(ff[:, 3], ff[:, 3], ff[:, 4], ALU.subtract), [ffB], [ffB])
                DVE.op(lambda: V.tensor_tensor(ff[:, 3], ff[:, 3], ff[:, 1], ALU.mult), [ffB], [ffB])
                if debug_taps and l == 0:
                    for nm, ap_, bb in (("th", th[:], thB), ("ff", ff[:], ffB), ("ab", ab[:], abB), ("s5s", s5s[:, 0], s5sB), ("tab15", tabt[1][:], tabtB[1])):
                        d_ = nc.dram_tensor("tap_pre_" + nm, list(ap_.shape), ap_.dtype, kind="ExternalOutput")
                        QP.dma("tap", d_.ap(), ap_, reads=[bb])
                fre = ff[:, 2].unsqueeze(2).broadcast_to([128, 16, 16])
                fim = ff[:, 3].unsqueeze(2).broadcast_to([128, 16, 16])
                DVE.op(lambda: V.tensor_tensor(bbar[:, 0], bc_in[:, 0], fre, ALU.mult), [bcinB, ffB], [bbarB])
                DVE.op(lambda: V.tensor_tensor(tmp[:, 0], bc_in[:, 1], fim, ALU.mult), [bcinB, ffB], [tmpB])
                DVE.op(lambda: V.tensor_tensor(bbar[:, 0], bbar[:, 0], tmp[:, 0], ALU.subtract), [bbarB, tmpB], [bbarB])
                DVE.op(lambda: V.tensor_tensor(bbar[:, 1], bc_in[:, 1], fre, ALU.mult), [bcinB, ffB], [bbarB])
                DVE.op(lambda: V.tensor_tensor(tmp[:, 1], bc_in[:, 0], fim, ALU.mult), [bcinB, ffB], [tmpB])
                DVE.op(lambda: V.tensor_tensor(bbar[:, 1], bbar[:, 1], tmp[:, 1], ALU.add), [bbarB, tmpB], [bbarB])
                DVE.op(lambda: V.tensor_scalar(bc_in[:, 3], bc_in[:, 3], -1.0, None, ALU.mult), [bcinB], [bcinB])
                POOL.op(lambda: nc.gpsimd.memset(CLs[:], 0.0), [], [CLsB])
                for j in range(16):
                    q = j % 4
                    for part in range(2):
                        for two in range(2):
                            c0 = 32 * q + 16 * two
                            POOL.op(lambda: nc.gpsimd.tensor_copy(
                                CLs[64 * two:64 * two + 64, 2 * j + part, c0:c0 + 16],
                                bc_in[64 * two:64 * two + 64, 2 + part, j, :]), [bcinB], [CLsB])
                for j in range(16):
                    q = j % 4
                    for part in range(2):
                        k = (2 * j + part) % 2
                        POOL.op(lambda: nc.gpsimd.memset(bd[k][:], 0.0), [], [bdB[k]])
                        for two in range(2):
                            c0 = 32 * q + 16 * two
                            POOL.op(lambda: nc.gpsimd.tensor_copy(
                                bd[k][64 * two:64 * two + 64, c0:c0 + 16],
                                bbar[64 * two:64 * two + 64, part, j, :]), [bbarB], [bdB[k]])
                        ps, pB = getps()
                        PE.op(lambda: nc.tensor.transpose(ps[:, 0:128], bd[k][:], identf), [bdB[k], cfB], [pB])
                        ACT.op(lambda: nc.scalar.copy(BLs[:, 2 * j + part, :], ps[:, 0:128]), [pB], [BLsB])
                if debug_taps and l == 0:
                    for nm, ap_, bb in (("BLs", BLs[:], BLsB), ("CLs", CLs[:], CLsB), ("bbar", bbar[:], bbarB)):
                        d_ = nc.dram_tensor("tap_pre_" + nm, list(ap_.shape), ap_.dtype, kind="ExternalOutput")
                        QP.dma("tap", d_.ap(), ap_, reads=[bb])
                for hf in range(4):
                    QP.dma("tabw", wblk_d.ap()[l, 33][:, hf * 1024:(hf + 1) * 1024].rearrange("p (a b) -> p a b", a=8),
                           BLs[:, hf * 8:(hf + 1) * 8, :], reads=[BLsB])
                    QP.dma("tabw", wblk_d.ap()[l, 34][:, hf * 1024:(hf + 1) * 1024].rearrange("p (a b) -> p a b", a=8),
                           CLs[:, hf * 8:(hf + 1) * 8, :], reads=[CLsB])
            for s in QP.sems.values():
                SP.e.wait_ge(s[0], s[1])
                POOL.e.wait_ge(s[0], s[1])

        hbuf = [sb(f"h{k}", [128, NT, D]) for k in range(2)]
        hbufB = [[Buf(f"h{k}_{i}") for i in range(NT)] for k in range(2)]
        xT = sb("xT", [128, 8, N], BF); xTB = Buf("xT")
        hnT = sb("hnT", [128, 8, N], BF); hnTB = Buf("hnT")
        zs = sb("zs", [128, NT, D], BF); zsB = [Buf(f"zs{i}") for i in range(NT)]
        gts = sb("gts", [128, NT, 2048], BF); gtsB = [Buf(f"gts{i}") for i in range(NT)]
        xc = sb("xc", [128, 16, N], BF); xcB = Buf("xc")
        hid = sb("hid", [128, 16, N], BF); hidB = Buf("hid")
        obuf = hid[:].rearrange("p a b -> p (a b)").bitcast(F32).rearrange("p (i d) -> p i d", d=D)
        xtok = sb("xtok", [128, NT, D], BF); xtokB = [Buf(f"xtok{i}") for i in range(NT)]
        btok = sb("btok", [128, NT, 512], BF); btokB = [Buf(f"btok{i}") for i in range(NT)]
        uTf = sb("uTf", [128, 4, N]); uTfB = Buf("uTf")
        uTb = sb("uTb", [128, 4, N], BF); uTbB = Buf("uTb")
        ybT = sb("ybT", [128, 4, N], BF); ybTB = Buf("ybT")
        yb = sb("yb", [128, NT, D], BF); ybB = [Buf(f"yb{i}") for i in range(NT)]
        dts = sb("dts", [128, NT, 16]); dtsB = [Buf(f"dts{i}") for i in range(NT)]
        NW = 5
        wring = [sb(f"wr{i}", [128, 4096], BF) for i in range(NW)]; wringB = [Buf(f"wr{i}") for i in range(NW)]
        tring = [sb(f"tr{i}", [128, 2, N]) for i in range(2)]; tringB = [Buf(f"tr{i}") for i in range(2)]
        snw = sb("snwr", [128, D]); snwB = Buf("snw")
        halo = sb("halo", [128, depth, 16, 3]); haloB = [Buf(f"halo{l}") for l in range(depth)]
        halo0 = sb("halo0", [128, depth, 16, 3]); halo0B = [Buf(f"halo0{l}") for l in range(depth)]
        gst0B = [Buf(f"gst0{l}") for l in range(depth)]
        Hs = sb("Hs", [128, depth, 16, 64]); HsB = [Buf(f"Hs{l}") for l in range(depth)]
        Hb = sb("Hb", [128, depth, 16, 64], BF); HbB = [Buf(f"Hb{l}") for l in range(depth)]
        xr = [sb(f"xr{i}", [128, N + 3]) for i in range(2)]; xrB = [Buf(f"xr{i}") for i in range(2)]
        acc = [sb(f"acc{i}", [128, N]) for i in range(2)]; accB = [Buf(f"acc{i}") for i in range(2)]
        rl = [sb(f"rl{i}", [128, N]) for i in range(2)]; rlB = [Buf(f"rl{i}") for i in range(2)]
        sm = sb("sm", [128, 8, 16]); smB = Buf("sm")
        lseg = sb("lseg", [128, 16, 128]); lsegB = Buf("lseg")
        Lm = sb("Lm", [128, 16, 128], BF); LmB = Buf("Lm")
        sg = lseg[:].rearrange("p a b -> p (a b)").bitcast(BF).rearrange("p (i c) -> p i c", c=D)
        sgB = [lsegB] * NT
        Mm = sb("Mm", [128, 16, 128], BF); MmB = Buf("Mm")
        scm = sb("scm", [128, 4, 128]); scmB = Buf("scm")
        xdt = sb("xdt", [128, 16, 64], BF); xdtB = Buf("xdt")
        xw = sb("xw", [128, 16, 64], BF); xwB = Buf("xw")
        yv = sb("yv", [128, D]); yvB = Buf("yv")
        y2 = sb("y2", [128, D]); y2B = Buf("y2")
        s5t = [sb(f"s5t{i}", [128, N]) for i in range(8)]; s5tB = [Buf(f"s5t{i}") for i in range(8)]
        hS = sb("hS", [128, 2, 2, 4, N], BF); hSB = [Buf("hS0"), Buf("hS1")]

        class Th:
            pass
        TM, TF = Th(), Th()
        for nm, th, xt_, xtB_ in (("M", TM, xT, xTB), ("F", TF, hnT, hnTB)):
            th.name = nm
            th.xT, th.xTB = xt_, xtB_
            th.ss = sb("ss" + nm, [128, 8]); th.ssB = Buf("ss" + nm)
            th.xnb = [sb(f"xnb{nm}{i}", [128, D], BF) for i in range(2)]
            th.xnbB = [Buf(f"xnb{nm}{i}") for i in range(2)]
            th.nw = sb("nwr" + nm, [128, D]); th.nwB = Buf("nwr" + nm)
            th.ctr = [0]
            th.held = []

        TM.junk = y2[:].bitcast(BF)[:, 0:D]; TM.junkB = y2B
        TM.tmpf = y2[:]; TM.tmpfB = y2B
        TF.tmpf = hid[:].rearrange("p a b -> p (a b)").bitcast(F32)[:, 0:D]; TF.tmpfB = hidB
        TF.junk = hid[:].rearrange("p a b -> p (a b)")[:, 0:D]; TF.junkB = hidB
        import os as _os
        if _os.environ.get('KDEBUG'): print('SBUF remaining after allocs', nc.sbuf_bytes_remaining)
        V = nc.vector
        A = nc.scalar
        G = nc.gpsimd
        T = nc.tensor

        POOL.op(lambda: G.memset(halo[:], 0.0), [], haloB)
        POOL.op(lambda: G.memset(Hs[:], 0.0), [], HsB)
        POOL.op(lambda: G.memset(Hb[:], 0.0), [], HbB)
        DVE.op(lambda: V.memset(gst[:].rearrange("p a b c -> p (a b c)"), 0.0), [], gstB)

        wstate = {"next_load": 0, "next_use": 0, "sched": [], "free": [], "slot_of": {}}

        def pump():
            while wstate["free"] and wstate["next_load"] < len(wstate["sched"]):
                k = wstate["next_load"]
                sl = wstate["free"].pop(0)
                l, b = wstate["sched"][k]
                QW.dma(f"w{sl}", wring[sl][:], wblk_d.ap()[l, b], writes=[wringB[sl]])
                wstate["slot_of"][k] = sl
                wstate["next_load"] += 1

        def relall(th):
            if not Eng.DRY:
                for k in th.held:
                    wstate["free"].append(wstate["slot_of"][k])
                pump()
            th.held = []

        def getblk(l, b, th, keep=False):
            if not keep:
                relall(th)
            k = wstate["next_use"]
            wstate["next_use"] += 1
            if Eng.DRY:
                wstate["sched"].append((l, b))
                return wring[0], wringB[0]
            assert wstate["sched"][k] == (l, b), (wstate["sched"][k], l, b)
            pump()
            assert k < wstate["next_load"], "weight ring exhausted"
            th.held.append(k)
            sl = wstate["slot_of"][k]
            return wring[sl], wringB[sl]

        def load_nw(th, row):
            QW.dma("nw" + th.name, th.nw[:], nw_d.ap()[row].partition_broadcast(128), writes=[th.nwB])
            return th.nw, th.nwB

        def norm_stats(th, hb, hbB, i):
            ss, ssB = th.ss, th.ssB
            ACT.op(lambda: A.activation(th.junk, hb[:, i, :], AF.Square, accum_out=ss[:, i:i + 1]), [hbB[i]], [th.junkB, ssB])
            ACT.op(lambda: A.activation(ss[:, 4 + i:5 + i], ss[:, i:i + 1], AF.Ln, bias=epsc, scale=1.0 / D), [ssB, cfB], [ssB])
            ACT.op(lambda: A.activation(ss[:, 4 + i:5 + i], ss[:, 4 + i:5 + i], AF.Exp, scale=-0.5), [ssB], [ssB])

        def transposes_to(srcs, srcB, dst_ap3, dstB, nk):
            ps, pB = getps()
            psb = ps[:].bitcast(BF)
            for k in range(nk):
                PE.op(lambda: T.transpose(psb[:, k * 128:(k + 1) * 128], srcs[k], identb[:]),
                      [srcB, identbB], [pB], sig=(k == nk - 1))
            ACT.op(lambda: A.copy(dst_ap3, psb[:, 0:nk * 128].rearrange("p (k t) -> p k t", k=nk)), [pB], [dstB])

        def rmsnorm_T(th, hb, hbB, nt, row):
            wr, wrB = load_nw(th, row)
            for i in range(nt):
                norm_stats(th, hb, hbB, i)
                xb, xbB = th.xnb[th.ctr[0] % 2], th.xnbB[th.ctr[0] % 2]
                th.ctr[0] += 1
                ACT.op(lambda: A.activation(th.tmpf, hb[:, i, :], AF.Copy, scale=th.ss[:, 4 + i:5 + i]), [hbB[i], th.ssB], [th.tmpfB])
                POOL.op(lambda: G.tensor_tensor(xb[:], th.tmpf, wr[:], ALU.mult), [th.tmpfB, wrB], [xbB])
                transposes_to([xb[:, k * 128:(k + 1) * 128] for k in range(8)], xbB,
                              th.xT[:, :, i * 128:(i + 1) * 128], th.xTB, 8)

        def mm_tok(blk, blkB, i, kc_n, src, srcB):
            ps, pB = getps()
            bv = blk[:, 0:kc_n * 512].rearrange("p (kc c) -> p kc c", kc=kc_n)
            for kc in range(kc_n):
                PE.op(lambda: T.matmul(ps[:], src[:, kc, i * 128:(i + 1) * 128], bv[:, kc, :],
                                       start=(kc == 0), stop=(kc == kc_n - 1)),
                      [srcB, blkB], [pB], sig=(kc == kc_n - 1))
            return ps, pB

        def mm_feat(blk, blkB, f, n, src, srcB):
            ps, pB = getps()
            bv = blk[:].rearrange("p (kc c) -> p kc c", kc=8)
            for kc in range(8):
                PE.op(lambda: T.matmul(ps[:, 0:n], bv[:, kc, f * 128:(f + 1) * 128], src[:, kc, 0:n],
                                       start=(kc == 0), stop=(kc == 7)),
                      [srcB, blkB], [pB], sig=(kc == 7))
            return ps, pB

        cctr = [0]
        rctr = [0]
        tctr = [0]
        taps = {}
        hs0_tok = [None] * depth

        def tap(name, ap, bufs):
            if not debug_taps or Eng.DRY:
                return
            shape = list(ap.shape)
            d = nc.dram_tensor("tap_" + name, shape, ap.dtype, kind="ExternalOutput")
            QP.dma("tap", d.ap(), ap, reads=bufs)

        def mixer(l, nt, is_meta, hb, hbB, info):
            n = nt * 128
            sc = nt / 2.0
            if l == 0:
                if is_meta:
                    QW.dma("xin" + info["par"], hb[:, 0, :], meta_d.ap(), writes=[hbB[0]])
                else:
                    seq, m = info["seq"], info["m"]
                    QW.dma("xin" + info["par"], hb[:], x_d.ap()[seq, m * N:(m + 1) * N, :].rearrange("(i p) d -> p i d", p=128),
                           writes=hbB)
            if info["restore"]:
                POOL.op(lambda: G.tensor_copy(halo[:, l], halo0[:, l]), [halo0B[l]], [haloB[l]])
                POOL.op(lambda: G.tensor_copy(gst[:, l], gst0[:, l]), [gst0B[l]], [gstB[l]])
                if not Eng.DRY:
                    SP.wait(hs0_tok[l])
                QW.dma("hs0r", Hs[:, l].rearrange("p h e -> p (h e)"), hs0_d.ap()[l], writes=[HsB[l]])
                ACT.op(lambda: A.copy(Hb[:, l], Hs[:, l]), [HsB[l]], [HbB[l]])
            QW.dma("snw", snw[:], snw_d.ap()[l].partition_broadcast(128), writes=[snwB])
            rmsnorm_T(TM, hb, hbB, nt, 2 * l)
            yield 6 * sc
            for cb in range(2):
                blk, bB = getblk(l, cb, TM)
                for i in range(nt):
                    ps, pB = mm_tok(blk, bB, i, 8, xT, xTB)
                    ACT.op(lambda: A.activation(zs[:, i, cb * 512:(cb + 1) * 512], ps[:], AF.Silu), [pB], [zsB[i]])
                    yield 1.8
            prev = None
            for cb in range(4):
                blk, bB = getblk(l, 2 + cb, TM)
                for f in range(4):
                    ft = cb * 4 + f
                    ps, pB = mm_feat(blk, bB, f, n, xT, xTB)
                    c = cctr[0] % 2
                    cctr[0] += 1
                    cw = convp[:, l, ft, :]
                    ACT.op(lambda: A.copy(xr[c][:, 3:3 + n], ps[:, 0:n]), [pB], [xrB[c]])
                    ACT.op(lambda: A.copy(xr[c][:, 0:3], halo[:, l, ft, :]), [haloB[l]], [xrB[c]])
                    ACT.op(lambda: A.activation(acc[c][:, 0:n], ps[:, 0:n], AF.Identity, bias=cw[:, 4:5], scale=cw[:, 3:4]),
                           [pB, convpB], [accB[c]])
                    ACT.op(lambda: A.copy(halo[:, l, ft, :], ps[:, n - 3:n]), [pB], [haloB[l]])
                    if prev is not None:
                        pft, pc = prev
                        ACT.op(lambda: A.activation(xc[:, pft, 0:n], acc[pc][:, 0:n], AF.Silu), [accB[pc]], [xcB])
                    for k in range(3):
                        DVE.op(lambda: V.scalar_tensor_tensor(acc[c][:, 0:n], xr[c][:, k:k + n], cw[:, k:k + 1], acc[c][:, 0:n],
                                                              ALU.mult, ALU.add), [xrB[c], accB[c], convpB], [accB[c]])
                    prev = (ft, c)
                    yield 1.5 * sc
            pft, pc = prev
            ACT.op(lambda: A.activation(xc[:, pft, 0:n], acc[pc][:, 0:n], AF.Silu), [accB[pc]], [xcB])
            blk, bB = getblk(l, 6, TM)
            for ct in range(4):
                ps, pB = mm_feat(blk, bB, ct, n, xT, xTB)
                ACT.op(lambda: A.copy(uTf[:, ct, 0:n], ps[:, 0:n]), [pB], [uTfB])
                ACT.op(lambda: A.copy(uTb[:, ct, 0:n], ps[:, 0:n]), [pB], [uTbB])
                yield 1.2 * sc
            for cb in range(4):
                blk, bB = getblk(l, 7 + cb, TM)
                for i in range(nt):
                    ps, pB = mm_tok(blk, bB, i, 8, xT, xTB)
                    ACT.op(lambda: A.activation(gts[:, i, cb * 512:(cb + 1) * 512], ps[:], AF.Sigmoid), [pB], [gtsB[i]])
                    yield 1.8
            for i in range(nt):
                ps, pB = getps()
                for kc in range(8):
                    PE.op(lambda: T.matmul(ps[:, 0:16], xT[:, kc, i * 128:(i + 1) * 128], dtw[:, l, kc, :],
                                           start=(kc == 0), stop=(kc == 7)), [xTB, dtwB], [pB], sig=(kc == 7))
                d0 = sm[:, 0, :]; d1 = sm[:, 1, :]
                DVE.op(lambda: V.tensor_tensor(d0, ps[:, 0:16], ssdp[:, l, 0, :], ALU.add), [pB, ssdpB], [smB])
                ACT.op(lambda: A.activation(d1, d0, AF.Abs), [smB], [smB])
                ACT.op(lambda: A.activation(d1, d1, AF.Exp, scale=-1.0), [smB], [smB])
                ACT.op(lambda: A.activation(d1, d1, AF.Ln, bias=1.0), [smB], [smB])
                DVE.op(lambda: V.tensor_scalar(d0, d0, 0.0, None, ALU.max), [smB], [smB])
                if is_meta:
                    DVE.op(lambda: V.tensor_tensor(d0, d0, d1, ALU.add), [smB], [smB])
                    DVE.op(lambda: V.tensor_scalar(dts[:, i, :], d0, padmask, None, ALU.mult), [smB, cfB], [dtsB[i]])
                else:
                    DVE.op(lambda: V.tensor_tensor(dts[:, i, :], d0, d1, ALU.add), [smB], [dtsB[i]])
                yield 1.5
            for i in range(nt):
                transposes_to([xc[:, k, i * 128:(i + 1) * 128] for k in range(8)], xcB,
                              xtok[:, i, :].rearrange("p (k t) -> p k t", k=8), xtokB[i], 8)
                transposes_to([xc[:, 8 + k, i * 128:(i + 1) * 128] for k in range(4)], xcB,
                              btok[:, i, :].rearrange("p (k t) -> p k t", k=4), btokB[i], 4)
                yield 2.0
            BLk, BLkB = getblk(l, 33, TM)
            CLk, CLkB = getblk(l, 34, TM, keep=True)
            BLv = BLk[:].rearrange("p (a b) -> p a b", a=32)
            CLv = CLk[:].rearrange("p (a b) -> p a b", a=32)
            def load_tab(j):
                i = tctr[0] % 2
                tctr[0] += 1
                QW.dma(f"tab{i}", tring[i][:, :, 0:n], tabs_d.ap()[l, j][:, :, 0:n], writes=[tringB[i]])
                return tring[i], tringB[i]
            t = [x[:, 0:n] for x in s5t]
            tB = s5tB
            TT = V.tensor_tensor
            def cproj(ct):
                psy, pyB = getps()
                idx = 0
                for qq in range(4):
                    for part in range(2):
                        jj = ct * 4 + qq
                        PE.op(lambda: T.matmul(psy[:, 0:n], CLv[:, 2 * jj + part, :], hS[:, ct % 2, part, qq, 0:n],
                                               start=(idx == 0), stop=(idx == 7)), [CLkB, hSB[ct % 2]], [pyB], sig=(idx == 7))
                        idx += 1
                a0, a1 = acc[0][:, 0:n], acc[1][:, 0:n]
                DVE.op(lambda: V.scalar_tensor_tensor(a0, uTf[:, ct, 0:n], s5dd[:, l, ct:ct + 1], psy[:, 0:n], ALU.mult, ALU.add),
                       [uTfB, s5ddB, pyB], [accB[0]])
                DVE.op(lambda: TT(a1, a0, a0, ALU.mult), [accB[0]], [accB[1]])
                DVE.op(lambda: V.tensor_scalar(a1, a1, 0.044715, 1.0, ALU.mult, ALU.add), [accB[1]], [accB[1]])
                DVE.op(lambda: TT(a1, a1, a0, ALU.mult), [accB[0], accB[1]], [accB[1]])
            def gelu_fin(ct):
                a0, a1 = acc[0][:, 0:n], acc[1][:, 0:n]
                ACT.op(lambda: A.activation(a1, a1, AF.Sigmoid, scale=1.5957691216057308), [accB[1]], [accB[1]])
                DVE.op(lambda: TT(ybT[:, ct, 0:n], a0, a1, ALU.mult), [accB[0], accB[1]], [ybTB])
            tabq = [load_tab(0)]
            pending = []
            for j in range(16):
                ct, q = j // 4, j % 4
                tb, tbB = tabq.pop(0)
                if j + 1 < 16:
                    tabq.append(load_tab(j + 1))
                Ec, Es = tb[:, 0, 0:n], tb[:, 1, 0:n]
                psr, prB = getps()
                PE.op(lambda: T.matmul(psr[:, 0:n], BLv[:, 2 * j, :], uTb[:, ct, 0:n], start=True, stop=True), [BLkB, uTbB], [prB])
                psi, piB = getps()
                PE.op(lambda: T.matmul(psi[:, 0:n], BLv[:, 2 * j + 1, :], uTb[:, ct, 0:n], start=True, stop=True), [BLkB, uTbB], [piB])
                DVE.op(lambda: TT(t[0], psr[:, 0:n], Ec, ALU.mult), [prB, tbB], [tB[0]])
                DVE.op(lambda: TT(t[1], psi[:, 0:n], Es, ALU.mult), [piB, tbB], [tB[1]])
                DVE.op(lambda: TT(t[0], t[0], t[1], ALU.add), [tB[0], tB[1]], [tB[0]])
                DVE.op(lambda: TT(t[2], psi[:, 0:n], Ec, ALU.mult), [piB, tbB], [tB[2]])
                DVE.op(lambda: TT(t[3], psr[:, 0:n], Es, ALU.mult), [prB, tbB], [tB[3]])
                DVE.op(lambda: TT(t[2], t[2], t[3], ALU.subtract), [tB[2], tB[3]], [tB[2]])
                rj = rtab[:, l, j:j + 1].broadcast_to([128, n])
                DVE.op(lambda: V.tensor_tensor_scan(t[4], rj, t[0], gst[:, l, j, 0:1], ALU.mult, ALU.add),
                       [rtabB, tB[0], gstB[l]], [tB[4]])
                DVE.op(lambda: V.tensor_tensor_scan(t[5], rj, t[2], gst[:, l, j, 1:2], ALU.mult, ALU.add),
                       [rtabB, tB[2], gstB[l]], [tB[5]])
                DVE.op(lambda: TT(t[0], t[4], Ec, ALU.mult), [tB[4], tbB], [tB[0]])
                DVE.op(lambda: TT(t[1], t[5], Es, ALU.mult), [tB[5], tbB], [tB[1]])
                DVE.op(lambda: TT(t[2], t[5], Ec, ALU.mult), [tB[5], tbB], [tB[2]])
                DVE.op(lambda: TT(t[3], t[4], Es, ALU.mult), [tB[4], tbB], [tB[3]])
                yield 5.0 * sc
                DVE.op(lambda: TT(hS[:, ct % 2, 0, q, 0:n], t[0], t[1], ALU.subtract), [tB[0], tB[1]], [hSB[ct % 2]])
                DVE.op(lambda: TT(hS[:, ct % 2, 1, q, 0:n], t[2], t[3], ALU.add), [tB[2], tB[3]], [hSB[ct % 2]])
                DVE.op(lambda: TT(gst[:, l, j, 0:1], t[0][:, n - 1:n], t[1][:, n - 1:n], ALU.subtract), [tB[0], tB[1]], [gstB[l]])
                DVE.op(lambda: TT(gst[:, l, j, 1:2], t[2][:, n - 1:n], t[3][:, n - 1:n], ALU.add), [tB[2], tB[3]], [gstB[l]])
                if pending and q == 1:
                    cproj(pending[0])
                    yield 2.0 * sc
                if pending and q == 3:
                    gelu_fin(pending.pop(0))
                    yield 0.5 * sc
                if q == 3:
                    pending.append(ct)
            while pending:
                cproj(pending[0])
                yield 2.0 * sc
                gelu_fin(pending.pop(0))
                yield 0.5 * sc
            for b in range(4):
                blk, bB = getblk(l, 11 + b, TM)
                for i in range(nt):
                    ps, pB = mm_tok(blk, bB, i, 4, ybT, ybTB)
                    if b < 2:
                        ACT.op(lambda: A.activation(sg[:, i, b * 512:(b + 1) * 512], ps[:], AF.Sigmoid), [pB], [sgB[i]])
                    else:
                        c0 = (b - 2) * 512
                        DVE.op(lambda: V.tensor_tensor(yb[:, i, c0:c0 + 512], ps[:], sg[:, i, c0:c0 + 512], ALU.mult),
                               [pB, sgB[i]], [ybB[i]])
                    yield 1.0
            for i in range(nt):
                dA = sm[:, 2, :]
                DVE.op(lambda: V.tensor_tensor(dA, dts[:, i, :], arep[:, l, :], ALU.mult), [dtsB[i], arepB], [smB])
                DVE.op(lambda: V.tensor_tensor(lseg[:], strict.unsqueeze(1).broadcast_to([128, 16, 128]),
                                               dA.unsqueeze(2).broadcast_to([128, 16, 128]), ALU.mult), [cfB, smB], [lsegB])
                psc, pcB = getps()
                PE.op(lambda: T.matmul(psc[:, 0:16], tri, dA, start=True, stop=True), [cfB, smB], [pcB])
                PE.op(lambda: T.matmul(psc[:, 16:32], ones, dA, start=True, stop=True), [cfB, smB], [pcB])
                cs = sm[:, 3, :]; ecs = sm[:, 4, :]; wv = sm[:, 5, :]; etot = sm[:, 6, :]
                ACT.op(lambda: A.copy(cs, psc[:, 0:16]), [pcB], [smB])
                ACT.op(lambda: A.activation(etot, psc[:, 16:32], AF.Exp), [pcB], [smB])
                DVE.op(lambda: V.tensor_tensor(wv, psc[:, 16:32], cs, ALU.subtract), [pcB, smB], [smB])
                ACT.op(lambda: A.activation(wv, wv, AF.Exp), [smB], [smB])
                ACT.op(lambda: A.activation(ecs, cs, AF.Exp), [smB], [smB])
                DVE.op(lambda: V.tensor_tensor(wv, wv, dts[:, i, :], ALU.mult), [smB, dtsB[i]], [smB])
                yield 4.0
                for g in range(4):
                    ps, pB = getps()
                    for r in range(4):
                        hh = g * 4 + r
                        PE.op(lambda: T.matmul(ps[:, r * 128:(r + 1) * 128], lseg[:, hh, :], tri, start=True, stop=True),
                              [lsegB, cfB], [pB], sig=(r == 3))
                    ACT.op(lambda: A.activation(Lm[:, g * 4:(g + 1) * 4, :], ps[:].rearrange("p (r t) -> p r t", r=4), AF.Exp),
                           [pB], [LmB])
                ps, pB = getps()
                for g in range(4):
                    PE.op(lambda: T.matmul(ps[:, g * 128:(g + 1) * 128], xc[:, 8 + g, i * 128:(i + 1) * 128],
                                           xc[:, 12 + g, i * 128:(i + 1) * 128], start=True, stop=True), [xcB], [pB], sig=(g == 3))
                DVE.op(lambda: V.tensor_tensor(scm[:], ps[:].rearrange("p (g t) -> p g t", g=4),
                                               tri.unsqueeze(1).broadcast_to([128, 4, 128]), ALU.mult), [pB, cfB], [scmB])
                yield 3.0
                DVE.op(lambda: V.tensor_tensor(Mm[:].rearrange("p (g r) t -> p g r t", g=4),
                                               Lm[:].rearrange("p (g r) t -> p g r t", g=4),
                                               scm[:].unsqueeze(2).broadcast_to([128, 4, 4, 128]), ALU.mult), [LmB, scmB], [MmB])
                x3 = xtok[:, i, :].rearrange("p (h e) -> p h e", h=16)
                DVE.op(lambda: V.tensor_tensor(xdt[:], x3, dts[:, i, :].unsqueeze(2).broadcast_to([128, 16, 64]), ALU.mult),
                       [xtokB[i], dtsB[i]], [xdtB])
                DVE.op(lambda: V.tensor_tensor(xw[:], x3, wv.unsqueeze(2).broadcast_to([128, 16, 64]), ALU.mult),
                       [xtokB[i], smB], [xwB])
                yield 5.0
                pyd = [getps() for _ in range(2)]
                for hh in range(16):
                    ps, pB = pyd[hh // 8]
                    PE.op(lambda: T.matmul(ps[:, (hh % 8) * 64:(hh % 8 + 1) * 64], Mm[:, hh, :], xdt[:, hh, :], start=True, stop=True),
                          [MmB, xdtB], [pB], sig=(hh % 8 == 7))
                pyo = [getps() for _ in range(2)]
                for g in range(4):
                    ps, pB = pyo[g // 2]
                    PE.op(lambda: T.matmul(ps[:, (g % 2) * 256:(g % 2 + 1) * 256], xc[:, 12 + g, i * 128:(i + 1) * 128],
                                           Hb[:, l, g * 4:(g + 1) * 4, :].rearrange("p r e -> p (r e)"), start=True, stop=True),
                          [xcB, HbB[l]], [pB], sig=(g % 2 == 1))
                for half in range(2):
                    sl = slice(half * 512, (half + 1) * 512)
                    hsl = slice(half * 8, (half + 1) * 8)
                    DVE.op(lambda: V.tensor_tensor(yv[:, sl].rearrange("p (h e) -> p h e", h=8),
                                                   pyo[half][0][:].rearrange("p (h e) -> p h e", h=8),
                                                   ecs[:, hsl].unsqueeze(2).broadcast_to([128, 8, 64]), ALU.mult),
                           [pyo[half][1], smB], [yvB])
                    DVE.op(lambda: V.tensor_tensor(yv[:, sl], yv[:, sl], pyd[half][0][:], ALU.add), [yvB, pyd[half][1]], [yvB])
                yield 3.0
                pss = [getps() for _ in range(2)]
                for g in range(4):
                    ps, pB = pss[g // 2]
                    PE.op(lambda: T.matmul(ps[:, (g % 2) * 256:(g % 2 + 1) * 256], btok[:, i, g * 128:(g + 1) * 128],
                                           xw[:, g * 4:(g + 1) * 4, :].rearrange("p r e -> p (r e)"), start=True, stop=True),
                          [btokB[i], xwB], [pB], sig=(g % 2 == 1))
                Hfl = Hs[:, l].rearrange("p h e -> p (h e)")
                DVE.op(lambda: V.tensor_tensor(Hs[:, l], Hs[:, l], etot.unsqueeze(2).broadcast_to([128, 16, 64]), ALU.mult),
                       [HsB[l], smB], [HsB[l]])
                for half in range(2):
                    sl = slice(half * 512, (half + 1) * 512)
                    DVE.op(lambda: V.tensor_tensor(Hfl[:, sl], Hfl[:, sl], pss[half][0][:], ALU.add),
                           [HsB[l], pss[half][1]], [HsB[l]])
                DVE.op(lambda: V.tensor_copy(Hb[:, l], Hs[:, l]), [HsB[l]], [HbB[l]])
                yield 3.0
                DVE.op(lambda: V.tensor_tensor(y2[:].rearrange("p (h e) -> p h e", h=16), x3,
                                               ssdp[:, l, 2, :].unsqueeze(2).broadcast_to([128, 16, 64]), ALU.mult),
                       [xtokB[i], ssdpB], [y2B])
                DVE.op(lambda: V.tensor_tensor(yv[:], yv[:], y2[:], ALU.add), [yvB, y2B], [yvB])
                DVE.op(lambda: V.tensor_tensor(yv[:], yv[:], zs[:, i, :], ALU.mult), [yvB, zsB[i]], [yvB])
                gss = sm[:, 7, 0:4]; grs = sm[:, 7, 4:8]
                for g in range(4):
                    ACT.op(lambda: A.activation(y2[:, g * 256:(g + 1) * 256], yv[:, g * 256:(g + 1) * 256], AF.Square,
                                                accum_out=gss[:, g:g + 1]), [yvB, smB], [y2B, smB])
                DVE.op(lambda: V.tensor_scalar(grs, gss, 1.0 / 256, EPS, ALU.mult, ALU.add), [smB], [smB])
                ACT.op(lambda: A.activation(grs, grs, AF.Sqrt), [smB], [smB])
                DVE.op(lambda: V.reciprocal(grs, grs), [smB], [smB])
                yield 5.0
                DVE.op(lambda: V.tensor_tensor(yv[:].rearrange("p (g e) -> p g e", g=4), yv[:].rearrange("p (g e) -> p g e", g=4),
                                               grs.unsqueeze(2).broadcast_to([128, 4, 256]), ALU.mult), [yvB, smB], [yvB])
                DVE.op(lambda: V.tensor_tensor(yv[:], yv[:], snw[:], ALU.mult), [yvB, snwB], [yvB])
                DVE.op(lambda: V.tensor_tensor(yv[:], yv[:], gts[:, i, 0:D], ALU.mult), [yvB, gtsB[i]], [yvB])
                DVE.op(lambda: V.tensor_tensor(y2[:], yb[:, i, :], gts[:, i, D:2 * D], ALU.mult), [ybB[i], gtsB[i]], [y2B])
                mixb, mixbB = TM.xnb[TM.ctr[0] % 2], TM.xnbB[TM.ctr[0] % 2]
                TM.ctr[0] += 1
                DVE.op(lambda: V.tensor_tensor(mixb[:], yv[:], y2[:], ALU.add), [yvB, y2B], [mixbB])
                transposes_to([mixb[:, k * 128:(k + 1) * 128] for k in range(8)], mixbB,
                              xT[:, :, i * 128:(i + 1) * 128], xTB, 8)
                yield 6.0
            for cb in range(2):
                blk, bB = getblk(l, 15 + cb, TM)
                for i in range(nt):
                    ps, pB = mm_tok(blk, bB, i, 8, xT, xTB)
                    DVE.op(lambda: V.tensor_tensor(hb[:, i, cb * 512:(cb + 1) * 512], hb[:, i, cb * 512:(cb + 1) * 512], ps[:], ALU.add),
                           [hbB[i], pB], [hbB[i]])
                    yield 1.8
            relall(TM)
            if info["save"]:
                POOL.op(lambda: G.tensor_copy(halo0[:, l], halo[:, l]), [haloB[l]], [halo0B[l]])
                POOL.op(lambda: G.tensor_copy(gst0[:, l], gst[:, l]), [gstB[l]], [gst0B[l]])
                tk = QW.dma("hs0w", hs0_d.ap()[l], Hs[:, l].rearrange("p h e -> p (h e)"), reads=[HsB[l]])
                if not Eng.DRY:
                    hs0_tok[l] = tk

        def ffn(l, nt, is_meta, hb, hbB, info):
            n = nt * 128
            sc = nt / 2.0
            rmsnorm_T(TF, hb, hbB, nt, 2 * l + 1)
            yield 6 * sc * FW
            for kg in range(4):
                for c in range(2):
                    blk, bB = getblk(l, 17 + kg * 4 + c, TF)
                    for f in range(4):
                        ps, pB = mm_feat(blk, bB, f, n, hnT, hnTB)
                        rr = rctr[0] % 2
                        rctr[0] += 1
                        ACT.op(lambda: A.activation(rl[rr][:, 0:n], ps[:, 0:n], AF.Relu), [pB], [rlB[rr]])
                        POOL.op(lambda: G.tensor_tensor(hid[:, (kg % 2) * 8 + c * 4 + f, 0:n], rl[rr][:, 0:n], rl[rr][:, 0:n], ALU.mult),
                                [rlB[rr]], [hidB])
                        yield 1.2 * sc * FW
                for c in range(2):
                    blk, bB = getblk(l, 17 + kg * 4 + 2 + c, TF)
                    for i in range(nt):
                        ps, pB = getps()
                        bv = blk[:].rearrange("p (kc c) -> p kc c", kc=8)
                        for kc in range(8):
                            PE.op(lambda: T.matmul(ps[:], hid[:, (kg % 2) * 8 + kc, i * 128:(i + 1) * 128], bv[:, kc, :],
                                                   start=(kc == 0), stop=(kc == 7)), [hidB, bB], [pB], sig=(kc == 7))
                        DVE.op(lambda: V.tensor_tensor(hb[:, i, c * 512:(c + 1) * 512], hb[:, i, c * 512:(c + 1) * 512], ps[:], ALU.add),
                               [hbB[i], pB], [hbB[i]])
                        yield 1.8 * FW
            relall(TF)
            if is_meta:
                DVE.op(lambda: V.tensor_scalar(hb[:, 0, :], hb[:, 0, :], padmask, None, ALU.mult), [hbB[0], cfB], [hbB[0]])
            tg = ("m" if is_meta else "s") + str(l)
            if tg not in taps:
                taps[tg] = 1
                tap(tg + "_h", hb[:, 0:nt, :], hbB)
            if l == depth - 1 and not is_meta:
                seq, m = info["seq"], info["m"]
                wr, wrB = load_nw(TF, 2 * depth)
                for i in range(NT):
                    norm_stats(TF, hb, hbB, i)
                for i in range(NT):
                    DVE.op(lambda: V.scalar_tensor_tensor(obuf[:, i, :], hb[:, i, :], TF.ss[:, 4 + i:5 + i], wr[:], ALU.mult, ALU.mult),
                           [hbB[i], TF.ssB, wrB], [hidB])
                QW.dma("oout", out_d.ap()[seq, m * N:(m + 1) * N, :].rearrange("(i p) d -> p i d", p=128), obuf[:],
                       reads=[hidB])
                yield 4.0 * FW

        FW = 3.0

        items = [("meta", 0, 0)] + [("seq", s_, m_) for s_ in range(NSEQ) for m_ in range(NMT)]

        def emit_all():
            for cnt in (pctr, cctr, rctr, tctr, TM.ctr, TF.ctr):
                cnt[0] = 0
            wstate["next_use"] = 0
            wstate["next_load"] = 0
            wstate["free"] = list(range(NW))
            wstate["slot_of"] = {}
            TM.held, TF.held = [], []
            taps.clear()
            Mx, Fx = [], []
            for p0 in range(0, len(items), 2):
                pair = [(k, items[k]) for k in range(p0, min(p0 + 2, len(items)))]
                for l in range(depth):
                    for slot in range(2):
                        if slot < len(pair):
                            k, (kind, seq, m) = pair[slot]
                            is_meta = kind == "meta"
                            nt = 1 if is_meta else NT
                            info = {"seq": seq, "m": m, "par": str(k % 2), "save": is_meta,
                                    "restore": (kind == "seq" and m == 0 and seq > 0)}
                            Mx.append((mixer, (l, nt, is_meta, hbuf[k % 2], hbufB[k % 2], info)))
                            Fx.append((ffn, (l, nt, is_meta, hbuf[k % 2], hbufB[k % 2], info)))
                        else:
                            Mx.append(None)
                            Fx.append(None)
            nsteps = len(Mx) + 1
            for st in range(nsteps):
                gens = []
                if st < len(Mx) and Mx[st] is not None:
                    gens.append(Mx[st][0](*Mx[st][1]))
                if st >= 1 and Fx[st - 1] is not None:
                    gens.append(Fx[st - 1][0](*Fx[st - 1][1]))
                tacc = [0.0] * len(gens)
                alive = list(range(len(gens)))
                while alive:
                    gi = min(alive, key=lambda a: tacc[a])
                    try:
                        tacc[gi] += next(gens[gi])
                    except StopIteration:
                        alive.remove(gi)

        Eng.DRY = True
        wstate["sched"] = []
        emit_all()
        Eng.DRY = False
        emit_all()
        assert wstate["next_use"] == len(wstate["sched"])

        for s in QP.sems.values():
            POOL.e.wait_ge(s[0], s[1])
        for s in QW.sems.values():
            SP.e.wait_ge(s[0], s[1])
        for E in (PE, ACT, DVE, POOL):
            for E2 in (PE, ACT, DVE, POOL):
                if E2.cnt > 0 and E2.last_sig:
                    E.e.wait_ge(E2.sem, E2.cnt)
    return nc


def host_prep(inputs, depth=DEPTH):
    f = lambda a: np.ascontiguousarray(np.asarray(a, dtype=np.float32))
    p = {}
    meta = f(inputs["meta_tokens"])
    mp = np.zeros((128, D), np.float32)
    mp[128 - NMETA:] = meta
    p["meta_pad"] = mp
    for k in ("w_in", "w_glu", "w_out", "w_ff_in", "w_ff_out"):
        p[k] = f(inputs[k])
    rows = []
    for l in range(depth):
        rows += [inputs["norm_mix_w"][l], inputs["norm_mlp_w"][l]]
    rows.append(inputs["final_norm_w"])
    p["nw"] = f(np.stack([np.asarray(r) for r in rows]))
    p["snw"] = f(inputs["ssd_norm_w"])
    cw = np.concatenate([np.asarray(inputs["conv_w"]), np.asarray(inputs["conv_b"])[:, None, :]], axis=1)
    p["convp"] = f(cw.reshape(depth, 5, 16, 128).transpose(0, 3, 2, 1))
    p["ssdp"] = f(np.stack([np.asarray(inputs["dt_bias"]), np.asarray(inputs["ssd_a_log"]), np.asarray(inputs["ssd_d"])], axis=1))
    def st(a):
        return np.asarray(a).reshape(depth, 16, 2, 64).transpose(0, 2, 3, 1).reshape(depth, 128, 16)
    ls = np.broadcast_to(np.asarray(inputs["s5_log_step"])[:, :, None], (depth, 32, 64))
    p["s5s"] = f(np.stack([st(inputs["s5_a_re"]), st(inputs["s5_a_im"]), st(ls)], axis=1))
    def stb(a):
        return np.asarray(a).reshape(depth, 16, 2, 64, 16).transpose(0, 2, 3, 1, 4).reshape(depth, 128, 16, 16)
    p["s5b"] = f(np.stack([stb(inputs["s5_b_re"]), stb(inputs["s5_b_im"])], axis=1))
    cre = np.asarray(inputs["s5_c_re"]).transpose(0, 1, 3, 2)
    cim = np.asarray(inputs["s5_c_im"]).transpose(0, 1, 3, 2)
    p["s5c"] = f(np.stack([stb(cre), stb(cim)], axis=1))
    p["s5d"] = f(np.asarray(inputs["s5_d"]).reshape(depth, 4, 128).transpose(0, 2, 1))
    k = np.arange(128)
    cf = np.zeros((128, 4 * 128 + 512 + 2), np.float32)
    cf[:, 0:128] = np.eye(128)
    cf[:, 128:256] = (k[:, None] <= k[None, :])
    cf[:, 256:384] = (k[:, None] > k[None, :])
    cf[:, 384:512] = 1.0
    cf[:, 512:1024] = np.arange(1, 513)[None, :]
    cf[:, 1024] = (k >= 128 - NMETA)
    cf[:, 1025] = EPS
    p["cf"] = cf
    p["cb"] = np.eye(128).astype(ml_dtypes.bfloat16)
    return p


_CACHE = {}


def kernel(**inputs):
    x = np.asarray(inputs["x"], dtype=np.float32)
    B, S, _ = x.shape
    ncores = 8
    nseq = B // ncores
    NT = 2
    nmt = S // (NT * 128)
    key = (nseq, nmt)
    if key not in _CACHE:
        _CACHE[key] = build_program(nseq, nmt, NT)
    nc = _CACHE[key]
    p = host_prep(inputs)
    in_maps = []
    for c in range(ncores):
        m = dict(p)
        m["x"] = np.ascontiguousarray(x[c * nseq:(c + 1) * nseq])
        in_maps.append(m)
    res = run_bass_kernel_spmd(nc, in_maps, core_ids=list(range(ncores)))
    out = np.concatenate([np.asarray(r["out"]) for r in res.results], axis=0)
    return out.astype(np.float32)
```
